# Optimizing a Trainium2 kernel written in Bass

```python
import math
import jax, jax.numpy as jnp
from jax import lax
import numpy as np

D_MODEL = 1024
BATCH = 4
SEQ = 4096
DEPTH = 2

A_HEADS = 8
A_HEAD_DIM = 64
A_WIDTH = A_HEADS * A_HEAD_DIM
DILATED_CONFIGS = ((128, 1), (512, 4), (2048, 16))
A_BLOCK = 128
A_SPAN = A_BLOCK * max(d for _, d in DILATED_CONFIGS)
B_WIDTH = 512
B_GROUP = 16
B_GROUPS = B_WIDTH // B_GROUP
B_STATE = 64
AB_WIDTH = A_WIDTH + B_WIDTH
AB_IN = 3 * A_WIDTH + B_WIDTH + AB_WIDTH
C_HEADS = 8
C_DK = 128
C_DV = 128
C_CONV = 4
C_CHUNK = 64
C_WIDTH = C_HEADS * C_DV
QKV_WIDTH = 2 * C_HEADS * C_DK + C_WIDTH
C_IN = QKV_WIDTH + C_WIDTH + 2 * C_HEADS
REL_BUCKETS = 32
REL_MAX_DIST = 2048
N_EVEN = (DEPTH + 1) // 2
N_ODD = DEPTH // 2
EPS = 1e-6

kernel_name = "hybrid_dilated_s5_gdn_block"


def rms_norm(x, g):
    xf = x.astype(jnp.float32)
    y = xf * lax.rsqrt(jnp.mean(xf * xf, axis=-1, keepdims=True) + EPS)
    return (y * g.astype(jnp.float32)).astype(x.dtype)


def t5_bucket_np(dist):
    dist = np.maximum(dist, 0)
    max_exact = REL_BUCKETS // 2
    large = max_exact + (np.log(np.maximum(dist, 1) / max_exact)
                         / math.log(REL_MAX_DIST / max_exact) * (REL_BUCKETS - max_exact)).astype(np.int32)
    large = np.minimum(large, REL_BUCKETS - 1)
    return np.where(dist < max_exact, dist, large).astype(np.int32)


def dilated_attention(q, k, v, rel_bias):
    bsz, s_len = q.shape[:2]
    sp = -(-s_len // A_SPAN) * A_SPAN
    pad = ((0, 0), (0, sp - s_len), (0, 0), (0, 0))
    q, k, v = jnp.pad(q, pad), jnp.pad(k, pad), jnp.pad(v, pad)
    scale = A_HEAD_DIM ** -0.5
    qi = np.arange(A_BLOCK)[:, None]
    kj = np.arange(2 * A_BLOCK)[None, :]
    rel = qi + A_BLOCK - kj
    neg = jnp.finfo(jnp.float32).min
    outs, lses = [], []
    for window, dil in DILATED_CONFIGS:
        n_keys = window // dil
        nb = sp // dil // A_BLOCK
        qb, kb, vb = (t.reshape(bsz, nb, A_BLOCK, dil, A_HEADS, A_HEAD_DIM) for t in (q, k, v))
        prev = lambda t: jnp.pad(t[:, :-1], ((0, 0), (1, 0), (0, 0), (0, 0), (0, 0), (0, 0)))
        kw = jnp.concatenate([prev(kb), kb], axis=2)
        vw = jnp.concatenate([prev(vb), vb], axis=2)
        bias = jnp.transpose(rel_bias[t5_bucket_np(rel * dil)], (2, 0, 1)).astype(jnp.float32)
        band = (rel >= 0) & (rel <= n_keys)
        first = band & (kj >= A_BLOCK)
        blk_mask = np.concatenate([first[None], np.broadcast_to(band, (nb - 1,) + band.shape)], 0)
        s = jnp.einsum('bnqrhd,bnkrhd->bnrhqk', qb, kw, preferred_element_type=jnp.float32) * scale
        s = jnp.where(blk_mask[None, :, None, None], s + bias[None, None, None], neg)
        m = jnp.max(s, axis=-1, keepdims=True)
        p = jnp.exp(s - m)
        den = jnp.sum(p, axis=-1)
        o = jnp.einsum('bnrhqk,bnkrhd->bnqrhd', p, vw.astype(jnp.float32))
        den_t = jnp.transpose(den, (0, 1, 4, 2, 3))
        lse = jnp.transpose(m[..., 0], (0, 1, 4, 2, 3)) + jnp.log(den_t)
        outs.append((o / den_t[..., None]).reshape(bsz, sp, A_HEADS, A_HEAD_DIM))
        lses.append(lse.reshape(bsz, sp, A_HEADS))
    w = jax.nn.softmax(jnp.stack(lses, 0), axis=0)
    out = jnp.einsum('cbsh,cbshd->bshd', w, jnp.stack(outs, 0))
    return out[:, :s_len].reshape(bsz, s_len, A_WIDTH)


def _complex_affine_combine(e1, e2):
    a1r, a1i, b1r, b1i = e1
    a2r, a2i, b2r, b2i = e2
    return (a2r * a1r - a2i * a1i, a2r * a1i + a2i * a1r,
            a2r * b1r - a2i * b1i + b2r, a2r * b1i + a2i * b1r + b2i)


def s5_layer(u, a_re, a_im, log_dt, b_re, b_im, c_re, c_im, d_skip, glu_w, glu_b):
    bsz, s_len = u.shape[:2]
    uf = u.astype(jnp.float32)
    ug = uf.reshape(bsz, s_len, B_GROUPS, B_GROUP)
    dt = jnp.exp(log_dt.astype(jnp.float32))[:, None]
    ar, ai = a_re.astype(jnp.float32), a_im.astype(jnp.float32)
    mag = jnp.exp(dt * ar)
    abar_r, abar_i = mag * jnp.cos(dt * ai), mag * jnp.sin(dt * ai)
    den = ar * ar + ai * ai
    fr = ((abar_r - 1.0) * ar + abar_i * ai) / den
    fi = (abar_i * ar - (abar_r - 1.0) * ai) / den
    br, bi = b_re.astype(jnp.float32), b_im.astype(jnp.float32)
    bbar_r = fr[..., None] * br - fi[..., None] * bi
    bbar_i = fr[..., None] * bi + fi[..., None] * br
    bu_r = jnp.einsum('bsgm,gpm->bsgp', ug, bbar_r)
    bu_i = jnp.einsum('bsgm,gpm->bsgp', ug, bbar_i)
    a_r = jnp.broadcast_to(abar_r, bu_r.shape)
    a_i = jnp.broadcast_to(abar_i, bu_i.shape)
    _, _, x_r, x_i = lax.associative_scan(_complex_affine_combine, (a_r, a_i, bu_r, bu_i), axis=1)
    y = (jnp.einsum('gmp,bsgp->bsgm', c_re.astype(jnp.float32), x_r)
         - jnp.einsum('gmp,bsgp->bsgm', c_im.astype(jnp.float32), x_i))
    y = jax.nn.gelu(y.reshape(bsz, s_len, B_WIDTH) + d_skip.astype(jnp.float32) * uf)
    return (y * jax.nn.sigmoid(y @ glu_w.astype(jnp.float32) + glu_b.astype(jnp.float32))).astype(u.dtype)


def ab_mixer(h, w_in, w_out, rel_bias, a_re, a_im, log_dt, b_re, b_im, c_re, c_im, d_skip, glu_w, glu_b):
    bsz, s_len, _ = h.shape
    z = h @ w_in
    q, k, v, u, gate = jnp.split(z, [A_WIDTH, 2 * A_WIDTH, 3 * A_WIDTH, 3 * A_WIDTH + B_WIDTH], axis=-1)
    hs = (bsz, s_len, A_HEADS, A_HEAD_DIM)
    o_a = dilated_attention(q.reshape(hs), k.reshape(hs), v.reshape(hs), rel_bias).astype(h.dtype)
    o_b = s5_layer(u, a_re, a_im, log_dt, b_re, b_im, c_re, c_im, d_skip, glu_w, glu_b)
    o = jnp.concatenate([o_a, o_b], axis=-1) * jax.nn.silu(gate)
    return o @ w_out


def l2_normalize(t):
    return t * lax.rsqrt(jnp.sum(t * t, axis=-1, keepdims=True) + EPS)


def gdn_mixer(h, w_in, conv_w, a_log, dt_bias, norm_g, w_out):
    bsz, s_len, _ = h.shape
    z = h @ w_in
    qkv, gate, beta_raw, a_raw = jnp.split(z, [QKV_WIDTH, QKV_WIDTH + C_WIDTH, QKV_WIDTH + C_WIDTH + C_HEADS], axis=-1)
    qkv = lax.conv_general_dilated(qkv, conv_w[:, None, :].astype(qkv.dtype), (1,), [(C_CONV - 1, 0)],
                                   dimension_numbers=('NWC', 'WIO', 'NWC'), feature_group_count=QKV_WIDTH)
    qkv = jax.nn.silu(qkv).astype(jnp.float32)
    q, k, v = jnp.split(qkv, [C_HEADS * C_DK, 2 * C_HEADS * C_DK], axis=-1)
    q = l2_normalize(q.reshape(bsz, s_len, C_HEADS, C_DK)) * (C_DK ** -0.5)
    k = l2_normalize(k.reshape(bsz, s_len, C_HEADS, C_DK))
    v = v.reshape(bsz, s_len, C_HEADS, C_DV)
    beta = jax.nn.sigmoid(beta_raw.astype(jnp.float32))
    g = -jnp.exp(a_log.astype(jnp.float32)) * jax.nn.softplus(a_raw.astype(jnp.float32) + dt_bias.astype(jnp.float32))
    n_chunks = s_len // C_CHUNK

    def chunks(t):
        return jnp.moveaxis(t.reshape(bsz, n_chunks, C_CHUNK, C_HEADS, *t.shape[3:]), 3, 1)

    qc, kc, vc, bc = chunks(q), chunks(k), chunks(v), chunks(beta)
    gc = jnp.cumsum(chunks(g), axis=-1)
    idx = np.arange(C_CHUNK)
    tril = idx[:, None] >= idx[None, :]
    strict = idx[:, None] > idx[None, :]
    decay = jnp.exp(jnp.where(tril, gc[..., :, None] - gc[..., None, :], -jnp.inf))
    kb = kc * bc[..., None]
    lower = jnp.where(strict, jnp.einsum('bhnid,bhnjd->bhnij', kb, kc) * decay, 0.0)
    tri = lower + jnp.eye(C_CHUNK, dtype=lower.dtype)
    u_c = lax.linalg.triangular_solve(tri, vc * bc[..., None], left_side=True, lower=True, unit_diagonal=True)
    w_c = lax.linalg.triangular_solve(tri, kb * jnp.exp(gc)[..., None], left_side=True, lower=True, unit_diagonal=True)
    aqk = jnp.einsum('bhnid,bhnjd->bhnij', qc, kc) * decay
    qg = qc * jnp.exp(gc)[..., None]
    g_last = gc[..., -1]
    kd = kc * jnp.exp(g_last[..., None] - gc)[..., None]
    xs = tuple(jnp.moveaxis(t, 2, 0) for t in (w_c, u_c, qg, kd, aqk, jnp.exp(g_last)))

    def step(state, xs_c):
        w_i, u_i, qg_i, kd_i, aqk_i, dec_i = xs_c
        v_new = u_i - jnp.einsum('bhik,bhkv->bhiv', w_i, state)
        o_i = jnp.einsum('bhik,bhkv->bhiv', qg_i, state) + jnp.einsum('bhij,bhjv->bhiv', aqk_i, v_new)
        state = state * dec_i[..., None, None] + jnp.einsum('bhik,bhiv->bhkv', kd_i, v_new)
        return state, o_i

    s0 = jnp.zeros((bsz, C_HEADS, C_DK, C_DV), jnp.float32)
    _, o = lax.scan(step, s0, xs)
    o = jnp.transpose(o, (1, 0, 3, 2, 4)).reshape(bsz, s_len, C_HEADS, C_DV)
    o = rms_norm(o, norm_g).reshape(bsz, s_len, C_WIDTH).astype(h.dtype) * jax.nn.silu(gate)
    return o @ w_out


def setup_inputs(seed: int = 0) -> dict:
    key = jax.random.key(seed)
    ks = iter(jax.random.split(key, 32))
    nrm = lambda shape, s: jax.random.normal(next(ks), shape, jnp.float32) * s
    uni = lambda shape, lo, hi: jax.random.uniform(next(ks), shape, jnp.float32, lo, hi)
    n_idx = jnp.arange(B_STATE, dtype=jnp.float32)
    dt_c = uni((N_ODD, C_HEADS), 0.001, 0.1)
    return {
        "x": nrm((BATCH, SEQ, D_MODEL), 1.0),
        "c": nrm((BATCH, D_MODEL), 1.0),
        "ada_w": nrm((DEPTH, D_MODEL, 3 * D_MODEL), 0.3 * D_MODEL ** -0.5),
        "ada_b": nrm((DEPTH, 3 * D_MODEL), 0.02),
        "pre_g": 1.0 + nrm((DEPTH, D_MODEL), 0.02),
        "post_g": 1.0 + nrm((DEPTH, D_MODEL), 0.02),
        "rel_bias": nrm((REL_BUCKETS, A_HEADS), 0.5),
        "ab_w_in": nrm((N_EVEN, D_MODEL, AB_IN), D_MODEL ** -0.5),
        "ab_w_out": nrm((N_EVEN, AB_WIDTH, D_MODEL), AB_WIDTH ** -0.5),
        "s5_a_re": -0.5 + nrm((N_EVEN, B_GROUPS, B_STATE), 0.01),
        "s5_a_im": math.pi * n_idx + nrm((N_EVEN, B_GROUPS, B_STATE), 0.01),
        "s5_log_dt": uni((N_EVEN, B_GROUPS), math.log(0.001), math.log(0.1)),
        "s5_b_re": nrm((N_EVEN, B_GROUPS, B_STATE, B_GROUP), (2 * B_GROUP) ** -0.5),
        "s5_b_im": nrm((N_EVEN, B_GROUPS, B_STATE, B_GROUP), (2 * B_GROUP) ** -0.5),
        "s5_c_re": nrm((N_EVEN, B_GROUPS, B_GROUP, B_STATE), (2 * B_STATE) ** -0.5),
        "s5_c_im": nrm((N_EVEN, B_GROUPS, B_GROUP, B_STATE), (2 * B_STATE) ** -0.5),
        "s5_d": nrm((N_EVEN, B_WIDTH), 1.0),
        "s5_glu_w": nrm((N_EVEN, B_WIDTH, B_WIDTH), B_WIDTH ** -0.5),
        "s5_glu_b": nrm((N_EVEN, B_WIDTH), 0.02),
        "gdn_w_in": nrm((N_ODD, D_MODEL, C_IN), D_MODEL ** -0.5),
        "gdn_conv": nrm((N_ODD, C_CONV, QKV_WIDTH), C_CONV ** -0.5),
        "gdn_a_log": jnp.log(uni((N_ODD, C_HEADS), 1.0, 16.0)),
        "gdn_dt_bias": dt_c + jnp.log(-jnp.expm1(-dt_c)),
        "gdn_norm_g": 1.0 + nrm((N_ODD, C_DV), 0.02),
        "gdn_w_out": nrm((N_ODD, C_WIDTH, D_MODEL), C_WIDTH ** -0.5),
    }


def reference(x, c, ada_w, ada_b, pre_g, post_g, rel_bias, ab_w_in, ab_w_out,
              s5_a_re, s5_a_im, s5_log_dt, s5_b_re, s5_b_im, s5_c_re, s5_c_im, s5_d, s5_glu_w, s5_glu_b,
              gdn_w_in, gdn_conv, gdn_a_log, gdn_dt_bias, gdn_norm_g, gdn_w_out):
    c_act = jax.nn.silu(c)
    for layer in range(DEPTH):
        mod = c_act @ ada_w[layer] + ada_b[layer]
        shift, scale, gate = jnp.split(mod, 3, axis=-1)
        h = rms_norm(x, pre_g[layer]) * (1.0 + scale[:, None]) + shift[:, None]
        j = layer // 2
        if layer % 2 == 0:
            y = ab_mixer(h, ab_w_in[j], ab_w_out[j], rel_bias, s5_a_re[j], s5_a_im[j], s5_log_dt[j],
                         s5_b_re[j], s5_b_im[j], s5_c_re[j], s5_c_im[j], s5_d[j], s5_glu_w[j], s5_glu_b[j])
        else:
            y = gdn_mixer(h, gdn_w_in[j], gdn_conv[j], gdn_a_log[j], gdn_dt_bias[j], gdn_norm_g[j], gdn_w_out[j])
        x = x + gate[:, None] * rms_norm(y, post_g[layer])
    return x
```

```python
import numpy as np
import concourse.bass as bass
import concourse.mybir as mybir
from concourse.bass_utils import run_bass_kernel_spmd

F32 = mybir.dt.float32
BF16 = mybir.dt.bfloat16
AF = mybir.ActivationFunctionType
ALU = mybir.AluOpType
AX = mybir.AxisListType

ENGS = ("pe", "act", "dve", "pool", "sp")
EPOCH = 16000


class Buf:
    _n = 0

    def __init__(self, name, t, kind):
        self.name = name
        self.t = t
        self.kind = kind
        Buf._n += 1
        self.id = Buf._n
        self.wr = {}
        self.pslast = {}
        self.rd = {}
        self.slot = None

    def __getitem__(self, idx):
        return self.t[idx]


class SemSlot:
    def __init__(self):
        self.handle = None
        self.count = 0
        self.last = None


class Op:
    __slots__ = ("eng", "fn", "reads", "writes", "deps", "is_dma", "dbuf", "sig", "sigval", "idx", "dmaval", "busy", "lat", "soft", "seg")

    def __init__(self, eng, fn, reads, writes, is_dma=False, dbuf=None):
        self.eng = eng
        self.fn = fn
        self.reads = reads
        self.writes = writes
        self.deps = set()
        self.is_dma = is_dma
        self.dbuf = dbuf
        self.sig = False
        self.sigval = None
        self.dmaval = None
        self.busy = 0.3
        self.lat = 0.3
        self.soft = set()


class PsView:
    def __init__(self, base, ap):
        self.base = base
        self.ap = ap

    def __getitem__(self, idx):
        return self.ap[idx]


def _acc(x):
    if isinstance(x, PsView):
        return (x.base, None)
    if isinstance(x, Buf):
        return (x, None)
    if isinstance(x[0], PsView):
        return (x[0].base, x[1])
    return x


class Prog:
    def __init__(self, nc):
        self.nc = nc
        self.ops = []
        self.bufs = []
        self.final_waits = []
        self.slots = []
        self.free_slots = []
        self.fence_deps = set()
        self.last_of_eng = {}
        self.scopes = []
        self.seg = 0

    def sb(self, name, shape, dtype=F32):
        if self.scopes:
            g = self.nc.sbuf_tensor(name + "_s%d" % len(self.bufs), list(shape), dtype)
            t = g.__enter__()
            b = Buf(name, t, "sb")
            self.scopes[-1].append((g, b))
        else:
            b = Buf(name, self.nc.alloc_sbuf_tensor(name, list(shape), dtype), "sb")
        self.bufs.append(b)
        return b

    def push_scope(self):
        self.scopes.append([])

    def pop_scope(self):
        self.fence()
        sc = self.scopes.pop()
        for (g, b) in reversed(sc):
            g.__exit__(None, None, None)
            if b.slot is not None:
                self.free_slots.append(b.slot)
                b.slot = None

    def fence(self):
        deps = set(self.last_of_eng.values())
        for sl in self.slots:
            if sl.last is not None:
                deps.add(sl.last)
        self.fence_deps = deps
        self.seg += 1

    def ps(self, name, shape, dtype=F32):
        b = Buf(name, self.nc.alloc_psum_tensor(name, list(shape), dtype), "ps")
        self.bufs.append(b)
        return b

    def dram(self, name, shape, dtype=F32, kind="Internal"):
        t = self.nc.dram_tensor(name, list(shape), dtype, kind=kind)
        b = Buf(name, t.ap(), "dr")
        self.bufs.append(b)
        return b

    def _overlap_w(self, b, k):
        if k is None:
            return list(b.wr.values())
        r = []
        if k in b.wr:
            r.append(b.wr[k])
        if None in b.wr:
            r.append(b.wr[None])
        return r

    def _overlap_r(self, b, k):
        if k is None:
            r = []
            for v in b.rd.values():
                r.extend(v)
            return r
        return list(b.rd.get(k, [])) + list(b.rd.get(None, []))

    def op(self, eng, fn, reads=(), writes=(), is_dma=False, dbuf=None, cost=None):
        o = Op(eng, fn, [_acc(x) for x in reads], [_acc(x) for x in writes], is_dma, dbuf)
        if cost is not None:
            o.busy, o.lat = cost
        o.idx = len(self.ops)
        o.seg = self.seg
        o.deps.update(self.fence_deps)
        self.last_of_eng[eng] = o.idx
        if is_dma:
            if dbuf.slot is None:
                if self.free_slots:
                    dbuf.slot = self.free_slots.pop()
                else:
                    dbuf.slot = SemSlot()
                    self.slots.append(dbuf.slot)
            sl = dbuf.slot
            sl.count += 16
            sl.last = o.idx
            o.dmaval = (sl, sl.count)
            sbacc = (dbuf, None)
            o.writes = [w for w in o.writes if w[0] is not dbuf] + [sbacc]
            o.reads = [r for r in o.reads if r[0] is not dbuf]
        psb = set(b for (b, k) in o.reads + o.writes if b.kind == "ps")
        o.reads = [(b, k) for (b, k) in o.reads if b.kind != "ps"]
        o.writes = [(b, k) for (b, k) in o.writes if b.kind != "ps"]
        for b in psb:
            for en, ix in b.pslast.items():
                if en != eng:
                    o.deps.add(ix)
                else:
                    o.soft.add(ix)
            b.pslast[eng] = o.idx
        for (b, k) in o.reads:
            o.deps.update(self._overlap_w(b, k))
        for (b, k) in o.writes:
            o.deps.update(self._overlap_w(b, k))
            o.deps.update(self._overlap_r(b, k))
        for (b, k) in o.reads:
            b.rd.setdefault(k, []).append(o.idx)
        for (b, k) in o.writes:
            if k is None:
                b.wr = {None: o.idx}
                b.rd = {}
            else:
                b.wr[k] = o.idx
                b.rd[k] = []
        o.deps.discard(o.idx)
        self.ops.append(o)
        return o

    def dma(self, out_ap, in_ap, sbuf, reads=(), writes=(), eng="sp", **kw):
        def fn(e, out_ap=out_ap, in_ap=in_ap, kw=kw):
            return e.dma_start(out=out_ap, in_=in_ap, **kw)
        n = 1
        for d_ in out_ap.shape:
            n *= int(d_)
        nbytes = n * (2 if out_ap.dtype == BF16 else 4)
        return self.op(eng, fn, reads, writes, is_dma=True, dbuf=sbuf, cost=(0.15, 2.0 + nbytes / 150e3))

    def schedule(self, window=48):
        ops = self.ops
        n = len(ops)
        segs = {}
        for o in ops:
            segs.setdefault(o.seg, []).append(o.idx)
        done = [False] * n
        fin = [0.0] * n
        free = {e: 0.0 for e in ENGS}
        order = []
        last_sched = {}
        for sg in sorted(segs):
            idxs = segs[sg]
            extra = set(last_sched.values())
            per = {e: [] for e in ENGS}
            for ix in idxs:
                ops[ix].deps.update(extra)
                per[ops[ix].eng].append(ix)
            alldeps = {ix: list(ops[ix].deps | ops[ix].soft) for ix in idxs}
            ptr = {e: 0 for e in ENGS}
            remaining = len(idxs)
            while remaining:
                best = None
                for e in ENGS:
                    lst = per[e]
                    p = ptr[e]
                    while p < len(lst) and done[lst[p]]:
                        p += 1
                    ptr[e] = p
                    cnt = 0
                    q = p
                    wnd = window * 8 if e == "pe" else window
                    while q < len(lst) and cnt < wnd:
                        ix = lst[q]
                        q += 1
                        if done[ix]:
                            continue
                        cnt += 1
                        ok = True
                        rdy = 0.0
                        for d in alldeps[ix]:
                            if not done[d]:
                                ok = False
                                break
                            t = fin[d] + (0.3 if ops[d].eng != e else 0.05)
                            if t > rdy:
                                rdy = t
                        if not ok:
                            continue
                        st = rdy if rdy > free[e] else free[e]
                        key = (st, ix)
                        if best is None or key < best[0]:
                            best = (key, e, ix)
                        if st <= free[e]:
                            break
                assert best is not None, "scheduler stuck"
                (st, _), e, ix = best
                done[ix] = True
                fin[ix] = st + ops[ix].lat
                free[e] = st + ops[ix].busy
                order.append(ix)
                last_sched[e] = ix
                remaining -= 1
        self.sim_time = max(fin) if fin else 0.0
        return order

    def load(self, sbuf, sb_ap, dr_buf, dr_ap, eng="sp", key=None, **kw):
        return self.dma(sb_ap, dr_ap, sbuf, reads=[(dr_buf, key)], writes=[sbuf], eng=eng, **kw)

    def store(self, dr_buf, dr_ap, sbuf, sb_ap, eng="act", key=None, **kw):
        return self.dma(dr_ap, sb_ap, sbuf, reads=[sbuf], writes=[(dr_buf, key)], eng=eng, **kw)

    def emit(self):
        nc = self.nc
        ops = self.ops
        order = self.schedule() if getattr(self, "reorder", True) else list(range(len(ops)))
        for o in ops:
            for d in o.deps:
                ops[d].sig = True
        cnt = {e: 0 for e in ENGS}
        sems = {e: [] for e in ENGS}
        oops = [ops[i] for i in order]
        for o in oops:
            if o.is_dma:
                sl = o.dmaval[0]
                if sl.handle is None:
                    sl.handle = nc.alloc_semaphore("dq_%d" % self.slots.index(sl))
            elif o.sig:
                c = cnt[o.eng]
                ep = c // EPOCH
                while len(sems[o.eng]) <= ep:
                    sems[o.eng].append(nc.alloc_semaphore("s_%s_%d" % (o.eng, len(sems[o.eng]))))
                o.sigval = (sems[o.eng][ep], c % EPOCH + 1, ep)
                cnt[o.eng] = c + 1
        engobj = {"pe": nc.tensor, "act": nc.scalar, "dve": nc.vector, "pool": nc.gpsimd, "sp": nc.sync}
        self.nwaits = 0

        def run_engine(ename, e):
            waited = {}
            for o in oops:
                if o.eng != ename:
                    continue
                need = {}
                for d in o.deps:
                    p = ops[d]
                    if p.is_dma:
                        s, v = p.dmaval[0].handle, p.dmaval[1]
                        key = ("d", id(p.dmaval[0]))
                    else:
                        s, v, ep = p.sigval
                        key = (p.eng, ep)
                    if key not in need or need[key][1] < v:
                        need[key] = (s, v)
                for key, (s, v) in need.items():
                    if key[0] != "d":
                        newer = [k2 for k2 in need if k2[0] == key[0] and k2[1] > key[1]]
                        if newer:
                            continue
                        w_ep = waited.get(("ep", key[0]), -1)
                        if w_ep > key[1]:
                            continue
                    if waited.get(key, 0) >= v:
                        continue
                    e.wait_ge(s, v)
                    self.nwaits += 1
                    waited[key] = v
                    if key[0] != "d":
                        waited[("ep", key[0])] = max(waited.get(("ep", key[0]), -1), key[1])
                ins = o.fn(e)
                if o.is_dma:
                    ins.then_inc(o.dmaval[0].handle, 16)
                elif o.sig:
                    ins.then_inc(o.sigval[0], 1)
            for (en, s, v) in self.final_waits:
                if en == ename:
                    e.wait_ge(s, v)

        for sl in self.slots:
            if sl.handle is not None:
                self.final_waits.append(("sp", sl.handle, sl.count))
        with nc.Block() as block:
            @block.tensor
            def _(e):
                run_engine("pe", e)

            @block.scalar
            def _(e):
                run_engine("act", e)

            @block.vector
            def _(e):
                run_engine("dve", e)

            @block.gpsimd
            def _(e):
                run_engine("pool", e)

            @block.sync
            def _(e):
                run_engine("sp", e)
        return nc


import math
import ml_dtypes

S = 4096
D = 1024
NT = S // 128
EPS = 1e-6
CFGS = ((1, 32), (4, 8), (16, 2))


def _fd(ap):
    n = 1
    for d_ in ap.shape[1:]:
        n *= int(d_)
    return n


def _ecost(eng, ap):
    fd = _fd(ap)
    if eng == "act":
        c = 0.2 + fd / 1200.0
    elif eng == "dve":
        c = 0.08 + fd / 960.0
    else:
        c = 0.12 + fd / 450.0
    return (c, c)


def mm(P, wr, out, lhsT, rhs, rd, start=True, stop=True):
    c = max(0.03, _fd(rhs) / 2400.0 * (4.0 if lhsT.dtype == F32 else 1.0)) + 0.01
    return P.op("pe", lambda e: e.matmul(out, lhsT, rhs, start=start, stop=stop), rd, wr, cost=(c, c + 0.1))


def tr(P, wr, out, in_, ident, rd):
    c = 0.28 if in_.dtype == F32 else 0.07
    return P.op("pe", lambda e: e.transpose(out, in_, ident), rd, wr, cost=(c, c + 0.1))


def act(P, wr, out, in_, func, rd, **kw):
    return P.op("act", lambda e: e.activation(out, in_, func, **kw), rd, wr, cost=_ecost("act", out))


def tt(P, eng, wr, out, in0, in1, op, rd):
    return P.op(eng, lambda e: e.tensor_tensor(out, in0, in1, op), rd, wr, cost=_ecost(eng, out))


def ts(P, eng, wr, out, in0, s1, s2, op0, rd, op1=None):
    if op1 is None:
        return P.op(eng, lambda e: e.tensor_scalar(out, in0, s1, None, op0), rd, wr, cost=_ecost(eng, out))
    return P.op(eng, lambda e: e.tensor_scalar(out, in0, s1, s2, op0, op1), rd, wr, cost=_ecost(eng, out))


def stt(P, eng, wr, out, in0, scalar, in1, op0, op1, rd):
    return P.op(eng, lambda e: e.scalar_tensor_tensor(out, in0, scalar, in1, op0, op1), rd, wr, cost=_ecost(eng, out))


def cp(P, eng, wr, out, in_, rd):
    return P.op(eng, lambda e: e.tensor_copy(out, in_), rd, wr, cost=_ecost(eng, out))


def mset(P, eng, wr, ap, val):
    return P.op(eng, lambda e: e.memset(ap, val), [], wr, cost=_ecost(eng, ap))


class Ctx:
    pass


def rstd_from_ss(P, C, ss_buf, ss_ap, out_buf, out_ap, n):
    act(P, [out_buf], out_ap, ss_ap, AF.Ln, [ss_buf, C.epsb], bias=C.epsb[0:ss_ap.shape[0], 0:1], scale=1.0 / n)
    act(P, [out_buf], out_ap, out_ap, AF.Exp, [out_buf], scale=-0.5)


def setup(P, C):
    C.ident = P.sb("ident", [128, 128], BF16)
    idd = P.dram("ident_in", [128, 128], BF16, kind="ExternalInput")
    P.load(C.ident, C.ident[:, :], idd, idd[:, :])
    C.identf = P.sb("identf", [128, 128], F32)
    iddf = P.dram("identf_in", [128, 128], F32, kind="ExternalInput")
    P.load(C.identf, C.identf[:, :], iddf, iddf[:, :])
    C.ones_f = P.sb("ones_f", [128, 128], F32)
    mset(P, "pool", [C.ones_f], C.ones_f[:, :], 1.0)
    C.ones_b = P.sb("ones_b", [128, 128], BF16)
    mset(P, "pool", [C.ones_b], C.ones_b[:, :], 1.0)
    C.epsb = P.sb("epsb", [128, 1], F32)
    mset(P, "pool", [C.epsb], C.epsb[:, :], EPS)
    C.psA = [P.ps("psA%d" % i, [128, 1024], F32) for i in range(2)]
    C.psB = [P.ps("psB%d" % i, [128, 512], F32) for i in range(2)]
    C.psTf = [P.ps("psT%d" % i, [128, 512], F32) for i in range(2)]
    C.psTs = [PsView(b, b.t[:, :].bitcast(BF16)) for b in C.psTf]
    C.psT = C.psTs[0]
    C.big = P.sb("big", [128, 8, S], BF16)
    C.oT_d = P.dram("oT_d", [8, 128, S], BF16)
    C.cfm = P.sb("cfm", [128, 8], F32)
    cd = P.dram("c_fm", [128, 8], F32, kind="ExternalInput")
    P.load(C.cfm, C.cfm[:, :], cd, cd[:, :])
    C.cact = P.sb("cact", [128, 8], F32)
    act(P, [C.cact], C.cact[:, :], C.cfm[:, :], AF.Silu, [C.cfm])
    C.ada_w = P.dram("ada_w", [2, 1024, 3072], F32, kind="ExternalInput")
    C.ada_b_fm = P.dram("ada_b_fm", [2, 128, 24], F32, kind="ExternalInput")
    C.ada_b_g = P.dram("ada_b_g", [2, 1, 1024], F32, kind="ExternalInput")
    C.pre_g_fm = P.dram("pre_g_fm", [2, 128, 8], F32, kind="ExternalInput")
    C.post_g_row = P.dram("post_g_row", [2, 1, 1024], F32, kind="ExternalInput")
    C.wstage = [P.sb("wstage%d" % i, [128, 8, 128], F32) for i in range(2)]
    C.wsi = 0
    C.ssq = P.sb("ssq", [128, 2 * NT], F32)
    C.rstd = P.sb("rstd", [128, 2 * NT], F32)
    C.modfm_l = [P.sb("modfm%d" % i, [128, 24], F32) for i in range(2)]
    C.Asc_l = [P.sb("Asc%d" % i, [128, 8], F32) for i in range(2)]
    C.GP_l = [P.sb("GP%d" % i, [128, 1024], F32) for i in range(2)]
    C.grow_l = [P.sb("grow%d" % i, [1, 1024], F32) for i in range(2)]
    C.modfm, C.Asc, C.GP, C.grow = C.modfm_l[0], C.Asc_l[0], C.GP_l[0], C.grow_l[0]
    C.tmpv = P.sb("tmpv", [128, 24], F32)
    C.trow = P.sb("trow", [1, 1024], F32)


def load_w_bf16(P, C, dst, dst_ap, wd, w_ap, ncols):
    st = C.wstage[C.wsi % 2]
    C.wsi += 1
    P.load(st, st[:, :, 0:ncols], wd, w_ap.rearrange("(k p) c -> p k c", p=128))
    eng = "pool" if (C.wsi % 2) else "dve"
    cp(P, eng, [dst], dst_ap, st[:, :, 0:ncols], [st])


def stage_adaln(P, C, layer, defer_pop=False):
    C.modfm, C.Asc, C.GP, C.grow = C.modfm_l[layer], C.Asc_l[layer], C.GP_l[layer], C.grow_l[layer]
    P.push_scope()
    awst = [P.sb("awst%d" % i, [128, 3072], F32) for i in range(2)]
    psF = C.psB[0]
    psG = C.psA[0]
    P.load(C.modfm, C.modfm[:, :], C.ada_b_fm, C.ada_b_fm[layer])
    P.load(C.grow, C.grow[:, :], C.ada_b_g, C.ada_b_g[layer])
    for k in range(8):
        st = awst[k % 2]
        P.load(st, st[:, :], C.ada_w, C.ada_w[layer, k * 128:(k + 1) * 128, :])
        for j in range(24):
            mm(P, [(psF, j)], psF[:, j:j + 1], st[:, j * 128:(j + 1) * 128], C.cact[:, k:k + 1], [st, C.cact])
        tt(P, "dve", [C.modfm], C.modfm[:, :], psF[:, 0:24], C.modfm[:, :], ALU.add, [psF, C.modfm])
        for cb in range(2):
            mm(P, [(psG, cb)], psG[0:1, cb * 512:(cb + 1) * 512], C.cact[:, k:k + 1],
               st[:, 2048 + cb * 512: 2048 + (cb + 1) * 512], [st, C.cact])
        tt(P, "dve", [C.grow], C.grow[:, :], psG[0:1, 0:1024], C.grow[:, :], ALU.add, [psG, C.grow])
    P.load(C.tmpv, C.tmpv[:, 0:8], C.pre_g_fm, C.pre_g_fm[layer])
    stt(P, "dve", [C.Asc], C.Asc[:, :], C.modfm[:, 8:16], 1.0, C.tmpv[:, 0:8], ALU.add, ALU.mult, [C.modfm, C.tmpv])
    P.load(C.trow, C.trow[:, :], C.post_g_row, C.post_g_row[layer])
    tt(P, "dve", [C.grow], C.grow[:, :], C.grow[:, :], C.trow[:, :], ALU.mult, [C.grow, C.trow])
    for cb in range(2):
        mm(P, [(psG, cb)], psG[:, cb * 512:(cb + 1) * 512], C.ones_f[0:1, :], C.grow[0:1, cb * 512:(cb + 1) * 512],
           [C.ones_f, C.grow])
    cp(P, "dve", [C.GP], C.GP[:, :], psG[:, :], [psG])
    if not defer_pop:
        P.pop_scope()


def stage_prenorm(P, C, xsrc, layer=0):
    C.modfm, C.Asc, C.GP = C.modfm_l[layer], C.Asc_l[layer], C.GP_l[layer]
    mset(P, "pool", [C.ssq], C.ssq[:, :], 0.0)
    P.push_scope()
    C.xt = [P.sb("xt%d" % i, [128, 1024], F32) for i in range(2)]
    C.xn = [P.sb("xn%d" % i, [128, 1024], BF16) for i in range(2)]
    C.junk = P.sb("junk", [128, 1024], BF16)
    for i in range(NT):
        xt = C.xt[i % 2]
        xn = C.xn[i % 2]
        P.load(xt, xt[:, :], xsrc, xsrc[i * 128:(i + 1) * 128, :])
        act(P, [C.junk, (C.ssq, i)], C.junk[:, :], xt[:, :], AF.Square, [xt], accum_out=C.ssq[:, i:i + 1])
        rstd_from_ss(P, C, (C.ssq, i), C.ssq[:, i:i + 1], (C.rstd, i), C.rstd[:, i:i + 1], D)
        ts(P, "dve", [xn], xn[:, :], xt[:, :], C.rstd[:, i:i + 1], None, ALU.mult, [xt, (C.rstd, i)])
        pt = C.psTs[i % 2]
        for j in range(8):
            tr(P, [(pt, j)], pt[:, j * 128:(j + 1) * 128], xn[:, j * 128:(j + 1) * 128], C.ident[:, :], [xn, C.ident])
        for j in range(8):
            ts(P, "dve", [(C.big, (j, i))], C.big[:, j, i * 128:(i + 1) * 128], pt[:, j * 128:(j + 1) * 128],
               C.Asc[:, j:j + 1], C.modfm[:, j:j + 1], ALU.mult, [(pt, j), C.Asc, C.modfm], op1=ALU.add)
    P.pop_scope()


def stage_out(P, C, w_out_d, xsrc, xdst, layer):
    C.GP = C.GP_l[layer]
    P.push_scope()
    wo = P.sb("wout", [128, 8, 1024], BF16)
    C.xt = [P.sb("xt%d" % i, [128, 1024], F32) for i in range(2)]
    C.yt = [P.sb("yt%d" % i, [128, 1024], F32) for i in range(2)]
    C.junk = P.sb("junk", [128, 1024], BF16)
    for k in range(8):
        for cb in range(8):
            st = C.wstage[C.wsi % 2]
            C.wsi += 1
            P.load(st, st[:, 0, :], w_out_d, w_out_d[k * 128:(k + 1) * 128, cb * 128:(cb + 1) * 128])
            cp(P, "pool" if cb % 2 else "dve", [(wo, (k, cb))], wo[:, k, cb * 128:(cb + 1) * 128], st[:, 0, :], [st])
    for k in range(8):
        P.load(C.big, C.big[:, k, :], C.oT_d, C.oT_d[k])
    for i in range(NT):
        py = C.psA[i % 2]
        for cb in range(2):
            for k in range(8):
                mm(P, [(py, cb)], py[:, cb * 512:(cb + 1) * 512], C.big[:, k, i * 128:(i + 1) * 128],
                   wo[:, k, cb * 512:(cb + 1) * 512], [C.big, wo], start=(k == 0), stop=(k == 7))
        col = NT + i
        act(P, [C.junk, (C.ssq, col)], C.junk[:, :], py[:, :], AF.Square, [py], accum_out=C.ssq[:, col:col + 1])
        rstd_from_ss(P, C, (C.ssq, col), C.ssq[:, col:col + 1], (C.rstd, col), C.rstd[:, col:col + 1], D)
        xt = C.xt[i % 2]
        P.load(xt, xt[:, :], xsrc, xsrc[i * 128:(i + 1) * 128, :])
        t = C.yt[i % 2]
        stt(P, "dve", [t], t[:, :], py[:, :], C.rstd[:, col:col + 1], C.GP[:, :], ALU.mult, ALU.mult,
            [py, (C.rstd, col), C.GP])
        tt(P, "pool", [t], t[:, :], t[:, :], xt[:, :], ALU.add, [t, xt])
        P.store(xdst, xdst[i * 128:(i + 1) * 128, :], t, t[:, :])
    P.pop_scope()


def blk_slice(dil, r, n, cnt=1):
    start = n * 128 * dil + r
    return slice(start, start + (128 * cnt - 1) * dil + 1, dil)


def proj_fm(P, C, w, ncols, ps_list, evac):
    hT = C.big
    for tb in range(8):
        tsl = slice(tb * 512, (tb + 1) * 512)
        ps = ps_list[tb % len(ps_list)]
        for k in range(8):
            mm(P, [ps], ps[0:ncols, :], w[:, k, 0:ncols], hT[:, k, tsl], [w, hT], start=(k == 0), stop=(k == 7))
        evac(ps, tb, tsl)


def stage_attn(P, C, w_in_d, Gd, maskd):
    P.push_scope()
    A = Ctx()
    A.w4 = [P.sb("aw%d" % i, [128, 8, 128], BF16) for i in range(4)]
    A.qT = P.sb("qT", [128, S], BF16)
    A.kT = P.sb("kT", [128, S], BF16)
    A.vT = P.sb("vT", [128, S], BF16)
    A.gT = P.sb("gT", [128, S], BF16)
    A.Vd = [P.sb("Vd%d" % i, [128, 32, 128], BF16) for i in range(2)]
    mset(P, "pool", [A.Vd[0]], A.Vd[0][:, :, 64:128], 1.0)
    mset(P, "pool", [A.Vd[1]], A.Vd[1][:, :, 0:64], 1.0)
    A.Gs = P.sb("Gs", [128, 768], F32)
    A.E = [P.sb("E%d" % i, [128, 3, 256], BF16) for i in range(2)]
    A.mask = P.sb("amask", [128, 256], F32)
    A.pex = [P.sb("pex%d" % i, [128, 1024], BF16) for i in range(2)]
    A.PT = [P.sb("PT%d" % i, [128, 1024], BF16) for i in range(2)]
    A.anum = P.sb("anum", [128, S], F32)
    A.aden = P.sb("aden", [128, S], F32)
    A.rden = [P.sb("rden%d" % i, [128, 512], F32) for i in range(2)]
    A.sqb = [P.sb("asqb%d" % i, [128, 512], BF16) for i in range(2)]
    A.blk1 = P.sb("ablk1", [128, 2], BF16)
    mset(P, "pool", [A.blk1], A.blk1[:, :], 0.0)
    mset(P, "pool", [A.blk1], A.blk1[0:64, 0:1], 1.0)
    mset(P, "pool", [A.blk1], A.blk1[64:128, 1:2], 1.0)
    A.mx = P.sb("amx", [2, 16], F32)
    A.m2 = P.sb("am2", [2, 4], F32)
    A.dg2 = P.sb("adg2", [2, 2], F32)
    A.nb = P.sb("anb", [128, 2], F32)
    A.oTc = [P.sb("oTc%d" % i, [128, 512], BF16) for i in range(2)]
    P.load(A.mask, A.mask[:, :], maskd, maskd[:, :])
    for pr in range(4):
        wq, wk, wv, wg = A.w4
        load_w_bf16(P, C, wq, wq[:, :, :], w_in_d, w_in_d[:, pr * 128:(pr + 1) * 128], 128)
        load_w_bf16(P, C, wk, wk[:, :, :], w_in_d, w_in_d[:, 512 + pr * 128: 512 + (pr + 1) * 128], 128)
        load_w_bf16(P, C, wv, wv[:, :, :], w_in_d, w_in_d[:, 1024 + pr * 128: 1024 + (pr + 1) * 128], 128)
        load_w_bf16(P, C, wg, wg[:, :, :], w_in_d, w_in_d[:, 2048 + pr * 128: 2048 + (pr + 1) * 128], 128)
        proj_fm(P, C, wq, 128, C.psB[0:2], lambda ps, tb, tsl: act(P, [(A.qT, tb)], A.qT[:, tsl], ps[:, :], AF.Copy, [ps], scale=0.125))
        proj_fm(P, C, wk, 128, C.psB[0:2], lambda ps, tb, tsl: cp(P, "dve", [(A.kT, tb)], A.kT[:, tsl], ps[:, :], [ps]))
        proj_fm(P, C, wv, 128, C.psB[0:2], lambda ps, tb, tsl: act(P, [(A.vT, tb)], A.vT[:, tsl], ps[:, :], AF.Copy, [ps]))
        proj_fm(P, C, wg, 128, C.psB[0:2], lambda ps, tb, tsl: act(P, [(A.gT, tb)], A.gT[:, tsl], ps[:, :], AF.Silu, [ps]))
        for wi, src in enumerate((A.qT, A.kT)):
            for tb in range(8):
                tsl = slice(tb * 512, (tb + 1) * 512)
                sqb = A.sqb[tb % 2]
                tt(P, "pool", [sqb], sqb[:, :], src[:, tsl], src[:, tsl], ALU.mult, [src])
                ps = C.psB[tb % 2]
                mm(P, [ps], ps[0:2, :], A.blk1[:, :], sqb[:, :], [A.blk1, sqb])
                P.op("dve", lambda e, ps=ps, wi=wi, tb=tb: e.reduce_max(A.mx[0:2, wi * 8 + tb: wi * 8 + tb + 1], ps[0:2, :], AX.X),
                     [ps], [(A.mx, (wi, tb))], cost=(0.7, 0.7))
        P.op("dve", lambda e: e.reduce_max(A.m2[0:2, 0:1], A.mx[0:2, 0:8], AX.X), [A.mx], [(A.m2, 0)], cost=(0.1, 0.1))
        P.op("dve", lambda e: e.reduce_max(A.m2[0:2, 1:2], A.mx[0:2, 8:16], AX.X), [A.mx], [(A.m2, 1)], cost=(0.1, 0.1))
        tt(P, "dve", [(A.m2, 2)], A.m2[0:2, 2:3], A.m2[0:2, 0:1], A.m2[0:2, 1:2], ALU.mult, [(A.m2, 0), (A.m2, 1)])
        act(P, [(A.m2, 3)], A.m2[0:2, 3:4], A.m2[0:2, 2:3], AF.Ln, [(A.m2, 2), C.epsb], bias=C.epsb[0:2, 0:1])
        act(P, [(A.m2, 3)], A.m2[0:2, 3:4], A.m2[0:2, 3:4], AF.Exp, [(A.m2, 3)], scale=0.5)
        ts(P, "dve", [(A.m2, 3)], A.m2[0:2, 3:4], A.m2[0:2, 3:4], -1.0, None, ALU.mult, [(A.m2, 3)])
        ts(P, "dve", [A.dg2], A.dg2[0:2, 0:2], C.identf[0:2, 0:2], A.m2[0:2, 3:4], None, ALU.mult, [C.identf, (A.m2, 3)])
        psn = C.psB[0]
        mm(P, [psn], psn[:, 0:2], C.ones_f[0:2, :], A.dg2[0:2, 0:2], [C.ones_f, A.dg2])
        cp(P, "dve", [A.nb], A.nb[:, :], psn[:, 0:2], [psn])
        for hh in range(2):
            hd = pr * 2 + hh
            P.load(A.Gs, A.Gs[:, :], Gd, Gd[hd])
            act(P, [A.Gs], A.Gs[:, :], A.Gs[:, :], AF.Exp, [A.Gs])
            tt(P, "dve", [A.E[hh]], A.E[hh][:, :, :], A.Gs[:, :].rearrange("p (c m) -> p c m", c=3),
               A.mask[:, :].unsqueeze(1).to_broadcast([128, 3, 256]), ALU.mult, [A.Gs, A.mask])
        gi = 0
        for ci, (dil, nb) in enumerate(CFGS):
            for bg in range(8):
                pt = C.psTs[bg % 2]
                for u in range(4):
                    bi = bg * 4 + u
                    r, n = bi // nb, bi % nb
                    tr(P, [(pt, u)], pt[:, u * 128:(u + 1) * 128], A.vT[:, blk_slice(dil, r, n)], C.ident[:, :],
                       [A.vT, C.ident])
                pv4 = pt[:, 0:512].rearrange("p (u d) -> p u d", u=4)
                act(P, [(A.Vd[0], bg)], A.Vd[0][:, bg * 4:(bg + 1) * 4, 0:64], pv4[:, :, 0:64], AF.Copy, [pt])
                cp(P, "dve", [(A.Vd[1], bg)], A.Vd[1][:, bg * 4:(bg + 1) * 4, 64:128], pv4[:, :, 64:128], [pt])
            for hh in range(2):
                rows = slice(hh * 64, (hh + 1) * 64)
                G = min(4, nb)
                for r in range(dil):
                    for n0 in range(0, nb, G):
                        pss = C.psA[gi % 2]
                        pex = A.pex[gi % 2]
                        PT = A.PT[gi % 2]
                        gi += 1
                        for g in range(G):
                            n = n0 + g
                            qs = blk_slice(dil, r, n)
                            if n > 0:
                                mm(P, [(pss, (g, 0))], pss[:, g * 256: g * 256 + 128],
                                   A.kT[rows, blk_slice(dil, r, n - 1)], A.qT[rows, qs], [A.kT, A.qT])
                            mm(P, [(pss, (g, 1))], pss[:, g * 256 + 128: g * 256 + 256], A.kT[rows, qs], A.qT[rows, qs],
                               [A.kT, A.qT])
                        act(P, [pex], pex[:, 0:G * 256], pss[:, 0:G * 256], AF.Exp, [pss, A.nb], bias=A.nb[:, hh:hh + 1])
                        tt(P, "dve", [PT], PT[:, 0:G * 256].rearrange("p (g m) -> p g m", g=G),
                           pex[:, 0:G * 256].rearrange("p (g m) -> p g m", g=G),
                           A.E[hh][:, ci, :].unsqueeze(1).to_broadcast([128, G, 256]), ALU.mult, [pex, A.E[hh]])
                        pn = C.psB[gi % 2]
                        Vh = A.Vd[hh]
                        for g in range(G):
                            n = n0 + g
                            kbs = [1] if n == 0 else [0, 1]
                            for ix, kb in enumerate(kbs):
                                bi = r * nb + (n - 1 + kb)
                                rhs = PT[:, g * 256 + kb * 128: g * 256 + (kb + 1) * 128]
                                mm(P, [(pn, g)], pn[:, g * 128:(g + 1) * 128], Vh[:, bi, :], rhs, [Vh, PT],
                                   start=(ix == 0), stop=(ix == len(kbs) - 1))
                        dsl = blk_slice(dil, r, n0, G)
                        orow = slice((1 - hh) * 64, (2 - hh) * 64)
                        if ci == 0:
                            cp(P, "dve", [(A.anum, hh)], A.anum[rows, dsl], pn[rows, 0:G * 128], [pn])
                            act(P, [(A.aden, hh)], A.aden[orow, dsl], pn[orow, 0:G * 128], AF.Copy, [pn])
                        else:
                            tt(P, "dve", [(A.anum, hh)], A.anum[rows, dsl], pn[rows, 0:G * 128], A.anum[rows, dsl],
                               ALU.add, [pn, (A.anum, hh)])
                            tt(P, "dve", [(A.aden, hh)], A.aden[orow, dsl], pn[orow, 0:G * 128], A.aden[orow, dsl],
                               ALU.add, [pn, (A.aden, hh)])
        for q4 in range(8):
            oc = A.oTc[q4 % 2]
            rd = A.rden[q4 % 2]
            csl = slice(q4 * 512, (q4 + 1) * 512)
            P.op("dve", lambda e, rd=rd, csl=csl: e.reciprocal(rd[0:64, :], A.aden[64:128, csl]), [A.aden], [(rd, 0)], cost=(3.0, 3.0))
            P.op("dve", lambda e, rd=rd, csl=csl: e.reciprocal(rd[64:128, :], A.aden[0:64, csl]), [A.aden], [(rd, 1)], cost=(3.0, 3.0))
            tt(P, "dve", [rd], rd[:, :], A.anum[:, csl], rd[:, :], ALU.mult, [A.anum, rd])
            tt(P, "pool", [oc], oc[:, :], rd[:, :], A.gT[:, csl], ALU.mult, [rd, A.gT])
            P.store(C.oT_d, C.oT_d[pr, :, csl], oc, oc[:, :], key=pr)
    P.pop_scope()


LSEG = 256


def sincos(P, V, th):
    I32 = mybir.dt.int32
    kf, ki = V("kf"), P.sb("s5_ki", [128, 16], I32)
    t = V("sc_t")
    ts(P, "dve", [t], t[:, :], th[:, :], 0.6366197723675814, None, ALU.mult, [th])
    cp(P, "dve", [ki], ki[:, :], t[:, :], [t])
    cp(P, "dve", [kf], kf[:, :], ki[:, :], [ki])
    r = V("sc_r")
    stt(P, "dve", [r], r[:, :], kf[:, :], -1.5707963705062866, th[:, :], ALU.mult, ALU.add, [kf, th])
    stt(P, "dve", [r], r[:, :], kf[:, :], 4.371139000186243e-08, r[:, :], ALU.mult, ALU.add, [kf, r])
    r2 = V("sc_r2")
    tt(P, "dve", [r2], r2[:, :], r[:, :], r[:, :], ALU.mult, [r])

    def horner(name, coefs):
        p = V(name)
        ts(P, "dve", [p], p[:, :], r2[:, :], coefs[0], coefs[1], ALU.mult, [r2], op1=ALU.add)
        for c in coefs[2:]:
            tt(P, "dve", [p], p[:, :], p[:, :], r2[:, :], ALU.mult, [p, r2])
            ts(P, "dve", [p], p[:, :], p[:, :], c, None, ALU.add, [p])
        return p
    sp = horner("sc_sp", [1.0 / 362880, -1.0 / 5040, 1.0 / 120, -1.0 / 6, 1.0])
    sr = V("sc_sr")
    tt(P, "dve", [sr], sr[:, :], sp[:, :], r[:, :], ALU.mult, [sp, r])
    cr = horner("sc_cr", [-1.0 / 3628800, 1.0 / 40320, -1.0 / 720, 1.0 / 24, -0.5, 1.0])
    fl, fi = V("sc_fl"), P.sb("s5_fi", [128, 16], I32)
    ts(P, "dve", [fl], fl[:, :], kf[:, :], -1.5, 0.25, ALU.add, [kf], op1=ALU.mult)
    cp(P, "dve", [fi], fi[:, :], fl[:, :], [fl])
    cp(P, "dve", [fl], fl[:, :], fi[:, :], [fi])
    q = V("sc_q")
    stt(P, "dve", [q], q[:, :], fl[:, :], -4.0, kf[:, :], ALU.mult, ALU.add, [fl, kf])
    qa, qab = V("sc_qa"), V("sc_qab")
    stt(P, "dve", [qa], qa[:, :], q[:, :], -1.0, q[:, :], ALU.add, ALU.mult, [q])
    stt(P, "dve", [qab], qab[:, :], q[:, :], -2.0, qa[:, :], ALU.add, ALU.mult, [q, qa])
    ts(P, "dve", [qab], qab[:, :], qab[:, :], 1.0 / 3.0, None, ALU.mult, [qab])
    cq, sq = V("sc_cq"), V("sc_sq")
    tt(P, "dve", [cq], cq[:, :], qab[:, :], q[:, :], ALU.subtract, [qab, q])
    ts(P, "dve", [cq], cq[:, :], cq[:, :], 1.0, None, ALU.add, [cq])
    tt(P, "dve", [sq], sq[:, :], qab[:, :], qa[:, :], ALU.subtract, [qab, qa])
    tt(P, "dve", [sq], sq[:, :], sq[:, :], q[:, :], ALU.add, [sq, q])
    co, si, t2 = V("sc_cos"), V("sc_sin"), V("sc_t2")
    tt(P, "dve", [co], co[:, :], cr[:, :], cq[:, :], ALU.mult, [cr, cq])
    tt(P, "dve", [t2], t2[:, :], sr[:, :], sq[:, :], ALU.mult, [sr, sq])
    tt(P, "dve", [co], co[:, :], co[:, :], t2[:, :], ALU.subtract, [co, t2])
    tt(P, "dve", [si], si[:, :], sr[:, :], cq[:, :], ALU.mult, [sr, cq])
    tt(P, "dve", [t2], t2[:, :], cr[:, :], sq[:, :], ALU.mult, [cr, sq])
    tt(P, "dve", [si], si[:, :], si[:, :], t2[:, :], ALU.add, [si, t2])
    return co, si


def cmul_small(P, V, name, ar_, ai_, br_, bi_, sl=None):
    o_r, o_i, t = V(name + "_r"), V(name + "_i"), V(name + "_t")
    tt(P, "dve", [o_r], o_r[:, :], ar_[:, :], br_[:, :], ALU.mult, [ar_, br_])
    tt(P, "dve", [t], t[:, :], ai_[:, :], bi_[:, :], ALU.mult, [ai_, bi_])
    tt(P, "dve", [o_r], o_r[:, :], o_r[:, :], t[:, :], ALU.subtract, [o_r, t])
    tt(P, "dve", [o_i], o_i[:, :], ar_[:, :], bi_[:, :], ALU.mult, [ar_, bi_])
    tt(P, "dve", [t], t[:, :], ai_[:, :], br_[:, :], ALU.mult, [ai_, br_])
    tt(P, "dve", [o_i], o_i[:, :], o_i[:, :], t[:, :], ALU.add, [o_i, t])
    return o_r, o_i


def stage_s5_proj(P, C, w_in_d):
    P.push_scope()
    w = P.sb("s5w", [128, 8, 128], BF16)
    ub = [P.sb("s5ub%d" % i, [128, 512], F32) for i in range(2)]
    gb = [P.sb("s5gb%d" % i, [128, 512], BF16) for i in range(2)]
    for cq in range(4):
        load_w_bf16(P, C, w, w[:, :, :], w_in_d, w_in_d[:, 1536 + cq * 128: 1536 + (cq + 1) * 128], 128)

        def ev_u(ps, tb, tsl, cq=cq):
            t = ub[tb % 2]
            act(P, [t], t[:, :], ps[:, :], AF.Copy, [ps])
            P.store(C.uT_d, C.uT_d[cq, :, tsl], t, t[:, :], key=cq)
        proj_fm(P, C, w, 128, C.psB, ev_u)
        load_w_bf16(P, C, w, w[:, :, :], w_in_d, w_in_d[:, 2560 + cq * 128: 2560 + (cq + 1) * 128], 128)

        def ev_g(ps, tb, tsl, cq=cq):
            t = gb[tb % 2]
            act(P, [t], t[:, :], ps[:, :], AF.Silu, [ps])
            P.store(C.gB_d, C.gB_d[cq, :, tsl], t, t[:, :], key=cq)
        proj_fm(P, C, w, 128, C.psB, ev_g)
    P.pop_scope()


def stage_s5(P, C, Dm):
    P.push_scope()
    nv = [0]

    def V(name):
        nv[0] += 1
        return P.sb("s5v_%s_%d" % (name, nv[0]), [128, 16], F32)

    def ldv(name, d):
        v = V(name)
        P.load(v, v[:, :], d, d[:, :])
        return v
    ar, ai, ldt = ldv("ar", Dm["ar"]), ldv("ai", Dm["ai"]), ldv("ldt", Dm["ldt"])
    dt, dar, mag, th = V("dt"), V("dar"), V("mag"), V("th")
    act(P, [dt], dt[:, :], ldt[:, :], AF.Exp, [ldt])
    tt(P, "dve", [dar], dar[:, :], dt[:, :], ar[:, :], ALU.mult, [dt, ar])
    act(P, [mag], mag[:, :], dar[:, :], AF.Exp, [dar])
    tt(P, "dve", [th], th[:, :], dt[:, :], ai[:, :], ALU.mult, [dt, ai])
    co, si = sincos(P, V, th)
    lr, li = V("lr"), V("li")
    tt(P, "dve", [lr], lr[:, :], mag[:, :], co[:, :], ALU.mult, [mag, co])
    tt(P, "dve", [li], li[:, :], mag[:, :], si[:, :], ALU.mult, [mag, si])
    lr1, den, fr, fi_, t = V("lr1"), V("den"), V("fr"), V("fi"), V("t")
    ts(P, "dve", [lr1], lr1[:, :], lr[:, :], -1.0, None, ALU.add, [lr])
    tt(P, "dve", [den], den[:, :], ar[:, :], ar[:, :], ALU.mult, [ar])
    tt(P, "dve", [t], t[:, :], ai[:, :], ai[:, :], ALU.mult, [ai])
    tt(P, "dve", [den], den[:, :], den[:, :], t[:, :], ALU.add, [den, t])
    P.op("dve", lambda e: e.reciprocal(den[:, :], den[:, :]), [den], [den])
    tt(P, "dve", [fr], fr[:, :], lr1[:, :], ar[:, :], ALU.mult, [lr1, ar])
    tt(P, "dve", [t], t[:, :], li[:, :], ai[:, :], ALU.mult, [li, ai])
    tt(P, "dve", [fr], fr[:, :], fr[:, :], t[:, :], ALU.add, [fr, t])
    tt(P, "dve", [fr], fr[:, :], fr[:, :], den[:, :], ALU.mult, [fr, den])
    tt(P, "dve", [fi_], fi_[:, :], li[:, :], ar[:, :], ALU.mult, [li, ar])
    tt(P, "dve", [t], t[:, :], lr1[:, :], ai[:, :], ALU.mult, [lr1, ai])
    tt(P, "dve", [fi_], fi_[:, :], fi_[:, :], t[:, :], ALU.subtract, [fi_, t])
    tt(P, "dve", [fi_], fi_[:, :], fi_[:, :], den[:, :], ALU.mult, [fi_, den])
    Bre, Bim = P.sb("s5Bre", [128, 16, 16], F32), P.sb("s5Bim", [128, 16, 16], F32)
    P.load(Bre, Bre[:, :, :], Dm["b_re"], Dm["b_re"][:, :, :])
    P.load(Bim, Bim[:, :, :], Dm["b_im"], Dm["b_im"][:, :, :])
    Bbr, Bbi, Bt = P.sb("s5Bbr", [128, 16, 16], F32), P.sb("s5Bbi", [128, 16, 16], F32), P.sb("s5Bt", [128, 16, 16], F32)
    bc = lambda v: v[:, :].unsqueeze(2).to_broadcast([128, 16, 16])
    tt(P, "dve", [Bbr], Bbr[:, :, :], Bre[:, :, :], bc(fr), ALU.mult, [Bre, fr])
    tt(P, "dve", [Bt], Bt[:, :, :], Bim[:, :, :], bc(fi_), ALU.mult, [Bim, fi_])
    tt(P, "dve", [Bbr], Bbr[:, :, :], Bbr[:, :, :], Bt[:, :, :], ALU.subtract, [Bbr, Bt])
    tt(P, "dve", [Bbi], Bbi[:, :, :], Bim[:, :, :], bc(fr), ALU.mult, [Bim, fr])
    tt(P, "dve", [Bt], Bt[:, :, :], Bre[:, :, :], bc(fi_), ALU.mult, [Bre, fi_])
    tt(P, "dve", [Bbi], Bbi[:, :, :], Bbi[:, :, :], Bt[:, :, :], ALU.add, [Bbi, Bt])
    BpT = P.sb("s5BpT", [128, 16, 2, 128], BF16)
    Bblk = [P.sb("s5Bblk%d" % i, [128, 128], F32) for i in range(2)]
    n = 0
    for q in range(16):
        base = 32 * (q % 4)
        for ri, src in enumerate((Bbr, Bbi)):
            bb = Bblk[n % 2]
            ps = C.psB[n % 2]
            n += 1
            mset(P, "pool", [bb], bb[:, :], 0.0)
            cp(P, "pool", [bb], bb[0:64, base:base + 16], src[0:64, q, :], [src, bb])
            cp(P, "pool", [bb], bb[64:128, base + 16:base + 32], src[64:128, q, :], [src, bb])
            tr(P, [ps], ps[:, 0:128], bb[:, :], C.identf[:, :], [bb, C.identf])
            act(P, [(BpT, (q, ri))], BpT[:, q, ri, :], ps[:, 0:128], AF.Copy, [ps])
    CpT = P.sb("s5CpT", [128, 16, 2, 128], BF16)
    Cre, Cim = P.sb("s5Cre", [128, 16, 16], F32), P.sb("s5Cim", [128, 16, 16], F32)
    P.load(Cre, Cre[:, :, :], Dm["cT_re"], Dm["cT_re"][:, :, :])
    P.load(Cim, Cim[:, :, :], Dm["cT_im"], Dm["cT_im"][:, :, :])
    ts(P, "dve", [Cim], Cim[:, :, :], Cim[:, :, :], -1.0, None, ALU.mult, [Cim])
    mset(P, "pool", [CpT], CpT[:, :, :, :], 0.0)
    for q in range(16):
        base = 32 * (q % 4)
        for ri, src in enumerate((Cre, Cim)):
            cp(P, "dve", [CpT], CpT[0:64, q, ri, base:base + 16], src[0:64, q, :], [src, CpT])
            cp(P, "dve", [CpT], CpT[64:128, q, ri, base + 16:base + 32], src[64:128, q, :], [src, CpT])
    L = LSEG
    ct, st = P.sb("s5ct", [128, 4, L], F32), P.sb("s5st", [128, 4, L], F32)
    tA, tB = P.sb("s5tA", [128, 4, L], F32), P.sb("s5tB", [128, 4, L], F32)
    btr, bti = P.sb("s5btr", [128, 4, L], F32), P.sb("s5bti", [128, 4, L], F32)
    xtr, xti = P.sb("s5xtr", [128, 4, L], F32), P.sb("s5xti", [128, 4, L], F32)
    xr, xi = P.sb("s5xr", [128, 4, L], BF16), P.sb("s5xi", [128, 4, L], BF16)
    uf = [P.sb("s5uf%d" % i, [128, L], F32) for i in range(2)]
    ubf = [P.sb("s5ubf%d" % i, [128, L], BF16) for i in range(2)]
    yv, y2, yw, ysg = (P.sb("s5y%d" % i, [128, L], F32) for i in range(4))
    car_r, car_i, cl_t = P.sb("s5car_r", [128, 4], F32), P.sb("s5car_i", [128, 4], F32), P.sb("s5cl_t", [128, 4], F32)
    ncr0, nci0 = P.sb("s5ncr", [128, 4], F32), P.sb("s5nci", [128, 4], F32)
    cl_t2 = P.sb("s5cl_t2", [128, 4], F32)
    dfm = P.sb("s5dfm", [128, 4], F32)
    P.load(dfm, dfm[:, :], Dm["d_fm"], Dm["d_fm"][:, :])
    ygT = C.big
    for cq in range(4):
        qs = slice(4 * cq, 4 * cq + 4)
        ur, ui = co, si
        mset(P, "pool", [ct], ct[:, :, 0:1], 1.0)
        mset(P, "pool", [st], st[:, :, 0:1], 0.0)
        cp(P, "dve", [ct], ct[:, :, 1:2], co[:, qs].unsqueeze(2), [co, ct])
        cp(P, "dve", [st], st[:, :, 1:2], si[:, qs].unsqueeze(2), [si, st])
        k = 1
        while (1 << k) < L:
            nn = 1 << k
            ur, ui = cmul_small(P, V, "u%d_%d" % (cq, k), ur, ui, ur, ui)
            bcu = lambda v: v[:, qs].unsqueeze(2).to_broadcast([128, 4, nn])
            tt(P, "dve", [tA], tA[:, :, 0:nn], ct[:, :, 0:nn], bcu(ur), ALU.mult, [ct, ur])
            tt(P, "dve", [tB], tB[:, :, 0:nn], st[:, :, 0:nn], bcu(ui), ALU.mult, [st, ui])
            tt(P, "dve", [ct], ct[:, :, nn:2 * nn], tA[:, :, 0:nn], tB[:, :, 0:nn], ALU.subtract, [tA, tB, ct])
            tt(P, "dve", [tA], tA[:, :, 0:nn], ct[:, :, 0:nn], bcu(ui), ALU.mult, [ct, ui])
            tt(P, "dve", [tB], tB[:, :, 0:nn], st[:, :, 0:nn], bcu(ur), ALU.mult, [st, ur])
            tt(P, "dve", [st], st[:, :, nn:2 * nn], tA[:, :, 0:nn], tB[:, :, 0:nn], ALU.add, [tA, tB, st])
            k += 1
        uLr, uLi = cmul_small(P, V, "uL%d" % cq, ur, ui, ur, ui)
        cars = ((car_r, car_i), (ncr0, nci0))
        mset(P, "pool", [car_r], car_r[:, :], 0.0)
        mset(P, "pool", [car_i], car_i[:, :], 0.0)
        for seg in range(S // L):
            tsl = slice(seg * L, (seg + 1) * L)
            car_a, car_b = cars[seg % 2]
            u_f, u_b = uf[seg % 2], ubf[seg % 2]
            P.load(u_f, u_f[:, :], C.uT_d, C.uT_d[cq, :, tsl], key=cq)
            cp(P, "pool", [u_b], u_b[:, :], u_f[:, :], [u_f])
            pre, pim = C.psA[0], C.psA[1]
            for pr in range(4):
                mm(P, [pre], pre[:, pr * L:(pr + 1) * L], BpT[:, 4 * cq + pr, 0, :], u_b[:, :], [BpT, u_b])
                mm(P, [pim], pim[:, pr * L:(pr + 1) * L], BpT[:, 4 * cq + pr, 1, :], u_b[:, :], [BpT, u_b])
            v3 = lambda b: b[:, :, :]
            p3 = lambda b: b[:, 0:4 * L].rearrange("p (a t) -> p a t", a=4)
            tt(P, "dve", [tA], v3(tA), p3(pre), v3(ct), ALU.mult, [pre, ct])
            tt(P, "dve", [tB], v3(tB), p3(pim), v3(st), ALU.mult, [pim, st])
            tt(P, "dve", [btr], v3(btr), v3(tA), v3(tB), ALU.add, [tA, tB])
            tt(P, "dve", [tA], v3(tA), p3(pim), v3(ct), ALU.mult, [pim, ct])
            tt(P, "dve", [tB], v3(tB), p3(pre), v3(st), ALU.mult, [pre, st])
            tt(P, "dve", [bti], v3(bti), v3(tA), v3(tB), ALU.subtract, [tA, tB])
            for pr in range(4):
                q = 4 * cq + pr
                for (src, dst, car) in ((btr, xtr, car_a), (bti, xti, car_b)):
                    P.op("dve", lambda e, src=src, dst=dst, car=car, pr=pr, q=q: e.tensor_tensor_scan(
                        dst[:, pr, :], mag[:, q:q + 1].to_broadcast([128, L]), src[:, pr, :], car[:, pr:pr + 1],
                        ALU.mult, ALU.add), [src, mag, car], [(dst, pr)])
            lre, lim = xtr[:, :, L - 1], xti[:, :, L - 1]
            ncr, nci = cars[(seg + 1) % 2]
            tt(P, "dve", [ncr], ncr[:, :], lre, uLr[:, qs], ALU.mult, [xtr, uLr])
            tt(P, "dve", [cl_t], cl_t[:, :], lim, uLi[:, qs], ALU.mult, [xti, uLi])
            tt(P, "dve", [ncr], ncr[:, :], ncr[:, :], cl_t[:, :], ALU.subtract, [ncr, cl_t])
            tt(P, "dve", [nci], nci[:, :], lre, uLi[:, qs], ALU.mult, [xtr, uLi])
            tt(P, "dve", [cl_t2], cl_t2[:, :], lim, uLr[:, qs], ALU.mult, [xti, uLr])
            tt(P, "dve", [nci], nci[:, :], nci[:, :], cl_t2[:, :], ALU.add, [nci, cl_t2])
            tt(P, "dve", [tA], v3(tA), v3(xtr), v3(ct), ALU.mult, [xtr, ct])
            tt(P, "pool", [tB], v3(tB), v3(xti), v3(st), ALU.mult, [xti, st])
            tt(P, "dve", [xr], v3(xr), v3(tA), v3(tB), ALU.subtract, [tA, tB])
            tt(P, "pool", [btr], v3(btr), v3(xtr), v3(st), ALU.mult, [xtr, st])
            tt(P, "pool", [bti], v3(bti), v3(xti), v3(ct), ALU.mult, [xti, ct])
            tt(P, "pool", [xi], v3(xi), v3(btr), v3(bti), ALU.add, [btr, bti])
            py = C.psB[seg % 2]
            for pr in range(4):
                q = 4 * cq + pr
                mm(P, [py], py[:, 0:L], CpT[:, q, 0, :], xr[:, pr, :], [CpT, xr], start=(pr == 0), stop=False)
                mm(P, [py], py[:, 0:L], CpT[:, q, 1, :], xi[:, pr, :], [CpT, xi], start=False, stop=(pr == 3))
            stt(P, "dve", [yv], yv[:, :], u_f[:, :], dfm[:, cq:cq + 1], py[:, 0:L], ALU.mult, ALU.add, [u_f, dfm, py])
            tt(P, "pool", [y2], y2[:, :], yv[:, :], yv[:, :], ALU.mult, [yv])
            ts(P, "pool", [y2], y2[:, :], y2[:, :], 0.044715, 1.0, ALU.mult, [y2], op1=ALU.add)
            tt(P, "pool", [yw], yw[:, :], y2[:, :], yv[:, :], ALU.mult, [y2, yv])
            act(P, [ysg], ysg[:, :], yw[:, :], AF.Sigmoid, [yw], scale=1.5957691216057308)
            tt(P, "pool", [(ygT, ("yg", cq, seg))], ygT[:, cq, tsl], yv[:, :], ysg[:, :], ALU.mult, [yv, ysg])
    gw = P.sb("s5gw", [128, 4, 128], BF16)
    gbias = P.sb("s5gbias", [128, 4], F32)
    P.load(gbias, gbias[:, :], Dm["glu_b_fm"], Dm["glu_b_fm"][:, :])
    gT = P.sb("s5gT", [128, S], BF16)
    sg = [P.sb("s5sg%d" % i, [128, 512], F32) for i in range(2)]
    oc_t = [P.sb("s5oc%d" % i, [128, 512], BF16) for i in range(2)]
    for oc in range(4):
        st_ = C.wstage[C.wsi % 2]
        C.wsi += 1
        P.load(st_, st_[:, 0:4, :], Dm["glu_w"], Dm["glu_w"][:, oc * 128:(oc + 1) * 128].rearrange("(k p) c -> p k c", p=128))
        cp(P, "dve", [gw], gw[:, :, :], st_[:, 0:4, :], [st_])
        P.load(gT, gT[:, :], C.gB_d, C.gB_d[oc], key=oc)
        for tb in range(8):
            tsl = slice(tb * 512, (tb + 1) * 512)
            ps = C.psB[tb % 2]
            for cq in range(4):
                mm(P, [ps], ps[:, :], gw[:, cq, :], ygT[:, cq, tsl], [gw, ygT], start=(cq == 0), stop=(cq == 3))
            s_ = sg[tb % 2]
            o_ = oc_t[tb % 2]
            act(P, [s_], s_[:, :], ps[:, :], AF.Sigmoid, [ps, gbias], bias=gbias[:, oc:oc + 1])
            tt(P, "dve", [s_], s_[:, :], s_[:, :], ygT[:, oc, tsl], ALU.mult, [s_, ygT])
            tt(P, "pool", [o_], o_[:, :], s_[:, :], gT[:, tsl], ALU.mult, [s_, gT])
            P.store(C.oT_d, C.oT_d[4 + oc, :, tsl], o_, o_[:, :], key=4 + oc)
    P.pop_scope()


def stage_gdn_proj(P, C, w_in_d, Dg, G):
    P.push_scope()
    w = P.sb("gdw", [128, 8, 128], BF16)
    convw = P.sb("gconvw", [128, 24, 4], F32)
    P.load(convw, convw[:, :, :], Dg["conv_fm"], Dg["conv_fm"][:, :, :])
    zp = [P.sb("gzp%d" % i, [128, 515], F32) for i in range(3)]
    acc = [P.sb("gacc%d" % i, [128, 512], F32) for i in range(4)]
    sl = [P.sb("gsl%d" % i, [128, 512], F32) for i in range(4)]
    sq = [P.sb("gsq%d" % i, [128, 512], F32) for i in range(4)]
    rs = [P.sb("grs%d" % i, [128, 512], F32) for i in range(4)]
    ob = [P.sb("gob%d" % i, [128, 512], BF16) for i in range(4)]
    dsts = (G.qT_d, G.kT_d, G.vT_d, G.gT_d)
    n = 0
    for typ in range(4):
        for hd in range(8):
            ch = typ * 8 + hd
            load_w_bf16(P, C, w, w[:, :, :], w_in_d, w_in_d[:, ch * 128:(ch + 1) * 128], 128)
            for tb in range(8):
                tsl = slice(tb * 512, (tb + 1) * 512)
                ps = (C.psB[0], C.psB[1], C.psTf[0])[n % 3]
                for k in range(8):
                    mm(P, [ps], ps[:, :], w[:, k, :], C.big[:, k, tsl], [w, C.big], start=(k == 0), stop=(k == 7))
                o_ = ob[n % 4]
                if typ == 3:
                    act(P, [o_], o_[:, :], ps[:, :], AF.Silu, [ps])
                else:
                    z, zprev = zp[tb % 3], zp[(tb + 2) % 3]
                    if tb == 0:
                        mset(P, "pool", [z], z[:, 0:3], 0.0)
                    else:
                        cp(P, "pool", [z], z[:, 0:3], zprev[:, 512:515], [zprev, z])
                    act(P, [z], z[:, 3:515], ps[:, :], AF.Copy, [ps, z])
                    a_ = acc[n % 4]
                    ts(P, "dve", [a_], a_[:, :], z[:, 3:515], convw[:, ch, 3:4], None, ALU.mult, [z, convw])
                    for j in (2, 1, 0):
                        stt(P, "dve", [a_], a_[:, :], z[:, j:j + 512], convw[:, ch, j:j + 1], a_[:, :], ALU.mult, ALU.add,
                            [z, convw, a_])
                    if typ == 2:
                        act(P, [o_], o_[:, :], a_[:, :], AF.Silu, [a_])
                    else:
                        s_, q_, r_ = sl[n % 4], sq[n % 4], rs[n % 4]
                        act(P, [s_], s_[:, :], a_[:, :], AF.Silu, [a_])
                        tt(P, "pool", [q_], q_[:, :], s_[:, :], s_[:, :], ALU.mult, [s_])
                        pss = (C.psA[0], C.psA[1], C.psTf[1])[n % 3]
                        mm(P, [pss], pss[:, 0:512], C.ones_f[:, :], q_[:, :], [C.ones_f, q_])
                        act(P, [r_], r_[:, :], pss[:, 0:512], AF.Ln, [pss, C.epsb], bias=C.epsb[:, 0:1])
                        act(P, [r_], r_[:, :], r_[:, :], AF.Exp, [r_], scale=-0.5)
                        stt(P, "dve", [o_], o_[:, :], s_[:, :], (128.0 ** -0.5) if typ == 0 else 1.0, r_[:, :],
                            ALU.mult, ALU.mult, [s_, r_])
                P.store(dsts[typ], dsts[typ][hd, :, tsl], o_, o_[:, :], key=hd)
                n += 1
    w8 = P.sb("gdw8", [128, 8, 8], BF16)
    for (c0, dst) in ((4096, G.R0), (4104, G.R1)):
        st_ = C.wstage[C.wsi % 2]
        C.wsi += 1
        P.load(st_, st_[:, :, 0:8], w_in_d, w_in_d[:, c0:c0 + 8].rearrange("(k p) c -> p k c", p=128))
        cp(P, "dve", [w8], w8[:, :, :], st_[:, :, 0:8], [st_])
        for tb in range(8):
            tsl = slice(tb * 512, (tb + 1) * 512)
            ps = C.psB[tb % 2]
            for k in range(8):
                mm(P, [ps], ps[0:8, :], w8[:, k, :], C.big[:, k, tsl], [w8, C.big], start=(k == 0), stop=(k == 7))
            act(P, [(dst, tb)], dst[0:8, tsl], ps[0:8, :], AF.Copy, [ps])
    P.pop_scope()


def stage_gdn_rows(P, C, Dg, G):
    P.push_scope()
    R0, R1 = G.R0, G.R1
    R2, R3 = P.sb("gR2", [8, S], F32), P.sb("gR3", [8, S], F32)
    alog, dtb, nA = P.sb("galog", [8, 1], F32), P.sb("gdtb", [8, 1], F32), P.sb("gnA", [8, 1], F32)
    P.load(alog, alog[:, :], Dg["a_log"], Dg["a_log"][:, :])
    P.load(dtb, dtb[:, :], Dg["dt_bias"], Dg["dt_bias"][:, :])
    act(P, [nA], nA[:, :], alog[:, :], AF.Exp, [alog])
    ts(P, "dve", [nA], nA[:, :], nA[:, :], -1.0, None, ALU.mult, [nA])
    act(P, [R0], R0[:, :], R0[:, :], AF.Sigmoid, [R0])
    act(P, [R1], R1[:, :], R1[:, :], AF.Exp, [R1, dtb], bias=dtb[:, 0:1])
    act(P, [R1], R1[:, :], R1[:, :], AF.Ln, [R1], bias=1.0)
    ts(P, "dve", [R1], R1[:, :], R1[:, :], nA[:, 0:1], None, ALU.mult, [R1, nA])
    mset(P, "pool", [R2], R2[:, :], 1.0)
    mset(P, "pool", [R2], R2[:, 0:S:64], 0.0)
    P.op("dve", lambda e: e.tensor_tensor_scan(R3[:, :], R2[:, :], R1[:, :], 0.0, ALU.mult, ALU.add), [R2, R1], [R3])

    def to_cols(row, kq):
        for t in range(NT):
            ps = C.psB[t % 2]
            tr(P, [ps], ps[:, 0:8], row[0:8, t * 128:(t + 1) * 128], C.identf[0:8, 0:8], [row, C.identf])
            act(P, [(G.cols, (t, kq))], G.cols[:, t, kq, :], ps[:, 0:8], AF.Copy, [ps])
    to_cols(R3, 0)
    to_cols(R0, 1)
    act(P, [R1], R1[:, :], R3[:, :], AF.Exp, [R3])
    tt(P, "dve", [R2], R2[:, :], R0[:, :], R1[:, :], ALU.mult, [R0, R1])
    to_cols(R2, 2)
    gc3 = R3[:, :].rearrange("p (c t) -> p c t", t=64)
    gl = R3[:, 63:S:64]
    tt(P, "dve", [R2], R2[:, :].rearrange("p (c t) -> p c t", t=64), gl.unsqueeze(2).to_broadcast([8, 64, 64]), gc3,
       ALU.subtract, [R3])
    act(P, [R2], R2[:, :], R2[:, :], AF.Exp, [R2])
    to_cols(R2, 3)
    dlrow = P.sb("gdlrow", [8, 64], F32)
    act(P, [dlrow], dlrow[:, :], gl, AF.Exp, [R3])
    for h in range(8):
        ps = C.psB[h % 2]
        mm(P, [ps], ps[:, 0:64], G.sel8[0:8, h, :], dlrow[0:8, :], [G.sel8, dlrow])
        act(P, [(G.DL, h)], G.DL[:, h, :], ps[:, 0:64], AF.Copy, [ps])
    ts(P, "dve", [R0], R0[:, :], R3[:, :], -1.0, None, ALU.mult, [R3])
    P.pop_scope()


def stage_gdn_main(P, C, Dg, G):
    P.push_scope()
    big = C.big
    sets = []
    for si in range(2):
        vcnt = [0]
        F3 = lambda nm: P.sb("g1%s_%d" % (nm, si), [128, 8, 128], F32)

        def B3(nm, si=si, vcnt=vcnt):
            if si == 0:
                return P.sb("g1%s_%d" % (nm, si), [128, 8, 128], BF16)
            k = vcnt[0]
            vcnt[0] += 1
            ap = big.t[:, 4 + k // 4, (k % 4) * 1024:(k % 4 + 1) * 1024].rearrange("p (a t) -> p a t", a=8)
            bb = Buf("g1v%s" % nm, ap, "sb")
            P.bufs.append(bb)
            return bb
        T = Ctx()
        T.Kbe, T.Kd, T.bV, T.ADf, T.ADT, T.Rb, T.WT = (B3(n) for n in ("Kbe", "Kd", "bV", "ADf", "ADT", "Rb", "WT"))
        T.E_, T.Rr = (F3(n) for n in ("E", "R"))
        T.Nn, T.Xx = B3("N"), B3("X")
        T.Nk, T.Xk = [B3("Nk%d" % i) for i in range(2)], [B3("Xk%d" % i) for i in range(2)]
        T.U = P.sb("g1U_%d" % si, [64, 16, 128], F32)
        sets.append(T)
    gcount = [0]
    madd, m01 = P.sb("gmadd", [128, 128], F32), P.sb("gm01", [128, 128], F32)
    P.load(madd, madd[:, :], Dg["maskadd"], Dg["maskadd"][:, :])
    P.load(m01, m01[:, :], Dg["strict01"], Dg["strict01"][:, :])
    qT, kT, vT, qgT = (big[:, i, :] for i in range(4))
    bc8 = lambda ap2: ap2.unsqueeze(2).to_broadcast([128, 8, 128])
    bcm = lambda m: m[:, :].unsqueeze(1).to_broadcast([128, 8, 128])
    v3 = lambda b: b[:, :, :]
    p3 = lambda ps: ps[:, 0:1024].rearrange("p (a t) -> p a t", a=8)
    for hd in range(8):
        for i_, d_ in enumerate((G.qT_d, G.kT_d, G.vT_d)):
            P.load(big, big[:, i_, :], d_, d_[hd], key=hd)
        for tb in range(8):
            tsl = slice(tb * 512, (tb + 1) * 512)
            ps = C.psB[tb % 2]
            mm(P, [ps], ps[:, :], G.sel8[0:8, hd, :], G.R1[0:8, tsl], [G.sel8, G.R1])
            tt(P, "dve", [(big, ("qg", tb))], big[:, 3, tsl], ps[:, :], big[:, 0, tsl], ALU.mult, [ps, (big, ("in", 0))])
        P.store(G.qg_d, G.qg_d[hd], big, big[:, 3, :], key=hd)
        for g in range(4):
            T = sets[gcount[0] % 2]
            gcount[0] += 1
            Kbe, Kd, bV, ADf, ADT, Rb, WT = T.Kbe, T.Kd, T.bV, T.ADf, T.ADT, T.Rb, T.WT
            E_, Rr, Nn, Xx, Nk, Xk, U = T.E_, T.Rr, T.Nn, T.Xx, T.Nk, T.Xk, T.U
            t0 = g * 8
            tsls = [slice((t0 + u) * 128, (t0 + u + 1) * 128) for u in range(8)]
            col = lambda kq: bc8(G.cols[:, t0:t0 + 8, kq, hd])
            pk_, pv_ = C.psTs
            for u in range(8):
                tr(P, [pk_], pk_[:, u * 128:(u + 1) * 128], kT[:, tsls[u]], C.ident[:, :], [big, C.ident])
            for u in range(8):
                tr(P, [pv_], pv_[:, u * 128:(u + 1) * 128], vT[:, tsls[u]], C.ident[:, :], [big, C.ident])
            tt(P, "dve", [Kbe], v3(Kbe), p3(pk_), col(2), ALU.mult, [pk_, G.cols])
            tt(P, "dve", [Kd], v3(Kd), p3(pk_), col(3), ALU.mult, [pk_, G.cols])
            tt(P, "dve", [bV], v3(bV), p3(pv_), col(1), ALU.mult, [pv_, G.cols])
            for hlf in range(2):
                pe_ = C.psB[hlf]
                mm(P, [pe_], pe_[:, :], G.sel8[0:8, hd, :], G.R0[0:8, (t0 + 4 * hlf) * 128:(t0 + 4 * hlf + 4) * 128], [G.sel8, G.R0])
                tt(P, "dve", [(E_, hlf)], E_[:, 4 * hlf:4 * hlf + 4, :], pe_[:, :].rearrange("p (a t) -> p a t", a=4),
                   madd[:, :].unsqueeze(1).to_broadcast([128, 4, 128]), ALU.add, [pe_, madd])
            tt(P, "pool", [E_], v3(E_), v3(E_), col(0), ALU.add, [E_, G.cols])
            act(P, [E_], v3(E_), v3(E_), AF.Exp, [E_])
            pkk, pa = C.psA
            for u in range(8):
                mm(P, [pkk], pkk[:, u * 128:(u + 1) * 128], kT[:, tsls[u]], kT[:, tsls[u]], [big])
            for u in range(8):
                mm(P, [pa], pa[:, u * 128:(u + 1) * 128], qT[:, tsls[u]], kT[:, tsls[u]], [big])
            tt(P, "dve", [ADf], v3(ADf), p3(pa), v3(E_), ALU.mult, [pa, E_])
            tt(P, "dve", [Nn], v3(Nn), p3(pkk), v3(E_), ALU.mult, [pkk, E_])
            tt(P, "pool", [Nn], v3(Nn), v3(Nn), col(1), ALU.mult, [Nn, G.cols])
            tt(P, "pool", [Nn], v3(Nn), v3(Nn), bcm(m01), ALU.mult, [Nn, m01])
            for u in range(8):
                tr(P, [pk_], pk_[:, u * 128:(u + 1) * 128], ADf[:, u, :], C.ident[:, :], [ADf, C.ident])
            act(P, [ADT], v3(ADT), p3(pk_), AF.Copy, [pk_])
            for u in range(8):
                tr(P, [pv_], pv_[:, u * 128:(u + 1) * 128], Nn[:, u, :], C.ident[:, :], [Nn, C.ident])
            act(P, [Xx], v3(Xx), p3(pv_), AF.Copy, [pv_])
            tt(P, "dve", [Rr], v3(Rr), bcm(C.identf), p3(pv_), ALU.subtract, [C.identf, pv_])
            act(P, [Rb], v3(Rb), v3(Rr), AF.Copy, [Rr])
            Ncur, Xcur = Nn, Xx
            for lv in range(1, 6):
                nk, xk = Nk[lv % 2], Xk[lv % 2]
                for u in range(8):
                    mm(P, [pa], pa[:, u * 128:(u + 1) * 128], Xcur[:, u, :], Ncur[:, u, :], [Xcur, Ncur])
                act(P, [nk], v3(nk), p3(pa), AF.Copy, [pa])
                if lv < 5:
                    for u in range(8):
                        mm(P, [pkk], pkk[:, u * 128:(u + 1) * 128], Ncur[:, u, :], Xcur[:, u, :], [Xcur, Ncur])
                    cp(P, "dve", [xk], v3(xk), p3(pkk), [pkk])
                for hlf in range(2):
                    pr_ = C.psB[hlf]
                    for u in range(4):
                        mm(P, [pr_], pr_[:, u * 128:(u + 1) * 128], nk[:, 4 * hlf + u, :], Rb[:, 4 * hlf + u, :], [nk, Rb])
                for hlf in range(2):
                    pr_ = C.psB[hlf]
                    tt(P, "dve", [(Rr, hlf)], Rr[:, 4 * hlf:4 * hlf + 4, :], Rr[:, 4 * hlf:4 * hlf + 4, :],
                       pr_[:, :].rearrange("p (a t) -> p a t", a=4), ALU.add, [(Rr, hlf), pr_])
                act(P, [Rb], v3(Rb), v3(Rr), AF.Copy, [Rr])
                Ncur, Xcur = nk, xk
            for hlf in range(2):
                pu = C.psA[hlf]
                for u in range(4):
                    for hh in range(2):
                        cidx = u * 2 + hh
                        mm(P, [pu], pu[0:64, cidx * 128:(cidx + 1) * 128], Rb[:, 4 * hlf + u, hh * 64:(hh + 1) * 64],
                           bV[:, 4 * hlf + u, :], [Rb, bV])
                act(P, [(U, hlf)], U[:, 8 * hlf:8 * hlf + 8, :], pu[0:64, 0:1024].rearrange("p (a t) -> p a t", a=8),
                    AF.Copy, [pu])
            pw = C.psA[0]
            for u in range(8):
                mm(P, [pw], pw[:, u * 128:(u + 1) * 128], Kbe[:, u, :], Rb[:, u, :], [Kbe, Rb])
            cp(P, "dve", [WT], v3(WT), p3(pw), [pw])
            P.store(G.sWT, G.sWT[hd, t0:t0 + 8].rearrange("t p d -> p t d"), WT, v3(WT), key=(hd, g), eng="sp")
            P.store(G.sADT, G.sADT[hd, t0:t0 + 8].rearrange("t p d -> p t d"), ADT, v3(ADT), key=(hd, g), eng="sp")
            P.store(G.sKd, G.sKd[hd, t0:t0 + 8].rearrange("t p d -> p t d"), Kd, v3(Kd), key=(hd, g), eng="sp")
            P.store(G.sU, G.sU[hd, 2 * t0:2 * t0 + 16].rearrange("c p d -> p c d"), U, U[:, :, :], key=(hd, g), eng="sp")
    P.pop_scope()


def stage_gdn_rec(P, C, Dg, G):
    P.push_scope()
    BT = lambda nm: [P.sb("g2%s%d" % (nm, i), [128, 8, 128], BF16) for i in range(2)]
    WTt, ADTt, Kdt, qgt, gTt = BT("WT"), BT("ADT"), BT("Kd"), BT("qg"), BT("gT")
    Ut = [P.sb("g2U%d" % i, [64, 8, 2, 128], F32) for i in range(2)]
    vP = [P.sb("g2vP%d" % i, [128, 8, 128], BF16) for i in range(2)]
    for hh in range(2):
        mset(P, "pool", [vP[hh]], vP[hh][:, :, :], 0.0)
    Sf, Sb = P.sb("g2Sf", [128, 8, 128], F32), P.sb("g2Sb", [128, 8, 128], BF16)
    mset(P, "pool", [Sf], Sf[:, :, :], 0.0)
    mset(P, "pool", [Sb], Sb[:, :, :], 0.0)
    Otm = [P.sb("g2O%d" % i, [128, 8, 128], F32) for i in range(2)]
    sq = P.sb("g2sq", [128, 8, 128], F32)
    onb = P.sb("g2onb", [128, 8, 128], BF16)
    oTt = [P.sb("g2oT%d" % i, [128, 8, 128], BF16) for i in range(2)]
    ss8, rs8 = P.sb("g2ss", [128, 8], F32), P.sb("g2rs", [128, 8], F32)
    ng = P.sb("gng", [128, 1], F32)
    P.load(ng, ng[:, :], Dg["norm_g"], Dg["norm_g"][:, :])
    v3 = lambda b: b[:, :, :]
    p3 = lambda ps, n=128: ps[0:n, 0:1024].rearrange("p (a t) -> p a t", a=8)
    for t in range(NT):
        b = t % 2
        tsl = slice(t * 128, (t + 1) * 128)
        P.load(WTt[b], v3(WTt[b]), G.sWT, G.sWT[:, t].rearrange("h p d -> p h d"))
        P.load(ADTt[b], v3(ADTt[b]), G.sADT, G.sADT[:, t].rearrange("h p d -> p h d"))
        P.load(Kdt[b], v3(Kdt[b]), G.sKd, G.sKd[:, t].rearrange("h p d -> p h d"))
        P.load(qgt[b], v3(qgt[b]), G.qg_d, G.qg_d[:, :, tsl].rearrange("h p d -> p h d"))
        P.load(gTt[b], v3(gTt[b]), G.gT_d, G.gT_d[:, :, tsl].rearrange("h p d -> p h d"))
        for hh in range(2):
            P.load(Ut[b], Ut[b][:, :, hh, :], G.sU, G.sU[:, 2 * t + hh].rearrange("h p d -> p h d"))
        for hh in range(2):
            c = 2 * t + hh
            isl = slice(hh * 64, (hh + 1) * 64)
            p1, p2 = C.psA
            for h in range(8):
                mm(P, [p1], p1[0:64, h * 128:(h + 1) * 128], WTt[b][:, h, isl], Sb[:, h, :], [WTt[b], Sb])
            tt(P, "dve", [vP[hh]], vP[hh][isl, :, :], Ut[b][:, :, hh, :], p3(p1, 64), ALU.subtract, [Ut[b], p1])
            for h in range(8):
                mm(P, [p2], p2[0:64, h * 128:(h + 1) * 128], qgt[b][:, h, isl], Sb[:, h, :], [qgt[b], Sb], start=True, stop=False)
                mm(P, [p2], p2[0:64, h * 128:(h + 1) * 128], ADTt[b][:, h, isl], vP[hh][:, h, :], [ADTt[b], vP[hh]],
                   start=False, stop=True)
            act(P, [(Otm[b], hh)], Otm[b][isl, :, :], p3(p2, 64), AF.Copy, [p2])
            for h in range(8):
                mm(P, [p1], p1[:, h * 128:(h + 1) * 128], Kdt[b][:, h, :], vP[hh][:, h, :], [Kdt[b], vP[hh]])
            tt(P, "pool", [Sf], v3(Sf), v3(Sf), G.DL[:, :, c:c + 1].to_broadcast([128, 8, 128]), ALU.mult, [Sf, G.DL])
            tt(P, "dve", [Sf], v3(Sf), v3(Sf), p3(p1), ALU.add, [Sf, p1])
            act(P, [Sb], v3(Sb), v3(Sf), AF.Copy, [Sf])
        tt(P, "pool", [sq], v3(sq), v3(Otm[b]), v3(Otm[b]), ALU.mult, [Otm[b]])
        P.op("dve", lambda e, sq=sq: e.reduce_sum(ss8[:, :], sq[:, :, :], AX.X), [sq], [ss8])
        rstd_from_ss(P, C, ss8, ss8[:, :], rs8, rs8[:, :], 128)
        tt(P, "dve", [onb], v3(onb), v3(Otm[b]), rs8[:, :].unsqueeze(2).to_broadcast([128, 8, 128]), ALU.mult, [Otm[b], rs8])
        pt = C.psTs[b]
        for h in range(8):
            tr(P, [pt], pt[:, h * 128:(h + 1) * 128], onb[:, h, :], C.ident[:, :], [onb, C.ident])
        stt(P, "dve", [oTt[b]], v3(oTt[b]), pt[:, 0:1024].rearrange("p (a t) -> p a t", a=8), ng[:, 0:1], v3(gTt[b]),
            ALU.mult, ALU.mult, [pt, ng, gTt[b]])
        P.store(C.oT_d, C.oT_d[:, :, tsl].rearrange("h p d -> p h d"), oTt[b], v3(oTt[b]))
    P.pop_scope()


def stage_gdn(P, C, w_in_d, Dg):
    P.push_scope()
    G = Ctx()
    G.R0, G.R1 = P.sb("gR0", [8, S], F32), P.sb("gR1", [8, S], F32)
    G.cols = P.sb("gcols", [128, NT, 4, 8], F32)
    G.DL = P.sb("gDL", [128, 8, 64], F32)
    G.sel8 = P.sb("gsel8", [8, 8, 128], F32)
    P.load(G.sel8, G.sel8[:, :, :], Dg["sel8"], Dg["sel8"][:, :, :])
    G.qT_d, G.kT_d, G.vT_d, G.gT_d, G.qg_d = C.gdn_scr
    G.sWT, G.sADT, G.sKd = C.gdn_scr2
    G.sU = C.gdn_scrU
    stage_gdn_proj(P, C, w_in_d, Dg, G)
    if C.upto != "l1a":
        stage_gdn_rows(P, C, Dg, G)
        if C.upto != "l1b":
            stage_gdn_main(P, C, Dg, G)
            if C.upto != "l1c":
                stage_gdn_rec(P, C, Dg, G)
    P.pop_scope()


def t5_bucket_np(dist, buckets=32, max_dist=2048):
    dist = np.maximum(dist, 0)
    max_exact = buckets // 2
    large = max_exact + (np.log(np.maximum(dist, 1) / max_exact)
                         / math.log(max_dist / max_exact) * (buckets - max_exact)).astype(np.int32)
    large = np.minimum(large, buckets - 1)
    return np.where(dist < max_exact, dist, large).astype(np.int32)


def attn_tables(rel_bias):
    k = np.arange(128)[:, None]
    q = np.arange(128)[None, :]
    mask = np.zeros((128, 2, 128), np.float32)
    mask[:, 0, :] = (k >= q)
    mask[:, 1, :] = (q >= k)
    idx = np.zeros((3, 128, 2, 128), np.int64)
    for ci, (dil, nb) in enumerate(CFGS):
        idx[ci, :, 0, :] = t5_bucket_np(np.clip(q + 128 - k, 0, 128) * dil)
        idx[ci, :, 1, :] = t5_bucket_np(np.clip(q - k, 0, 128) * dil)
    G = rel_bias[idx]
    G = np.ascontiguousarray(np.transpose(G, (4, 1, 0, 2, 3))).reshape(8, 128, 768)
    return G.astype(np.float32), mask.reshape(128, 256)


def build(upto="all"):
    nc = bass.Bass("TRN2", target_bir_lowering=False)
    P = Prog(nc)
    C = Ctx()
    C.upto = upto
    dbg = upto != "all"
    C.x = P.dram("x", [S, D], F32, kind="ExternalInput")
    C.out = P.dram("out", [S, D], F32, kind="ExternalOutput")
    C.x1 = P.dram("x1", [S, D], F32, kind=("ExternalOutput" if dbg else "Internal"))
    setup(P, C)
    if dbg:
        C.oT_d = P.dram("dbg_oT", [8, 128, S], BF16, kind="ExternalOutput")
        C.dbg_h = P.dram("dbg_h", [8, 128, S], BF16, kind="ExternalOutput")
    C.w_in0 = P.dram("w_in0", [1024, 3072], F32, kind="ExternalInput")
    C.w_out0 = P.dram("w_out0", [1024, 1024], F32, kind="ExternalInput")
    C.w_in1 = P.dram("w_in1", [1024, 4112], F32, kind="ExternalInput")
    C.w_out1 = P.dram("w_out1", [1024, 1024], F32, kind="ExternalInput")
    C.Gd = P.dram("attn_G", [8, 128, 768], F32, kind="ExternalInput")
    C.maskd = P.dram("attn_mask", [128, 256], F32, kind="ExternalInput")
    C.uT_d = P.dram("uT_d", [4, 128, S], F32)
    C.gB_d = P.dram("gB_d", [4, 128, S], BF16)
    C.gdn_scr = [P.dram("gdn_scr%d" % i, [8, 128, S], BF16) for i in range(5)]
    C.gdn_scr2 = [P.dram("gdn_scrB%d" % i, [8, NT, 128, 128], BF16) for i in range(3)]
    C.gdn_scrU = P.dram("gdn_scrU", [8, 2 * NT, 64, 128], F32)
    Dm = {}
    for nm, shp in (("ar", [128, 16]), ("ai", [128, 16]), ("ldt", [128, 16]), ("b_re", [128, 16, 16]), ("b_im", [128, 16, 16]),
                    ("cT_re", [128, 16, 16]), ("cT_im", [128, 16, 16]), ("d_fm", [128, 4]), ("glu_b_fm", [128, 4]),
                    ("glu_w", [512, 512])):
        Dm[nm] = P.dram("s5_" + nm, shp, F32, kind="ExternalInput")
    Dg = {}
    for nm, shp in (("conv_fm", [128, 24, 4]), ("a_log", [8, 1]), ("dt_bias", [8, 1]), ("norm_g", [128, 1]),
                    ("maskadd", [128, 128]), ("strict01", [128, 128]), ("sel8", [8, 8, 128])):
        Dg[nm] = P.dram("gdn_" + nm, shp, F32, kind="ExternalInput")

    def layer0():
        stage_adaln(P, C, 0)
        if upto == "ada":
            C.dbg_m = P.dram("dbg_m", [128, 24 + 8 + 1024], F32, kind="ExternalOutput")
            P.store(C.dbg_m, C.dbg_m[:, 0:24], C.modfm, C.modfm[:, :], eng="sp")
            P.store(C.dbg_m, C.dbg_m[:, 24:32], C.Asc, C.Asc[:, :], eng="sp")
            P.store(C.dbg_m, C.dbg_m[:, 32:1056], C.GP, C.GP[:, :], eng="sp")
            return
        if upto == "all":
            stage_adaln(P, C, 1, defer_pop=True)
        stage_prenorm(P, C, C.x, 0)
        if upto == "all":
            P.pop_scope()
        if dbg:
            for k in range(8):
                P.store(C.dbg_h, C.dbg_h[k], C.big, C.big[:, k, :], eng="sp")
        if upto == "pre":
            return
        if upto != "s5":
            stage_attn(P, C, C.w_in0, C.Gd, C.maskd)
        if upto != "attn":
            stage_s5_proj(P, C, C.w_in0)
            stage_s5(P, C, Dm)
        if upto in ("attn", "s5"):
            return
        stage_out(P, C, C.w_out0, C.x, C.x1, 0)

    def layer1(xin, xout):
        if upto != "all":
            stage_adaln(P, C, 1)
        stage_prenorm(P, C, xin, 1)
        if dbg:
            for k in range(8):
                P.store(C.dbg_h, C.dbg_h[k], C.big, C.big[:, k, :], eng="sp")
        stage_gdn(P, C, C.w_in1, Dg)
        if upto in ("l1a", "l1b", "l1c"):
            return
        stage_out(P, C, C.w_out1, xin, xout, 1)

    if upto in ("l1", "l1a", "l1b", "l1c"):
        layer1(C.x, C.x1)
    else:
        layer0()
        if upto == "all":
            layer1(C.x1, C.out)
    P.emit()
    return nc, P


def host_inputs(inputs, b):
    f32 = np.float32
    G, mask = attn_tables(np.asarray(inputs["rel_bias"], f32))
    m = {
        "x": np.ascontiguousarray(inputs["x"][b], f32),
        "c_fm": np.ascontiguousarray(np.asarray(inputs["c"][b], f32).reshape(8, 128).T),
        "ada_w": np.ascontiguousarray(inputs["ada_w"], f32),
        "ada_b_fm": np.ascontiguousarray(np.transpose(np.asarray(inputs["ada_b"], f32).reshape(2, 24, 128), (0, 2, 1))),
        "ada_b_g": np.ascontiguousarray(np.asarray(inputs["ada_b"], f32)[:, 2048:3072].reshape(2, 1, 1024)),
        "pre_g_fm": np.ascontiguousarray(np.transpose(np.asarray(inputs["pre_g"], f32).reshape(2, 8, 128), (0, 2, 1))),
        "post_g_row": np.ascontiguousarray(np.asarray(inputs["post_g"], f32).reshape(2, 1, 1024)),
        "ident_in": np.eye(128).astype(ml_dtypes.bfloat16),
        "identf_in": np.eye(128).astype(f32),
        "w_in0": np.ascontiguousarray(inputs["ab_w_in"][0], f32),
        "w_out0": np.ascontiguousarray(inputs["ab_w_out"][0], f32),
        "attn_G": G, "attn_mask": mask,
    }

    def pair_layout(a):
        a = np.asarray(a, f32)
        a = a.reshape((16, 2, 64) + a.shape[2:])
        return np.ascontiguousarray(np.moveaxis(a, 0, 2).reshape((128, 16) + a.shape[3:]))
    m["s5_ar"] = pair_layout(inputs["s5_a_re"][0])
    m["s5_ai"] = pair_layout(inputs["s5_a_im"][0])
    m["s5_ldt"] = pair_layout(np.broadcast_to(np.asarray(inputs["s5_log_dt"][0], f32)[:, None], (32, 64)))
    m["s5_b_re"] = pair_layout(inputs["s5_b_re"][0])
    m["s5_b_im"] = pair_layout(inputs["s5_b_im"][0])
    m["s5_cT_re"] = pair_layout(np.transpose(np.asarray(inputs["s5_c_re"][0], f32), (0, 2, 1)))
    m["s5_cT_im"] = pair_layout(np.transpose(np.asarray(inputs["s5_c_im"][0], f32), (0, 2, 1)))
    m["s5_d_fm"] = np.ascontiguousarray(np.asarray(inputs["s5_d"][0], f32).reshape(4, 128).T)
    m["s5_glu_b_fm"] = np.ascontiguousarray(np.asarray(inputs["s5_glu_b"][0], f32).reshape(4, 128).T)
    m["s5_glu_w"] = np.ascontiguousarray(inputs["s5_glu_w"][0], f32)
    m["w_in1"] = np.ascontiguousarray(inputs["gdn_w_in"][0], f32)
    m["w_out1"] = np.ascontiguousarray(inputs["gdn_w_out"][0], f32)
    m["gdn_conv_fm"] = np.ascontiguousarray(np.transpose(np.asarray(inputs["gdn_conv"][0], f32).reshape(4, 24, 128), (2, 1, 0)))
    m["gdn_a_log"] = np.ascontiguousarray(np.asarray(inputs["gdn_a_log"][0], f32).reshape(8, 1))
    m["gdn_dt_bias"] = np.ascontiguousarray(np.asarray(inputs["gdn_dt_bias"][0], f32).reshape(8, 1))
    m["gdn_norm_g"] = np.ascontiguousarray(np.asarray(inputs["gdn_norm_g"][0], f32).reshape(128, 1))
    ii = np.arange(128)[:, None]
    jj = np.arange(128)[None, :]
    same = (ii // 64) == (jj // 64)
    m["gdn_maskadd"] = np.where(same & (ii >= jj), 0.0, -30000.0).astype(f32)
    m["gdn_strict01"] = (same & (ii > jj)).astype(f32)
    sel = np.zeros((8, 8, 128), f32)
    for h in range(8):
        sel[h, h, :] = 1.0
    m["gdn_sel8"] = sel
    return m


_PROG = {}


def kernel(**inputs):
    if "nc" not in _PROG:
        _PROG["nc"] = build("all")[0]
    nc = _PROG["nc"]
    nb = int(np.asarray(inputs["x"]).shape[0])
    maps = [host_inputs(inputs, b) for b in range(nb)]
    in_maps = [maps[i % nb] for i in range(8)]
    res = run_bass_kernel_spmd(nc, in_maps, core_ids=list(range(8)))
    out = np.stack([np.asarray(res.results[b]["out"], dtype=np.float32) for b in range(nb)], axis=0)
    return out
```

```python
import numpy as np
import concourse.bass as bass
import concourse.mybir as mybir
from concourse.bass_utils import run_bass_kernel_spmd

F32 = mybir.dt.float32
BF16 = mybir.dt.bfloat16
AF = mybir.ActivationFunctionType
ALU = mybir.AluOpType
AX = mybir.AxisListType

ENGS = ("pe", "act", "dve", "pool", "sp")
EPOCH = 16000


class Buf:
    _n = 0

    def __init__(self, name, t, kind):
        self.name = name
        self.t = t
        self.kind = kind
        Buf._n += 1
        self.id = Buf._n
        self.wr = {}
        self.pslast = {}
        self.rd = {}
        self.slot = None

    def __getitem__(self, idx):
        return self.t[idx]


class SemSlot:
    def __init__(self):
        self.handle = None
        self.count = 0
        self.last = None


class Op:
    __slots__ = ("eng", "fn", "reads", "writes", "deps", "is_dma", "dbuf", "sig", "sigval", "idx", "dmaval", "busy", "lat", "soft", "seg")

    def __init__(self, eng, fn, reads, writes, is_dma=False, dbuf=None):
        self.eng = eng
        self.fn = fn
        self.reads = reads
        self.writes = writes
        self.deps = set()
        self.is_dma = is_dma
        self.dbuf = dbuf
        self.sig = False
        self.sigval = None
        self.dmaval = None
        self.busy = 0.3
        self.lat = 0.3
        self.soft = set()


class PsView:
    def __init__(self, base, ap):
        self.base = base
        self.ap = ap

    def __getitem__(self, idx):
        return self.ap[idx]


def _acc(x):
    if isinstance(x, PsView):
        return (x.base, None)
    if isinstance(x, Buf):
        return (x, None)
    if isinstance(x[0], PsView):
        return (x[0].base, x[1])
    return x


class Prog:
    def __init__(self, nc):
        self.nc = nc
        self.ops = []
        self.bufs = []
        self.final_waits = []
        self.slots = []
        self.free_slots = []
        self.fence_deps = set()
        self.last_of_eng = {}
        self.scopes = []
        self.seg = 0

    def sb(self, name, shape, dtype=F32):
        if self.scopes:
            g = self.nc.sbuf_tensor(name + "_s%d" % len(self.bufs), list(shape), dtype)
            t = g.__enter__()
            b = Buf(name, t, "sb")
            self.scopes[-1].append((g, b))
        else:
            b = Buf(name, self.nc.alloc_sbuf_tensor(name, list(shape), dtype), "sb")
        self.bufs.append(b)
        return b

    def push_scope(self):
        self.scopes.append([])

    def pop_scope(self):
        self.fence()
        sc = self.scopes.pop()
        for (g, b) in reversed(sc):
            g.__exit__(None, None, None)
            if b.slot is not None:
                self.free_slots.append(b.slot)
                b.slot = None

    def fence(self):
        deps = set(self.last_of_eng.values())
        for sl in self.slots:
            if sl.last is not None:
                deps.add(sl.last)
        self.fence_deps = deps
        self.seg += 1

    def ps(self, name, shape, dtype=F32):
        b = Buf(name, self.nc.alloc_psum_tensor(name, list(shape), dtype), "ps")
        self.bufs.append(b)
        return b

    def dram(self, name, shape, dtype=F32, kind="Internal"):
        t = self.nc.dram_tensor(name, list(shape), dtype, kind=kind)
        b = Buf(name, t.ap(), "dr")
        self.bufs.append(b)
        return b

    def _overlap_w(self, b, k):
        if k is None:
            return list(b.wr.values())
        r = []
        if k in b.wr:
            r.append(b.wr[k])
        if None in b.wr:
            r.append(b.wr[None])
        return r

    def _overlap_r(self, b, k):
        if k is None:
            r = []
            for v in b.rd.values():
                r.extend(v)
            return r
        return list(b.rd.get(k, [])) + list(b.rd.get(None, []))

    def op(self, eng, fn, reads=(), writes=(), is_dma=False, dbuf=None, cost=None):
        o = Op(eng, fn, [_acc(x) for x in reads], [_acc(x) for x in writes], is_dma, dbuf)
        if cost is not None:
            o.busy, o.lat = cost
        o.idx = len(self.ops)
        o.seg = self.seg
        o.deps.update(self.fence_deps)
        self.last_of_eng[eng] = o.idx
        if is_dma:
            if dbuf.slot is None:
                if self.free_slots:
                    dbuf.slot = self.free_slots.pop()
                else:
                    dbuf.slot = SemSlot()
                    self.slots.append(dbuf.slot)
            sl = dbuf.slot
            sl.count += 16
            sl.last = o.idx
            o.dmaval = (sl, sl.count)
            sbacc = (dbuf, None)
            o.writes = [w for w in o.writes if w[0] is not dbuf] + [sbacc]
            o.reads = [r for r in o.reads if r[0] is not dbuf]
        psb = set(b for (b, k) in o.reads + o.writes if b.kind == "ps")
        o.reads = [(b, k) for (b, k) in o.reads if b.kind != "ps"]
        o.writes = [(b, k) for (b, k) in o.writes if b.kind != "ps"]
        for b in psb:
            for en, ix in b.pslast.items():
                if en != eng:
                    o.deps.add(ix)
                else:
                    o.soft.add(ix)
            b.pslast[eng] = o.idx
        for (b, k) in o.reads:
            o.deps.update(self._overlap_w(b, k))
        for (b, k) in o.writes:
            o.deps.update(self._overlap_w(b, k))
            o.deps.update(self._overlap_r(b, k))
        for (b, k) in o.reads:
            b.rd.setdefault(k, []).append(o.idx)
        for (b, k) in o.writes:
            if k is None:
                b.wr = {None: o.idx}
                b.rd = {}
            else:
                b.wr[k] = o.idx
                b.rd[k] = []
        o.deps.discard(o.idx)
        self.ops.append(o)
        return o

    def dma(self, out_ap, in_ap, sbuf, reads=(), writes=(), eng="sp", **kw):
        def fn(e, out_ap=out_ap, in_ap=in_ap, kw=kw):
            return e.dma_start(out=out_ap, in_=in_ap, **kw)
        n = 1
        for d_ in out_ap.shape:
            n *= int(d_)
        nbytes = n * (2 if out_ap.dtype == BF16 else 4)
        return self.op(eng, fn, reads, writes, is_dma=True, dbuf=sbuf, cost=(0.15, 2.0 + nbytes / 150e3))

    def schedule(self, window=48):
        ops = self.ops
        n = len(ops)
        segs = {}
        for o in ops:
            segs.setdefault(o.seg, []).append(o.idx)
        done = [False] * n
        fin = [0.0] * n
        free = {e: 0.0 for e in ENGS}
        order = []
        last_sched = {}
        for sg in sorted(segs):
            idxs = segs[sg]
            extra = set(last_sched.values())
            per = {e: [] for e in ENGS}
            for ix in idxs:
                ops[ix].deps.update(extra)
                per[ops[ix].eng].append(ix)
            alldeps = {ix: list(ops[ix].deps | ops[ix].soft) for ix in idxs}
            ptr = {e: 0 for e in ENGS}
            remaining = len(idxs)
            while remaining:
                best = None
                for e in ENGS:
                    lst = per[e]
                    p = ptr[e]
                    while p < len(lst) and done[lst[p]]:
                        p += 1
                    ptr[e] = p
                    cnt = 0
                    q = p
                    wnd = window * 8 if e == "pe" else window
                    while q < len(lst) and cnt < wnd:
                        ix = lst[q]
                        q += 1
                        if done[ix]:
                            continue
                        cnt += 1
                        ok = True
                        rdy = 0.0
                        for d in alldeps[ix]:
                            if not done[d]:
                                ok = False
                                break
                            t = fin[d] + (0.3 if ops[d].eng != e else 0.05)
                            if t > rdy:
                                rdy = t
                        if not ok:
                            continue
                        st = rdy if rdy > free[e] else free[e]
                        key = (st, ix)
                        if best is None or key < best[0]:
                            best = (key, e, ix)
                        if st <= free[e]:
                            break
                assert best is not None, "scheduler stuck"
                (st, _), e, ix = best
                done[ix] = True
                fin[ix] = st + ops[ix].lat
                free[e] = st + ops[ix].busy
                order.append(ix)
                last_sched[e] = ix
                remaining -= 1
        self.sim_time = max(fin) if fin else 0.0
        return order

    def load(self, sbuf, sb_ap, dr_buf, dr_ap, eng="sp", key=None, **kw):
        return self.dma(sb_ap, dr_ap, sbuf, reads=[(dr_buf, key)], writes=[sbuf], eng=eng, **kw)

    def store(self, dr_buf, dr_ap, sbuf, sb_ap, eng="act", key=None, **kw):
        return self.dma(dr_ap, sb_ap, sbuf, reads=[sbuf], writes=[(dr_buf, key)], eng=eng, **kw)

    def emit(self):
        nc = self.nc
        ops = self.ops
        order = self.schedule() if getattr(self, "reorder", True) else list(range(len(ops)))
        for o in ops:
            for d in o.deps:
                ops[d].sig = True
        cnt = {e: 0 for e in ENGS}
        sems = {e: [] for e in ENGS}
        oops = [ops[i] for i in order]
        for o in oops:
            if o.is_dma:
                sl = o.dmaval[0]
                if sl.handle is None:
                    sl.handle = nc.alloc_semaphore("dq_%d" % self.slots.index(sl))
            elif o.sig:
                c = cnt[o.eng]
                ep = c // EPOCH
                while len(sems[o.eng]) <= ep:
                    sems[o.eng].append(nc.alloc_semaphore("s_%s_%d" % (o.eng, len(sems[o.eng]))))
                o.sigval = (sems[o.eng][ep], c % EPOCH + 1, ep)
                cnt[o.eng] = c + 1
        engobj = {"pe": nc.tensor, "act": nc.scalar, "dve": nc.vector, "pool": nc.gpsimd, "sp": nc.sync}
        self.nwaits = 0

        def run_engine(ename, e):
            waited = {}
            for o in oops:
                if o.eng != ename:
                    continue
                need = {}
                for d in o.deps:
                    p = ops[d]
                    if p.is_dma:
                        s, v = p.dmaval[0].handle, p.dmaval[1]
                        key = ("d", id(p.dmaval[0]))
                    else:
                        s, v, ep = p.sigval
                        key = (p.eng, ep)
                    if key not in need or need[key][1] < v:
                        need[key] = (s, v)
                for key, (s, v) in need.items():
                    if key[0] != "d":
                        newer = [k2 for k2 in need if k2[0] == key[0] and k2[1] > key[1]]
                        if newer:
                            continue
                        w_ep = waited.get(("ep", key[0]), -1)
                        if w_ep > key[1]:
                            continue
                    if waited.get(key, 0) >= v:
                        continue
                    e.wait_ge(s, v)
                    self.nwaits += 1
                    waited[key] = v
                    if key[0] != "d":
                        waited[("ep", key[0])] = max(waited.get(("ep", key[0]), -1), key[1])
                ins = o.fn(e)
                if o.is_dma:
                    ins.then_inc(o.dmaval[0].handle, 16)
                elif o.sig:
                    ins.then_inc(o.sigval[0], 1)
            for (en, s, v) in self.final_waits:
                if en == ename:
                    e.wait_ge(s, v)

        for sl in self.slots:
            if sl.handle is not None:
                self.final_waits.append(("sp", sl.handle, sl.count))
        with nc.Block() as block:
            @block.tensor
            def _(e):
                run_engine("pe", e)

            @block.scalar
            def _(e):
                run_engine("act", e)

            @block.vector
            def _(e):
                run_engine("dve", e)

            @block.gpsimd
            def _(e):
                run_engine("pool", e)

            @block.sync
            def _(e):
                run_engine("sp", e)
        return nc


import math
import ml_dtypes

S = 4096
D = 1024
NT = S // 128
EPS = 1e-6
CFGS = ((1, 32), (4, 8), (16, 2))


def _fd(ap):
    n = 1
    for d_ in ap.shape[1:]:
        n *= int(d_)
    return n


def _ecost(eng, ap):
    fd = _fd(ap)
    if eng == "act":
        c = 0.2 + fd / 1200.0
    elif eng == "dve":
        c = 0.08 + fd / 960.0
    else:
        c = 0.12 + fd / 450.0
    return (c, c)


def mm(P, wr, out, lhsT, rhs, rd, start=True, stop=True):
    c = max(0.03, _fd(rhs) / 2400.0 * (4.0 if lhsT.dtype == F32 else 1.0)) + 0.01
    return P.op("pe", lambda e: e.matmul(out, lhsT, rhs, start=start, stop=stop), rd, wr, cost=(c, c + 0.1))


def tr(P, wr, out, in_, ident, rd):
    c = 0.28 if in_.dtype == F32 else 0.07
    return P.op("pe", lambda e: e.transpose(out, in_, ident), rd, wr, cost=(c, c + 0.1))


def act(P, wr, out, in_, func, rd, **kw):
    return P.op("act", lambda e: e.activation(out, in_, func, **kw), rd, wr, cost=_ecost("act", out))


def tt(P, eng, wr, out, in0, in1, op, rd):
    return P.op(eng, lambda e: e.tensor_tensor(out, in0, in1, op), rd, wr, cost=_ecost(eng, out))


def ts(P, eng, wr, out, in0, s1, s2, op0, rd, op1=None):
    if op1 is None:
        return P.op(eng, lambda e: e.tensor_scalar(out, in0, s1, None, op0), rd, wr, cost=_ecost(eng, out))
    return P.op(eng, lambda e: e.tensor_scalar(out, in0, s1, s2, op0, op1), rd, wr, cost=_ecost(eng, out))


def stt(P, eng, wr, out, in0, scalar, in1, op0, op1, rd):
    return P.op(eng, lambda e: e.scalar_tensor_tensor(out, in0, scalar, in1, op0, op1), rd, wr, cost=_ecost(eng, out))


def cp(P, eng, wr, out, in_, rd):
    return P.op(eng, lambda e: e.tensor_copy(out, in_), rd, wr, cost=_ecost(eng, out))


def mset(P, eng, wr, ap, val):
    return P.op(eng, lambda e: e.memset(ap, val), [], wr, cost=_ecost(eng, ap))


class Ctx:
    pass


def rstd_from_ss(P, C, ss_buf, ss_ap, out_buf, out_ap, n):
    act(P, [out_buf], out_ap, ss_ap, AF.Ln, [ss_buf, C.epsb], bias=C.epsb[0:ss_ap.shape[0], 0:1], scale=1.0 / n)
    act(P, [out_buf], out_ap, out_ap, AF.Exp, [out_buf], scale=-0.5)


def setup(P, C):
    C.ident = P.sb("ident", [128, 128], BF16)
    idd = P.dram("ident_in", [128, 128], BF16, kind="ExternalInput")
    P.load(C.ident, C.ident[:, :], idd, idd[:, :])
    C.identf = P.sb("identf", [128, 128], F32)
    iddf = P.dram("identf_in", [128, 128], F32, kind="ExternalInput")
    P.load(C.identf, C.identf[:, :], iddf, iddf[:, :])
    C.ones_f = P.sb("ones_f", [128, 128], F32)
    mset(P, "pool", [C.ones_f], C.ones_f[:, :], 1.0)
    C.ones_b = P.sb("ones_b", [128, 128], BF16)
    mset(P, "pool", [C.ones_b], C.ones_b[:, :], 1.0)
    C.epsb = P.sb("epsb", [128, 1], F32)
    mset(P, "pool", [C.epsb], C.epsb[:, :], EPS)
    C.psA = [P.ps("psA%d" % i, [128, 1024], F32) for i in range(2)]
    C.psB = [P.ps("psB%d" % i, [128, 512], F32) for i in range(2)]
    C.psTf = [P.ps("psT%d" % i, [128, 512], F32) for i in range(2)]
    C.psTs = [PsView(b, b.t[:, :].bitcast(BF16)) for b in C.psTf]
    C.psT = C.psTs[0]
    C.big = P.sb("big", [128, 8, S], BF16)
    C.oT_d = P.dram("oT_d", [8, 128, S], BF16)
    C.cfm = P.sb("cfm", [128, 8], F32)
    cd = P.dram("c_fm", [128, 8], F32, kind="ExternalInput")
    P.load(C.cfm, C.cfm[:, :], cd, cd[:, :])
    C.cact = P.sb("cact", [128, 8], F32)
    act(P, [C.cact], C.cact[:, :], C.cfm[:, :], AF.Silu, [C.cfm])
    C.ada_w = P.dram("ada_w", [2, 1024, 3072], F32, kind="ExternalInput")
    C.ada_b_fm = P.dram("ada_b_fm", [2, 128, 24], F32, kind="ExternalInput")
    C.ada_b_g = P.dram("ada_b_g", [2, 1, 1024], F32, kind="ExternalInput")
    C.pre_g_fm = P.dram("pre_g_fm", [2, 128, 8], F32, kind="ExternalInput")
    C.post_g_row = P.dram("post_g_row", [2, 1, 1024], F32, kind="ExternalInput")
    C.wstage = [P.sb("wstage%d" % i, [128, 8, 128], F32) for i in range(2)]
    C.wsi = 0
    C.ssq = P.sb("ssq", [128, 2 * NT], F32)
    C.rstd = P.sb("rstd", [128, 2 * NT], F32)
    C.modfm_l = [P.sb("modfm%d" % i, [128, 24], F32) for i in range(2)]
    C.Asc_l = [P.sb("Asc%d" % i, [128, 8], F32) for i in range(2)]
    C.GP_l = [P.sb("GP%d" % i, [128, 1024], F32) for i in range(2)]
    C.grow_l = [P.sb("grow%d" % i, [1, 1024], F32) for i in range(2)]
    C.modfm, C.Asc, C.GP, C.grow = C.modfm_l[0], C.Asc_l[0], C.GP_l[0], C.grow_l[0]
    C.tmpv = P.sb("tmpv", [128, 24], F32)
    C.trow = P.sb("trow", [1, 1024], F32)


def load_w_bf16(P, C, dst, dst_ap, wd, w_ap, ncols):
    st = C.wstage[C.wsi % 2]
    C.wsi += 1
    P.load(st, st[:, :, 0:ncols], wd, w_ap.rearrange("(k p) c -> p k c", p=128))
    eng = "pool" if (C.wsi % 2) else "dve"
    cp(P, eng, [dst], dst_ap, st[:, :, 0:ncols], [st])


def stage_adaln(P, C, layer, defer_pop=False):
    C.modfm, C.Asc, C.GP, C.grow = C.modfm_l[layer], C.Asc_l[layer], C.GP_l[layer], C.grow_l[layer]
    P.push_scope()
    awst = [P.sb("awst%d" % i, [128, 3072], F32) for i in range(2)]
    psF = C.psB[0]
    psG = C.psA[0]
    P.load(C.modfm, C.modfm[:, :], C.ada_b_fm, C.ada_b_fm[layer])
    P.load(C.grow, C.grow[:, :], C.ada_b_g, C.ada_b_g[layer])
    for k in range(8):
        st = awst[k % 2]
        P.load(st, st[:, :], C.ada_w, C.ada_w[layer, k * 128:(k + 1) * 128, :])
        for j in range(24):
            mm(P, [(psF, j)], psF[:, j:j + 1], st[:, j * 128:(j + 1) * 128], C.cact[:, k:k + 1], [st, C.cact])
        tt(P, "dve", [C.modfm], C.modfm[:, :], psF[:, 0:24], C.modfm[:, :], ALU.add, [psF, C.modfm])
        for cb in range(2):
            mm(P, [(psG, cb)], psG[0:1, cb * 512:(cb + 1) * 512], C.cact[:, k:k + 1],
               st[:, 2048 + cb * 512: 2048 + (cb + 1) * 512], [st, C.cact])
        tt(P, "dve", [C.grow], C.grow[:, :], psG[0:1, 0:1024], C.grow[:, :], ALU.add, [psG, C.grow])
    P.load(C.tmpv, C.tmpv[:, 0:8], C.pre_g_fm, C.pre_g_fm[layer])
    stt(P, "dve", [C.Asc], C.Asc[:, :], C.modfm[:, 8:16], 1.0, C.tmpv[:, 0:8], ALU.add, ALU.mult, [C.modfm, C.tmpv])
    P.load(C.trow, C.trow[:, :], C.post_g_row, C.post_g_row[layer])
    tt(P, "dve", [C.grow], C.grow[:, :], C.grow[:, :], C.trow[:, :], ALU.mult, [C.grow, C.trow])
    for cb in range(2):
        mm(P, [(psG, cb)], psG[:, cb * 512:(cb + 1) * 512], C.ones_f[0:1, :], C.grow[0:1, cb * 512:(cb + 1) * 512],
           [C.ones_f, C.grow])
    cp(P, "dve", [C.GP], C.GP[:, :], psG[:, :], [psG])
    if not defer_pop:
        P.pop_scope()


def stage_prenorm(P, C, xsrc, layer=0):
    C.modfm, C.Asc, C.GP = C.modfm_l[layer], C.Asc_l[layer], C.GP_l[layer]
    mset(P, "pool", [C.ssq], C.ssq[:, :], 0.0)
    P.push_scope()
    C.xt = [P.sb("xt%d" % i, [128, 1024], F32) for i in range(2)]
    C.xn = [P.sb("xn%d" % i, [128, 1024], BF16) for i in range(2)]
    C.junk = P.sb("junk", [128, 1024], BF16)
    for i in range(NT):
        xt = C.xt[i % 2]
        xn = C.xn[i % 2]
        P.load(xt, xt[:, :], xsrc, xsrc[i * 128:(i + 1) * 128, :])
        act(P, [C.junk, (C.ssq, i)], C.junk[:, :], xt[:, :], AF.Square, [xt], accum_out=C.ssq[:, i:i + 1])
        rstd_from_ss(P, C, (C.ssq, i), C.ssq[:, i:i + 1], (C.rstd, i), C.rstd[:, i:i + 1], D)
        ts(P, "dve", [xn], xn[:, :], xt[:, :], C.rstd[:, i:i + 1], None, ALU.mult, [xt, (C.rstd, i)])
        pt = C.psTs[i % 2]
        for j in range(8):
            tr(P, [(pt, j)], pt[:, j * 128:(j + 1) * 128], xn[:, j * 128:(j + 1) * 128], C.ident[:, :], [xn, C.ident])
        for j in range(8):
            ts(P, "dve", [(C.big, (j, i))], C.big[:, j, i * 128:(i + 1) * 128], pt[:, j * 128:(j + 1) * 128],
               C.Asc[:, j:j + 1], C.modfm[:, j:j + 1], ALU.mult, [(pt, j), C.Asc, C.modfm], op1=ALU.add)
    P.pop_scope()


def stage_out(P, C, w_out_d, xsrc, xdst, layer):
    C.GP = C.GP_l[layer]
    P.push_scope()
    wo = P.sb("wout", [128, 8, 1024], BF16)
    C.xt = [P.sb("xt%d" % i, [128, 1024], F32) for i in range(2)]
    C.yt = [P.sb("yt%d" % i, [128, 1024], F32) for i in range(2)]
    C.junk = P.sb("junk", [128, 1024], BF16)
    for k in range(8):
        for cb in range(8):
            st = C.wstage[C.wsi % 2]
            C.wsi += 1
            P.load(st, st[:, 0, :], w_out_d, w_out_d[k * 128:(k + 1) * 128, cb * 128:(cb + 1) * 128])
            cp(P, "pool" if cb % 2 else "dve", [(wo, (k, cb))], wo[:, k, cb * 128:(cb + 1) * 128], st[:, 0, :], [st])
    for k in range(8):
        P.load(C.big, C.big[:, k, :], C.oT_d, C.oT_d[k])
    for i in range(NT):
        py = C.psA[i % 2]
        for cb in range(2):
            for k in range(8):
                mm(P, [(py, cb)], py[:, cb * 512:(cb + 1) * 512], C.big[:, k, i * 128:(i + 1) * 128],
                   wo[:, k, cb * 512:(cb + 1) * 512], [C.big, wo], start=(k == 0), stop=(k == 7))
        col = NT + i
        act(P, [C.junk, (C.ssq, col)], C.junk[:, :], py[:, :], AF.Square, [py], accum_out=C.ssq[:, col:col + 1])
        rstd_from_ss(P, C, (C.ssq, col), C.ssq[:, col:col + 1], (C.rstd, col), C.rstd[:, col:col + 1], D)
        xt = C.xt[i % 2]
        P.load(xt, xt[:, :], xsrc, xsrc[i * 128:(i + 1) * 128, :])
        t = C.yt[i % 2]
        stt(P, "dve", [t], t[:, :], py[:, :], C.rstd[:, col:col + 1], C.GP[:, :], ALU.mult, ALU.mult,
            [py, (C.rstd, col), C.GP])
        tt(P, "pool", [t], t[:, :], t[:, :], xt[:, :], ALU.add, [t, xt])
        P.store(xdst, xdst[i * 128:(i + 1) * 128, :], t, t[:, :])
    P.pop_scope()


def blk_slice(dil, r, n, cnt=1):
    start = n * 128 * dil + r
    return slice(start, start + (128 * cnt - 1) * dil + 1, dil)


def proj_fm(P, C, w, ncols, ps_list, evac):
    hT = C.big
    for tb in range(8):
        tsl = slice(tb * 512, (tb + 1) * 512)
        ps = ps_list[tb % len(ps_list)]
        for k in range(8):
            mm(P, [ps], ps[0:ncols, :], w[:, k, 0:ncols], hT[:, k, tsl], [w, hT], start=(k == 0), stop=(k == 7))
        evac(ps, tb, tsl)


def stage_attn(P, C, w_in_d, Gd, maskd):
    P.push_scope()
    A = Ctx()
    A.w4 = [P.sb("aw%d" % i, [128, 8, 128], BF16) for i in range(4)]
    A.qT = P.sb("qT", [128, S], BF16)
    A.kT = P.sb("kT", [128, S], BF16)
    A.vT = P.sb("vT", [128, S], BF16)
    A.gT = P.sb("gT", [128, S], BF16)
    A.Vd = [P.sb("Vd%d" % i, [128, 32, 128], BF16) for i in range(2)]
    mset(P, "pool", [A.Vd[0]], A.Vd[0][:, :, 64:128], 1.0)
    mset(P, "pool", [A.Vd[1]], A.Vd[1][:, :, 0:64], 1.0)
    A.Gs = P.sb("Gs", [128, 768], F32)
    A.E = [P.sb("E%d" % i, [128, 3, 256], BF16) for i in range(2)]
    A.mask = P.sb("amask", [128, 256], F32)
    A.pex = [P.sb("pex%d" % i, [128, 1024], BF16) for i in range(2)]
    A.PT = [P.sb("PT%d" % i, [128, 1024], BF16) for i in range(2)]
    A.anum = P.sb("anum", [128, S], F32)
    A.aden = P.sb("aden", [128, S], F32)
    A.rden = [P.sb("rden%d" % i, [128, 512], F32) for i in range(2)]
    A.sqb = [P.sb("asqb%d" % i, [128, 512], BF16) for i in range(2)]
    A.blk1 = P.sb("ablk1", [128, 2], BF16)
    mset(P, "pool", [A.blk1], A.blk1[:, :], 0.0)
    mset(P, "pool", [A.blk1], A.blk1[0:64, 0:1], 1.0)
    mset(P, "pool", [A.blk1], A.blk1[64:128, 1:2], 1.0)
    A.mx = P.sb("amx", [2, 16], F32)
    A.m2 = P.sb("am2", [2, 4], F32)
    A.dg2 = P.sb("adg2", [2, 2], F32)
    A.nb = P.sb("anb", [128, 2], F32)
    A.oTc = [P.sb("oTc%d" % i, [128, 512], BF16) for i in range(2)]
    P.load(A.mask, A.mask[:, :], maskd, maskd[:, :])
    for pr in range(4):
        wq, wk, wv, wg = A.w4
        load_w_bf16(P, C, wq, wq[:, :, :], w_in_d, w_in_d[:, pr * 128:(pr + 1) * 128], 128)
        load_w_bf16(P, C, wk, wk[:, :, :], w_in_d, w_in_d[:, 512 + pr * 128: 512 + (pr + 1) * 128], 128)
        load_w_bf16(P, C, wv, wv[:, :, :], w_in_d, w_in_d[:, 1024 + pr * 128: 1024 + (pr + 1) * 128], 128)
        load_w_bf16(P, C, wg, wg[:, :, :], w_in_d, w_in_d[:, 2048 + pr * 128: 2048 + (pr + 1) * 128], 128)
        proj_fm(P, C, wq, 128, C.psB[0:2], lambda ps, tb, tsl: act(P, [(A.qT, tb)], A.qT[:, tsl], ps[:, :], AF.Copy, [ps], scale=0.125))
        proj_fm(P, C, wk, 128, C.psB[0:2], lambda ps, tb, tsl: cp(P, "dve", [(A.kT, tb)], A.kT[:, tsl], ps[:, :], [ps]))
        proj_fm(P, C, wv, 128, C.psB[0:2], lambda ps, tb, tsl: act(P, [(A.vT, tb)], A.vT[:, tsl], ps[:, :], AF.Copy, [ps]))
        proj_fm(P, C, wg, 128, C.psB[0:2], lambda ps, tb, tsl: act(P, [(A.gT, tb)], A.gT[:, tsl], ps[:, :], AF.Silu, [ps]))
        for wi, src in enumerate((A.qT, A.kT)):
            for tb in range(8):
                tsl = slice(tb * 512, (tb + 1) * 512)
                sqb = A.sqb[tb % 2]
                tt(P, "pool", [sqb], sqb[:, :], src[:, tsl], src[:, tsl], ALU.mult, [src])
                ps = C.psB[tb % 2]
                mm(P, [ps], ps[0:2, :], A.blk1[:, :], sqb[:, :], [A.blk1, sqb])
                P.op("dve", lambda e, ps=ps, wi=wi, tb=tb: e.reduce_max(A.mx[0:2, wi * 8 + tb: wi * 8 + tb + 1], ps[0:2, :], AX.X),
                     [ps], [(A.mx, (wi, tb))], cost=(0.7, 0.7))
        P.op("dve", lambda e: e.reduce_max(A.m2[0:2, 0:1], A.mx[0:2, 0:8], AX.X), [A.mx], [(A.m2, 0)], cost=(0.1, 0.1))
        P.op("dve", lambda e: e.reduce_max(A.m2[0:2, 1:2], A.mx[0:2, 8:16], AX.X), [A.mx], [(A.m2, 1)], cost=(0.1, 0.1))
        tt(P, "dve", [(A.m2, 2)], A.m2[0:2, 2:3], A.m2[0:2, 0:1], A.m2[0:2, 1:2], ALU.mult, [(A.m2, 0), (A.m2, 1)])
        act(P, [(A.m2, 3)], A.m2[0:2, 3:4], A.m2[0:2, 2:3], AF.Ln, [(A.m2, 2), C.epsb], bias=C.epsb[0:2, 0:1])
        act(P, [(A.m2, 3)], A.m2[0:2, 3:4], A.m2[0:2, 3:4], AF.Exp, [(A.m2, 3)], scale=0.5)
        ts(P, "dve", [(A.m2, 3)], A.m2[0:2, 3:4], A.m2[0:2, 3:4], -1.0, None, ALU.mult, [(A.m2, 3)])
        ts(P, "dve", [A.dg2], A.dg2[0:2, 0:2], C.identf[0:2, 0:2], A.m2[0:2, 3:4], None, ALU.mult, [C.identf, (A.m2, 3)])
        psn = C.psB[0]
        mm(P, [psn], psn[:, 0:2], C.ones_f[0:2, :], A.dg2[0:2, 0:2], [C.ones_f, A.dg2])
        cp(P, "dve", [A.nb], A.nb[:, :], psn[:, 0:2], [psn])
        for hh in range(2):
            hd = pr * 2 + hh
            P.load(A.Gs, A.Gs[:, :], Gd, Gd[hd])
            act(P, [A.Gs], A.Gs[:, :], A.Gs[:, :], AF.Exp, [A.Gs])
            tt(P, "dve", [A.E[hh]], A.E[hh][:, :, :], A.Gs[:, :].rearrange("p (c m) -> p c m", c=3),
               A.mask[:, :].unsqueeze(1).to_broadcast([128, 3, 256]), ALU.mult, [A.Gs, A.mask])
        gi = 0
        for ci, (dil, nb) in enumerate(CFGS):
            for bg in range(8):
                pt = C.psTs[bg % 2]
                for u in range(4):
                    bi = bg * 4 + u
                    r, n = bi // nb, bi % nb
                    tr(P, [(pt, u)], pt[:, u * 128:(u + 1) * 128], A.vT[:, blk_slice(dil, r, n)], C.ident[:, :],
                       [A.vT, C.ident])
                pv4 = pt[:, 0:512].rearrange("p (u d) -> p u d", u=4)
                act(P, [(A.Vd[0], bg)], A.Vd[0][:, bg * 4:(bg + 1) * 4, 0:64], pv4[:, :, 0:64], AF.Copy, [pt])
                cp(P, "dve", [(A.Vd[1], bg)], A.Vd[1][:, bg * 4:(bg + 1) * 4, 64:128], pv4[:, :, 64:128], [pt])
            for hh in range(2):
                rows = slice(hh * 64, (hh + 1) * 64)
                G = min(4, nb)
                for r in range(dil):
                    for n0 in range(0, nb, G):
                        pss = C.psA[gi % 2]
                        pex = A.pex[gi % 2]
                        PT = A.PT[gi % 2]
                        gi += 1
                        for g in range(G):
                            n = n0 + g
                            qs = blk_slice(dil, r, n)
                            if n > 0:
                                mm(P, [(pss, (g, 0))], pss[:, g * 256: g * 256 + 128],
                                   A.kT[rows, blk_slice(dil, r, n - 1)], A.qT[rows, qs], [A.kT, A.qT])
                            mm(P, [(pss, (g, 1))], pss[:, g * 256 + 128: g * 256 + 256], A.kT[rows, qs], A.qT[rows, qs],
                               [A.kT, A.qT])
                        act(P, [pex], pex[:, 0:G * 256], pss[:, 0:G * 256], AF.Exp, [pss, A.nb], bias=A.nb[:, hh:hh + 1])
                        tt(P, "dve", [PT], PT[:, 0:G * 256].rearrange("p (g m) -> p g m", g=G),
                           pex[:, 0:G * 256].rearrange("p (g m) -> p g m", g=G),
                           A.E[hh][:, ci, :].unsqueeze(1).to_broadcast([128, G, 256]), ALU.mult, [pex, A.E[hh]])
                        pn = C.psB[gi % 2]
                        Vh = A.Vd[hh]
                        for g in range(G):
                            n = n0 + g
                            kbs = [1] if n == 0 else [0, 1]
                            for ix, kb in enumerate(kbs):
                                bi = r * nb + (n - 1 + kb)
                                rhs = PT[:, g * 256 + kb * 128: g * 256 + (kb + 1) * 128]
                                mm(P, [(pn, g)], pn[:, g * 128:(g + 1) * 128], Vh[:, bi, :], rhs, [Vh, PT],
                                   start=(ix == 0), stop=(ix == len(kbs) - 1))
                        dsl = blk_slice(dil, r, n0, G)
                        orow = slice((1 - hh) * 64, (2 - hh) * 64)
                        if ci == 0:
                            cp(P, "dve", [(A.anum, hh)], A.anum[rows, dsl], pn[rows, 0:G * 128], [pn])
                            act(P, [(A.aden, hh)], A.aden[orow, dsl], pn[orow, 0:G * 128], AF.Copy, [pn])
                        else:
                            tt(P, "dve", [(A.anum, hh)], A.anum[rows, dsl], pn[rows, 0:G * 128], A.anum[rows, dsl],
                               ALU.add, [pn, (A.anum, hh)])
                            tt(P, "dve", [(A.aden, hh)], A.aden[orow, dsl], pn[orow, 0:G * 128], A.aden[orow, dsl],
                               ALU.add, [pn, (A.aden, hh)])
        for q4 in range(8):
            oc = A.oTc[q4 % 2]
            rd = A.rden[q4 % 2]
            csl = slice(q4 * 512, (q4 + 1) * 512)
            P.op("dve", lambda e, rd=rd, csl=csl: e.reciprocal(rd[0:64, :], A.aden[64:128, csl]), [A.aden], [(rd, 0)], cost=(3.0, 3.0))
            P.op("dve", lambda e, rd=rd, csl=csl: e.reciprocal(rd[64:128, :], A.aden[0:64, csl]), [A.aden], [(rd, 1)], cost=(3.0, 3.0))
            tt(P, "dve", [rd], rd[:, :], A.anum[:, csl], rd[:, :], ALU.mult, [A.anum, rd])
            tt(P, "pool", [oc], oc[:, :], rd[:, :], A.gT[:, csl], ALU.mult, [rd, A.gT])
            P.store(C.oT_d, C.oT_d[pr, :, csl], oc, oc[:, :], key=pr)
    P.pop_scope()


LSEG = 256


def sincos(P, V, th):
    I32 = mybir.dt.int32
    kf, ki = V("kf"), P.sb("s5_ki", [128, 16], I32)
    t = V("sc_t")
    ts(P, "dve", [t], t[:, :], th[:, :], 0.6366197723675814, None, ALU.mult, [th])
    cp(P, "dve", [ki], ki[:, :], t[:, :], [t])
    cp(P, "dve", [kf], kf[:, :], ki[:, :], [ki])
    r = V("sc_r")
    stt(P, "dve", [r], r[:, :], kf[:, :], -1.5707963705062866, th[:, :], ALU.mult, ALU.add, [kf, th])
    stt(P, "dve", [r], r[:, :], kf[:, :], 4.371139000186243e-08, r[:, :], ALU.mult, ALU.add, [kf, r])
    r2 = V("sc_r2")
    tt(P, "dve", [r2], r2[:, :], r[:, :], r[:, :], ALU.mult, [r])

    def horner(name, coefs):
        p = V(name)
        ts(P, "dve", [p], p[:, :], r2[:, :], coefs[0], coefs[1], ALU.mult, [r2], op1=ALU.add)
        for c in coefs[2:]:
            tt(P, "dve", [p], p[:, :], p[:, :], r2[:, :], ALU.mult, [p, r2])
            ts(P, "dve", [p], p[:, :], p[:, :], c, None, ALU.add, [p])
        return p
    sp = horner("sc_sp", [1.0 / 362880, -1.0 / 5040, 1.0 / 120, -1.0 / 6, 1.0])
    sr = V("sc_sr")
    tt(P, "dve", [sr], sr[:, :], sp[:, :], r[:, :], ALU.mult, [sp, r])
    cr = horner("sc_cr", [-1.0 / 3628800, 1.0 / 40320, -1.0 / 720, 1.0 / 24, -0.5, 1.0])
    fl, fi = V("sc_fl"), P.sb("s5_fi", [128, 16], I32)
    ts(P, "dve", [fl], fl[:, :], kf[:, :], -1.5, 0.25, ALU.add, [kf], op1=ALU.mult)
    cp(P, "dve", [fi], fi[:, :], fl[:, :], [fl])
    cp(P, "dve", [fl], fl[:, :], fi[:, :], [fi])
    q = V("sc_q")
    stt(P, "dve", [q], q[:, :], fl[:, :], -4.0, kf[:, :], ALU.mult, ALU.add, [fl, kf])
    qa, qab = V("sc_qa"), V("sc_qab")
    stt(P, "dve", [qa], qa[:, :], q[:, :], -1.0, q[:, :], ALU.add, ALU.mult, [q])
    stt(P, "dve", [qab], qab[:, :], q[:, :], -2.0, qa[:, :], ALU.add, ALU.mult, [q, qa])
    ts(P, "dve", [qab], qab[:, :], qab[:, :], 1.0 / 3.0, None, ALU.mult, [qab])
    cq, sq = V("sc_cq"), V("sc_sq")
    tt(P, "dve", [cq], cq[:, :], qab[:, :], q[:, :], ALU.subtract, [qab, q])
    ts(P, "dve", [cq], cq[:, :], cq[:, :], 1.0, None, ALU.add, [cq])
    tt(P, "dve", [sq], sq[:, :], qab[:, :], qa[:, :], ALU.subtract, [qab, qa])
    tt(P, "dve", [sq], sq[:, :], sq[:, :], q[:, :], ALU.add, [sq, q])
    co, si, t2 = V("sc_cos"), V("sc_sin"), V("sc_t2")
    tt(P, "dve", [co], co[:, :], cr[:, :], cq[:, :], ALU.mult, [cr, cq])
    tt(P, "dve", [t2], t2[:, :], sr[:, :], sq[:, :], ALU.mult, [sr, sq])
    tt(P, "dve", [co], co[:, :], co[:, :], t2[:, :], ALU.subtract, [co, t2])
    tt(P, "dve", [si], si[:, :], sr[:, :], cq[:, :], ALU.mult, [sr, cq])
    tt(P, "dve", [t2], t2[:, :], cr[:, :], sq[:, :], ALU.mult, [cr, sq])
    tt(P, "dve", [si], si[:, :], si[:, :], t2[:, :], ALU.add, [si, t2])
    return co, si


def cmul_small(P, V, name, ar_, ai_, br_, bi_, sl=None):
    o_r, o_i, t = V(name + "_r"), V(name + "_i"), V(name + "_t")
    tt(P, "dve", [o_r], o_r[:, :], ar_[:, :], br_[:, :], ALU.mult, [ar_, br_])
    tt(P, "dve", [t], t[:, :], ai_[:, :], bi_[:, :], ALU.mult, [ai_, bi_])
    tt(P, "dve", [o_r], o_r[:, :], o_r[:, :], t[:, :], ALU.subtract, [o_r, t])
    tt(P, "dve", [o_i], o_i[:, :], ar_[:, :], bi_[:, :], ALU.mult, [ar_, bi_])
    tt(P, "dve", [t], t[:, :], ai_[:, :], br_[:, :], ALU.mult, [ai_, br_])
    tt(P, "dve", [o_i], o_i[:, :], o_i[:, :], t[:, :], ALU.add, [o_i, t])
    return o_r, o_i


def stage_s5_proj(P, C, w_in_d):
    P.push_scope()
    w = P.sb("s5w", [128, 8, 128], BF16)
    ub = [P.sb("s5ub%d" % i, [128, 512], F32) for i in range(2)]
    gb = [P.sb("s5gb%d" % i, [128, 512], BF16) for i in range(2)]
    for cq in range(4):
        load_w_bf16(P, C, w, w[:, :, :], w_in_d, w_in_d[:, 1536 + cq * 128: 1536 + (cq + 1) * 128], 128)

        def ev_u(ps, tb, tsl, cq=cq):
            t = ub[tb % 2]
            act(P, [t], t[:, :], ps[:, :], AF.Copy, [ps])
            P.store(C.uT_d, C.uT_d[cq, :, tsl], t, t[:, :], key=cq)
        proj_fm(P, C, w, 128, C.psB, ev_u)
        load_w_bf16(P, C, w, w[:, :, :], w_in_d, w_in_d[:, 2560 + cq * 128: 2560 + (cq + 1) * 128], 128)

        def ev_g(ps, tb, tsl, cq=cq):
            t = gb[tb % 2]
            act(P, [t], t[:, :], ps[:, :], AF.Silu, [ps])
            P.store(C.gB_d, C.gB_d[cq, :, tsl], t, t[:, :], key=cq)
        proj_fm(P, C, w, 128, C.psB, ev_g)
    P.pop_scope()


def stage_s5(P, C, Dm):
    P.push_scope()
    nv = [0]

    def V(name):
        nv[0] += 1
        return P.sb("s5v_%s_%d" % (name, nv[0]), [128, 16], F32)

    def ldv(name, d):
        v = V(name)
        P.load(v, v[:, :], d, d[:, :])
        return v
    ar, ai, ldt = ldv("ar", Dm["ar"]), ldv("ai", Dm["ai"]), ldv("ldt", Dm["ldt"])
    dt, dar, mag, th = V("dt"), V("dar"), V("mag"), V("th")
    act(P, [dt], dt[:, :], ldt[:, :], AF.Exp, [ldt])
    tt(P, "dve", [dar], dar[:, :], dt[:, :], ar[:, :], ALU.mult, [dt, ar])
    act(P, [mag], mag[:, :], dar[:, :], AF.Exp, [dar])
    tt(P, "dve", [th], th[:, :], dt[:, :], ai[:, :], ALU.mult, [dt, ai])
    co, si = sincos(P, V, th)
    lr, li = V("lr"), V("li")
    tt(P, "dve", [lr], lr[:, :], mag[:, :], co[:, :], ALU.mult, [mag, co])
    tt(P, "dve", [li], li[:, :], mag[:, :], si[:, :], ALU.mult, [mag, si])
    lr1, den, fr, fi_, t = V("lr1"), V("den"), V("fr"), V("fi"), V("t")
    ts(P, "dve", [lr1], lr1[:, :], lr[:, :], -1.0, None, ALU.add, [lr])
    tt(P, "dve", [den], den[:, :], ar[:, :], ar[:, :], ALU.mult, [ar])
    tt(P, "dve", [t], t[:, :], ai[:, :], ai[:, :], ALU.mult, [ai])
    tt(P, "dve", [den], den[:, :], den[:, :], t[:, :], ALU.add, [den, t])
    P.op("dve", lambda e: e.reciprocal(den[:, :], den[:, :]), [den], [den])
    tt(P, "dve", [fr], fr[:, :], lr1[:, :], ar[:, :], ALU.mult, [lr1, ar])
    tt(P, "dve", [t], t[:, :], li[:, :], ai[:, :], ALU.mult, [li, ai])
    tt(P, "dve", [fr], fr[:, :], fr[:, :], t[:, :], ALU.add, [fr, t])
    tt(P, "dve", [fr], fr[:, :], fr[:, :], den[:, :], ALU.mult, [fr, den])
    tt(P, "dve", [fi_], fi_[:, :], li[:, :], ar[:, :], ALU.mult, [li, ar])
    tt(P, "dve", [t], t[:, :], lr1[:, :], ai[:, :], ALU.mult, [lr1, ai])
    tt(P, "dve", [fi_], fi_[:, :], fi_[:, :], t[:, :], ALU.subtract, [fi_, t])
    tt(P, "dve", [fi_], fi_[:, :], fi_[:, :], den[:, :], ALU.mult, [fi_, den])
    Bre, Bim = P.sb("s5Bre", [128, 16, 16], F32), P.sb("s5Bim", [128, 16, 16], F32)
    P.load(Bre, Bre[:, :, :], Dm["b_re"], Dm["b_re"][:, :, :])
    P.load(Bim, Bim[:, :, :], Dm["b_im"], Dm["b_im"][:, :, :])
    Bbr, Bbi, Bt = P.sb("s5Bbr", [128, 16, 16], F32), P.sb("s5Bbi", [128, 16, 16], F32), P.sb("s5Bt", [128, 16, 16], F32)
    bc = lambda v: v[:, :].unsqueeze(2).to_broadcast([128, 16, 16])
    tt(P, "dve", [Bbr], Bbr[:, :, :], Bre[:, :, :], bc(fr), ALU.mult, [Bre, fr])
    tt(P, "dve", [Bt], Bt[:, :, :], Bim[:, :, :], bc(fi_), ALU.mult, [Bim, fi_])
    tt(P, "dve", [Bbr], Bbr[:, :, :], Bbr[:, :, :], Bt[:, :, :], ALU.subtract, [Bbr, Bt])
    tt(P, "dve", [Bbi], Bbi[:, :, :], Bim[:, :, :], bc(fr), ALU.mult, [Bim, fr])
    tt(P, "dve", [Bt], Bt[:, :, :], Bre[:, :, :], bc(fi_), ALU.mult, [Bre, fi_])
    tt(P, "dve", [Bbi], Bbi[:, :, :], Bbi[:, :, :], Bt[:, :, :], ALU.add, [Bbi, Bt])
    BpT = P.sb("s5BpT", [128, 16, 2, 128], BF16)
    Bblk = [P.sb("s5Bblk%d" % i, [128, 128], F32) for i in range(2)]
    n = 0
    for q in range(16):
        base = 32 * (q % 4)
        for ri, src in enumerate((Bbr, Bbi)):
            bb = Bblk[n % 2]
            ps = C.psB[n % 2]
            n += 1
            mset(P, "pool", [bb], bb[:, :], 0.0)
            cp(P, "pool", [bb], bb[0:64, base:base + 16], src[0:64, q, :], [src, bb])
            cp(P, "pool", [bb], bb[64:128, base + 16:base + 32], src[64:128, q, :], [src, bb])
            tr(P, [ps], ps[:, 0:128], bb[:, :], C.identf[:, :], [bb, C.identf])
            act(P, [(BpT, (q, ri))], BpT[:, q, ri, :], ps[:, 0:128], AF.Copy, [ps])
    CpT = P.sb("s5CpT", [128, 16, 2, 128], BF16)
    Cre, Cim = P.sb("s5Cre", [128, 16, 16], F32), P.sb("s5Cim", [128, 16, 16], F32)
    P.load(Cre, Cre[:, :, :], Dm["cT_re"], Dm["cT_re"][:, :, :])
    P.load(Cim, Cim[:, :, :], Dm["cT_im"], Dm["cT_im"][:, :, :])
    ts(P, "dve", [Cim], Cim[:, :, :], Cim[:, :, :], -1.0, None, ALU.mult, [Cim])
    mset(P, "pool", [CpT], CpT[:, :, :, :], 0.0)
    for q in range(16):
        base = 32 * (q % 4)
        for ri, src in enumerate((Cre, Cim)):
            cp(P, "dve", [CpT], CpT[0:64, q, ri, base:base + 16], src[0:64, q, :], [src, CpT])
            cp(P, "dve", [CpT], CpT[64:128, q, ri, base + 16:base + 32], src[64:128, q, :], [src, CpT])
    L = LSEG
    ct, st = P.sb("s5ct", [128, 4, L], F32), P.sb("s5st", [128, 4, L], F32)
    tA, tB = P.sb("s5tA", [128, 4, L], F32), P.sb("s5tB", [128, 4, L], F32)
    btr, bti = P.sb("s5btr", [128, 4, L], F32), P.sb("s5bti", [128, 4, L], F32)
    xtr, xti = P.sb("s5xtr", [128, 4, L], F32), P.sb("s5xti", [128, 4, L], F32)
    xr, xi = P.sb("s5xr", [128, 4, L], BF16), P.sb("s5xi", [128, 4, L], BF16)
    uf = [P.sb("s5uf%d" % i, [128, L], F32) for i in range(2)]
    ubf = [P.sb("s5ubf%d" % i, [128, L], BF16) for i in range(2)]
    yv, y2, yw, ysg = (P.sb("s5y%d" % i, [128, L], F32) for i in range(4))
    car_r, car_i, cl_t = P.sb("s5car_r", [128, 4], F32), P.sb("s5car_i", [128, 4], F32), P.sb("s5cl_t", [128, 4], F32)
    ncr0, nci0 = P.sb("s5ncr", [128, 4], F32), P.sb("s5nci", [128, 4], F32)
    cl_t2 = P.sb("s5cl_t2", [128, 4], F32)
    dfm = P.sb("s5dfm", [128, 4], F32)
    P.load(dfm, dfm[:, :], Dm["d_fm"], Dm["d_fm"][:, :])
    ygT = C.big
    for cq in range(4):
        qs = slice(4 * cq, 4 * cq + 4)
        ur, ui = co, si
        mset(P, "pool", [ct], ct[:, :, 0:1], 1.0)
        mset(P, "pool", [st], st[:, :, 0:1], 0.0)
        cp(P, "dve", [ct], ct[:, :, 1:2], co[:, qs].unsqueeze(2), [co, ct])
        cp(P, "dve", [st], st[:, :, 1:2], si[:, qs].unsqueeze(2), [si, st])
        k = 1
        while (1 << k) < L:
            nn = 1 << k
            ur, ui = cmul_small(P, V, "u%d_%d" % (cq, k), ur, ui, ur, ui)
            bcu = lambda v: v[:, qs].unsqueeze(2).to_broadcast([128, 4, nn])
            tt(P, "dve", [tA], tA[:, :, 0:nn], ct[:, :, 0:nn], bcu(ur), ALU.mult, [ct, ur])
            tt(P, "dve", [tB], tB[:, :, 0:nn], st[:, :, 0:nn], bcu(ui), ALU.mult, [st, ui])
            tt(P, "dve", [ct], ct[:, :, nn:2 * nn], tA[:, :, 0:nn], tB[:, :, 0:nn], ALU.subtract, [tA, tB, ct])
            tt(P, "dve", [tA], tA[:, :, 0:nn], ct[:, :, 0:nn], bcu(ui), ALU.mult, [ct, ui])
            tt(P, "dve", [tB], tB[:, :, 0:nn], st[:, :, 0:nn], bcu(ur), ALU.mult, [st, ur])
            tt(P, "dve", [st], st[:, :, nn:2 * nn], tA[:, :, 0:nn], tB[:, :, 0:nn], ALU.add, [tA, tB, st])
            k += 1
        uLr, uLi = cmul_small(P, V, "uL%d" % cq, ur, ui, ur, ui)
        cars = ((car_r, car_i), (ncr0, nci0))
        mset(P, "pool", [car_r], car_r[:, :], 0.0)
        mset(P, "pool", [car_i], car_i[:, :], 0.0)
        for seg in range(S // L):
            tsl = slice(seg * L, (seg + 1) * L)
            car_a, car_b = cars[seg % 2]
            u_f, u_b = uf[seg % 2], ubf[seg % 2]
            P.load(u_f, u_f[:, :], C.uT_d, C.uT_d[cq, :, tsl], key=cq)
            cp(P, "pool", [u_b], u_b[:, :], u_f[:, :], [u_f])
            pre, pim = C.psA[0], C.psA[1]
            for pr in range(4):
                mm(P, [pre], pre[:, pr * L:(pr + 1) * L], BpT[:, 4 * cq + pr, 0, :], u_b[:, :], [BpT, u_b])
                mm(P, [pim], pim[:, pr * L:(pr + 1) * L], BpT[:, 4 * cq + pr, 1, :], u_b[:, :], [BpT, u_b])
            v3 = lambda b: b[:, :, :]
            p3 = lambda b: b[:, 0:4 * L].rearrange("p (a t) -> p a t", a=4)
            tt(P, "dve", [tA], v3(tA), p3(pre), v3(ct), ALU.mult, [pre, ct])
            tt(P, "dve", [tB], v3(tB), p3(pim), v3(st), ALU.mult, [pim, st])
            tt(P, "dve", [btr], v3(btr), v3(tA), v3(tB), ALU.add, [tA, tB])
            tt(P, "dve", [tA], v3(tA), p3(pim), v3(ct), ALU.mult, [pim, ct])
            tt(P, "dve", [tB], v3(tB), p3(pre), v3(st), ALU.mult, [pre, st])
            tt(P, "dve", [bti], v3(bti), v3(tA), v3(tB), ALU.subtract, [tA, tB])
            for pr in range(4):
                q = 4 * cq + pr
                for (src, dst, car) in ((btr, xtr, car_a), (bti, xti, car_b)):
                    P.op("dve", lambda e, src=src, dst=dst, car=car, pr=pr, q=q: e.tensor_tensor_scan(
                        dst[:, pr, :], mag[:, q:q + 1].to_broadcast([128, L]), src[:, pr, :], car[:, pr:pr + 1],
                        ALU.mult, ALU.add), [src, mag, car], [(dst, pr)])
            lre, lim = xtr[:, :, L - 1], xti[:, :, L - 1]
            ncr, nci = cars[(seg + 1) % 2]
            tt(P, "dve", [ncr], ncr[:, :], lre, uLr[:, qs], ALU.mult, [xtr, uLr])
            tt(P, "dve", [cl_t], cl_t[:, :], lim, uLi[:, qs], ALU.mult, [xti, uLi])
            tt(P, "dve", [ncr], ncr[:, :], ncr[:, :], cl_t[:, :], ALU.subtract, [ncr, cl_t])
            tt(P, "dve", [nci], nci[:, :], lre, uLi[:, qs], ALU.mult, [xtr, uLi])
            tt(P, "dve", [cl_t2], cl_t2[:, :], lim, uLr[:, qs], ALU.mult, [xti, uLr])
            tt(P, "dve", [nci], nci[:, :], nci[:, :], cl_t2[:, :], ALU.add, [nci, cl_t2])
            tt(P, "dve", [tA], v3(tA), v3(xtr), v3(ct), ALU.mult, [xtr, ct])
            tt(P, "pool", [tB], v3(tB), v3(xti), v3(st), ALU.mult, [xti, st])
            tt(P, "dve", [xr], v3(xr), v3(tA), v3(tB), ALU.subtract, [tA, tB])
            tt(P, "pool", [btr], v3(btr), v3(xtr), v3(st), ALU.mult, [xtr, st])
            tt(P, "pool", [bti], v3(bti), v3(xti), v3(ct), ALU.mult, [xti, ct])
            tt(P, "pool", [xi], v3(xi), v3(btr), v3(bti), ALU.add, [btr, bti])
            py = C.psB[seg % 2]
            for pr in range(4):
                q = 4 * cq + pr
                mm(P, [py], py[:, 0:L], CpT[:, q, 0, :], xr[:, pr, :], [CpT, xr], start=(pr == 0), stop=False)
                mm(P, [py], py[:, 0:L], CpT[:, q, 1, :], xi[:, pr, :], [CpT, xi], start=False, stop=(pr == 3))
            stt(P, "dve", [yv], yv[:, :], u_f[:, :], dfm[:, cq:cq + 1], py[:, 0:L], ALU.mult, ALU.add, [u_f, dfm, py])
            tt(P, "pool", [y2], y2[:, :], yv[:, :], yv[:, :], ALU.mult, [yv])
            ts(P, "pool", [y2], y2[:, :], y2[:, :], 0.044715, 1.0, ALU.mult, [y2], op1=ALU.add)
            tt(P, "pool", [yw], yw[:, :], y2[:, :], yv[:, :], ALU.mult, [y2, yv])
            act(P, [ysg], ysg[:, :], yw[:, :], AF.Sigmoid, [yw], scale=1.5957691216057308)
            tt(P, "pool", [(ygT, ("yg", cq, seg))], ygT[:, cq, tsl], yv[:, :], ysg[:, :], ALU.mult, [yv, ysg])
    gw = P.sb("s5gw", [128, 4, 128], BF16)
    gbias = P.sb("s5gbias", [128, 4], F32)
    P.load(gbias, gbias[:, :], Dm["glu_b_fm"], Dm["glu_b_fm"][:, :])
    gT = P.sb("s5gT", [128, S], BF16)
    sg = [P.sb("s5sg%d" % i, [128, 512], F32) for i in range(2)]
    oc_t = [P.sb("s5oc%d" % i, [128, 512], BF16) for i in range(2)]
    for oc in range(4):
        st_ = C.wstage[C.wsi % 2]
        C.wsi += 1
        P.load(st_, st_[:, 0:4, :], Dm["glu_w"], Dm["glu_w"][:, oc * 128:(oc + 1) * 128].rearrange("(k p) c -> p k c", p=128))
        cp(P, "dve", [gw], gw[:, :, :], st_[:, 0:4, :], [st_])
        P.load(gT, gT[:, :], C.gB_d, C.gB_d[oc], key=oc)
        for tb in range(8):
            tsl = slice(tb * 512, (tb + 1) * 512)
            ps = C.psB[tb % 2]
            for cq in range(4):
                mm(P, [ps], ps[:, :], gw[:, cq, :], ygT[:, cq, tsl], [gw, ygT], start=(cq == 0), stop=(cq == 3))
            s_ = sg[tb % 2]
            o_ = oc_t[tb % 2]
            act(P, [s_], s_[:, :], ps[:, :], AF.Sigmoid, [ps, gbias], bias=gbias[:, oc:oc + 1])
            tt(P, "dve", [s_], s_[:, :], s_[:, :], ygT[:, oc, tsl], ALU.mult, [s_, ygT])
            tt(P, "pool", [o_], o_[:, :], s_[:, :], gT[:, tsl], ALU.mult, [s_, gT])
            P.store(C.oT_d, C.oT_d[4 + oc, :, tsl], o_, o_[:, :], key=4 + oc)
    P.pop_scope()


def stage_gdn_proj(P, C, w_in_d, Dg, G):
    P.push_scope()
    w = P.sb("gdw", [128, 8, 128], BF16)
    convw = P.sb("gconvw", [128, 24, 4], F32)
    P.load(convw, convw[:, :, :], Dg["conv_fm"], Dg["conv_fm"][:, :, :])
    zp = [P.sb("gzp%d" % i, [128, 515], F32) for i in range(3)]
    acc = [P.sb("gacc%d" % i, [128, 512], F32) for i in range(4)]
    sl = [P.sb("gsl%d" % i, [128, 512], F32) for i in range(4)]
    sq = [P.sb("gsq%d" % i, [128, 512], F32) for i in range(4)]
    rs = [P.sb("grs%d" % i, [128, 512], F32) for i in range(4)]
    ob = [P.sb("gob%d" % i, [128, 512], BF16) for i in range(4)]
    dsts = (G.qT_d, G.kT_d, G.vT_d, G.gT_d)
    n = 0
    for typ in range(4):
        for hd in range(8):
            ch = typ * 8 + hd
            load_w_bf16(P, C, w, w[:, :, :], w_in_d, w_in_d[:, ch * 128:(ch + 1) * 128], 128)
            for tb in range(8):
                tsl = slice(tb * 512, (tb + 1) * 512)
                ps = (C.psB[0], C.psB[1], C.psTf[0])[n % 3]
                for k in range(8):
                    mm(P, [ps], ps[:, :], w[:, k, :], C.big[:, k, tsl], [w, C.big], start=(k == 0), stop=(k == 7))
                o_ = ob[n % 4]
                if typ == 3:
                    act(P, [o_], o_[:, :], ps[:, :], AF.Silu, [ps])
                else:
                    z, zprev = zp[tb % 3], zp[(tb + 2) % 3]
                    if tb == 0:
                        mset(P, "pool", [z], z[:, 0:3], 0.0)
                    else:
                        cp(P, "pool", [z], z[:, 0:3], zprev[:, 512:515], [zprev, z])
                    act(P, [z], z[:, 3:515], ps[:, :], AF.Copy, [ps, z])
                    a_ = acc[n % 4]
                    ts(P, "dve", [a_], a_[:, :], z[:, 3:515], convw[:, ch, 3:4], None, ALU.mult, [z, convw])
                    for j in (2, 1, 0):
                        stt(P, "dve", [a_], a_[:, :], z[:, j:j + 512], convw[:, ch, j:j + 1], a_[:, :], ALU.mult, ALU.add,
                            [z, convw, a_])
                    if typ == 2:
                        act(P, [o_], o_[:, :], a_[:, :], AF.Silu, [a_])
                    else:
                        s_, q_, r_ = sl[n % 4], sq[n % 4], rs[n % 4]
                        act(P, [s_], s_[:, :], a_[:, :], AF.Silu, [a_])
                        tt(P, "pool", [q_], q_[:, :], s_[:, :], s_[:, :], ALU.mult, [s_])
                        pss = (C.psA[0], C.psA[1], C.psTf[1])[n % 3]
                        mm(P, [pss], pss[:, 0:512], C.ones_f[:, :], q_[:, :], [C.ones_f, q_])
                        act(P, [r_], r_[:, :], pss[:, 0:512], AF.Ln, [pss, C.epsb], bias=C.epsb[:, 0:1])
                        act(P, [r_], r_[:, :], r_[:, :], AF.Exp, [r_], scale=-0.5)
                        stt(P, "dve", [o_], o_[:, :], s_[:, :], (128.0 ** -0.5) if typ == 0 else 1.0, r_[:, :],
                            ALU.mult, ALU.mult, [s_, r_])
                P.store(dsts[typ], dsts[typ][hd, :, tsl], o_, o_[:, :], key=hd)
                n += 1
    w8 = P.sb("gdw8", [128, 8, 8], BF16)
    for (c0, dst) in ((4096, G.R0), (4104, G.R1)):
        st_ = C.wstage[C.wsi % 2]
        C.wsi += 1
        P.load(st_, st_[:, :, 0:8], w_in_d, w_in_d[:, c0:c0 + 8].rearrange("(k p) c -> p k c", p=128))
        cp(P, "dve", [w8], w8[:, :, :], st_[:, :, 0:8], [st_])
        for tb in range(8):
            tsl = slice(tb * 512, (tb + 1) * 512)
            ps = C.psB[tb % 2]
            for k in range(8):
                mm(P, [ps], ps[0:8, :], w8[:, k, :], C.big[:, k, tsl], [w8, C.big], start=(k == 0), stop=(k == 7))
            act(P, [(dst, tb)], dst[0:8, tsl], ps[0:8, :], AF.Copy, [ps])
    P.pop_scope()


def stage_gdn_rows(P, C, Dg, G):
    P.push_scope()
    R0, R1 = G.R0, G.R1
    R2, R3 = P.sb("gR2", [8, S], F32), P.sb("gR3", [8, S], F32)
    alog, dtb, nA = P.sb("galog", [8, 1], F32), P.sb("gdtb", [8, 1], F32), P.sb("gnA", [8, 1], F32)
    P.load(alog, alog[:, :], Dg["a_log"], Dg["a_log"][:, :])
    P.load(dtb, dtb[:, :], Dg["dt_bias"], Dg["dt_bias"][:, :])
    act(P, [nA], nA[:, :], alog[:, :], AF.Exp, [alog])
    ts(P, "dve", [nA], nA[:, :], nA[:, :], -1.0, None, ALU.mult, [nA])
    act(P, [R0], R0[:, :], R0[:, :], AF.Sigmoid, [R0])
    act(P, [R1], R1[:, :], R1[:, :], AF.Exp, [R1, dtb], bias=dtb[:, 0:1])
    act(P, [R1], R1[:, :], R1[:, :], AF.Ln, [R1], bias=1.0)
    ts(P, "dve", [R1], R1[:, :], R1[:, :], nA[:, 0:1], None, ALU.mult, [R1, nA])
    mset(P, "pool", [R2], R2[:, :], 1.0)
    mset(P, "pool", [R2], R2[:, 0:S:64], 0.0)
    P.op("dve", lambda e: e.tensor_tensor_scan(R3[:, :], R2[:, :], R1[:, :], 0.0, ALU.mult, ALU.add), [R2, R1], [R3])

    def to_cols(row, kq):
        for t in range(NT):
            ps = C.psB[t % 2]
            tr(P, [ps], ps[:, 0:8], row[0:8, t * 128:(t + 1) * 128], C.identf[0:8, 0:8], [row, C.identf])
            act(P, [(G.cols, (t, kq))], G.cols[:, t, kq, :], ps[:, 0:8], AF.Copy, [ps])
    to_cols(R3, 0)
    to_cols(R0, 1)
    act(P, [R1], R1[:, :], R3[:, :], AF.Exp, [R3])
    tt(P, "dve", [R2], R2[:, :], R0[:, :], R1[:, :], ALU.mult, [R0, R1])
    to_cols(R2, 2)
    gc3 = R3[:, :].rearrange("p (c t) -> p c t", t=64)
    gl = R3[:, 63:S:64]
    tt(P, "dve", [R2], R2[:, :].rearrange("p (c t) -> p c t", t=64), gl.unsqueeze(2).to_broadcast([8, 64, 64]), gc3,
       ALU.subtract, [R3])
    act(P, [R2], R2[:, :], R2[:, :], AF.Exp, [R2])
    to_cols(R2, 3)
    dlrow = P.sb("gdlrow", [8, 64], F32)
    act(P, [dlrow], dlrow[:, :], gl, AF.Exp, [R3])
    for h in range(8):
        ps = C.psB[h % 2]
        mm(P, [ps], ps[:, 0:64], G.sel8[0:8, h, :], dlrow[0:8, :], [G.sel8, dlrow])
        act(P, [(G.DL, h)], G.DL[:, h, :], ps[:, 0:64], AF.Copy, [ps])
    ts(P, "dve", [R0], R0[:, :], R3[:, :], -1.0, None, ALU.mult, [R3])
    P.pop_scope()


def stage_gdn_main(P, C, Dg, G):
    P.push_scope()
    big = C.big
    sets = []
    for si in range(2):
        vcnt = [0]
        F3 = lambda nm: P.sb("g1%s_%d" % (nm, si), [128, 8, 128], F32)

        def B3(nm, si=si, vcnt=vcnt):
            if si == 0:
                return P.sb("g1%s_%d" % (nm, si), [128, 8, 128], BF16)
            k = vcnt[0]
            vcnt[0] += 1
            ap = big.t[:, 4 + k // 4, (k % 4) * 1024:(k % 4 + 1) * 1024].rearrange("p (a t) -> p a t", a=8)
            bb = Buf("g1v%s" % nm, ap, "sb")
            P.bufs.append(bb)
            return bb
        T = Ctx()
        T.Kbe, T.Kd, T.bV, T.ADf, T.ADT, T.Rb, T.WT = (B3(n) for n in ("Kbe", "Kd", "bV", "ADf", "ADT", "Rb", "WT"))
        T.E_, T.Rr = (F3(n) for n in ("E", "R"))
        T.Nn, T.Xx = B3("N"), B3("X")
        T.Nk, T.Xk = [B3("Nk%d" % i) for i in range(2)], [B3("Xk%d" % i) for i in range(2)]
        T.U = P.sb("g1U_%d" % si, [64, 16, 128], F32)
        sets.append(T)
    gcount = [0]
    madd, m01 = P.sb("gmadd", [128, 128], F32), P.sb("gm01", [128, 128], F32)
    P.load(madd, madd[:, :], Dg["maskadd"], Dg["maskadd"][:, :])
    P.load(m01, m01[:, :], Dg["strict01"], Dg["strict01"][:, :])
    qT, kT, vT, qgT = (big[:, i, :] for i in range(4))
    bc8 = lambda ap2: ap2.unsqueeze(2).to_broadcast([128, 8, 128])
    bcm = lambda m: m[:, :].unsqueeze(1).to_broadcast([128, 8, 128])
    v3 = lambda b: b[:, :, :]
    p3 = lambda ps: ps[:, 0:1024].rearrange("p (a t) -> p a t", a=8)
    for hd in range(8):
        for i_, d_ in enumerate((G.qT_d, G.kT_d, G.vT_d)):
            P.load(big, big[:, i_, :], d_, d_[hd], key=hd)
        for tb in range(8):
            tsl = slice(tb * 512, (tb + 1) * 512)
            ps = C.psB[tb % 2]
            mm(P, [ps], ps[:, :], G.sel8[0:8, hd, :], G.R1[0:8, tsl], [G.sel8, G.R1])
            tt(P, "dve", [(big, ("qg", tb))], big[:, 3, tsl], ps[:, :], big[:, 0, tsl], ALU.mult, [ps, (big, ("in", 0))])
        P.store(G.qg_d, G.qg_d[hd], big, big[:, 3, :], key=hd)
        for g in range(4):
            T = sets[gcount[0] % 2]
            gcount[0] += 1
            Kbe, Kd, bV, ADf, ADT, Rb, WT = T.Kbe, T.Kd, T.bV, T.ADf, T.ADT, T.Rb, T.WT
            E_, Rr, Nn, Xx, Nk, Xk, U = T.E_, T.Rr, T.Nn, T.Xx, T.Nk, T.Xk, T.U
            t0 = g * 8
            tsls = [slice((t0 + u) * 128, (t0 + u + 1) * 128) for u in range(8)]
            col = lambda kq: bc8(G.cols[:, t0:t0 + 8, kq, hd])
            pk_, pv_ = C.psTs
            for u in range(8):
                tr(P, [pk_], pk_[:, u * 128:(u + 1) * 128], kT[:, tsls[u]], C.ident[:, :], [big, C.ident])
            for u in range(8):
                tr(P, [pv_], pv_[:, u * 128:(u + 1) * 128], vT[:, tsls[u]], C.ident[:, :], [big, C.ident])
            tt(P, "dve", [Kbe], v3(Kbe), p3(pk_), col(2), ALU.mult, [pk_, G.cols])
            tt(P, "dve", [Kd], v3(Kd), p3(pk_), col(3), ALU.mult, [pk_, G.cols])
            tt(P, "dve", [bV], v3(bV), p3(pv_), col(1), ALU.mult, [pv_, G.cols])
            for hlf in range(2):
                pe_ = C.psB[hlf]
                mm(P, [pe_], pe_[:, :], G.sel8[0:8, hd, :], G.R0[0:8, (t0 + 4 * hlf) * 128:(t0 + 4 * hlf + 4) * 128], [G.sel8, G.R0])
                tt(P, "dve", [(E_, hlf)], E_[:, 4 * hlf:4 * hlf + 4, :], pe_[:, :].rearrange("p (a t) -> p a t", a=4),
                   madd[:, :].unsqueeze(1).to_broadcast([128, 4, 128]), ALU.add, [pe_, madd])
            tt(P, "pool", [E_], v3(E_), v3(E_), col(0), ALU.add, [E_, G.cols])
            act(P, [E_], v3(E_), v3(E_), AF.Exp, [E_])
            pkk, pa = C.psA
            for u in range(8):
                mm(P, [pkk], pkk[:, u * 128:(u + 1) * 128], kT[:, tsls[u]], kT[:, tsls[u]], [big])
            for u in range(8):
                mm(P, [pa], pa[:, u * 128:(u + 1) * 128], qT[:, tsls[u]], kT[:, tsls[u]], [big])
            tt(P, "dve", [ADf], v3(ADf), p3(pa), v3(E_), ALU.mult, [pa, E_])
            tt(P, "dve", [Nn], v3(Nn), p3(pkk), v3(E_), ALU.mult, [pkk, E_])
            tt(P, "pool", [Nn], v3(Nn), v3(Nn), col(1), ALU.mult, [Nn, G.cols])
            tt(P, "pool", [Nn], v3(Nn), v3(Nn), bcm(m01), ALU.mult, [Nn, m01])
            for u in range(8):
                tr(P, [pk_], pk_[:, u * 128:(u + 1) * 128], ADf[:, u, :], C.ident[:, :], [ADf, C.ident])
            act(P, [ADT], v3(ADT), p3(pk_), AF.Copy, [pk_])
            for u in range(8):
                tr(P, [pv_], pv_[:, u * 128:(u + 1) * 128], Nn[:, u, :], C.ident[:, :], [Nn, C.ident])
            act(P, [Xx], v3(Xx), p3(pv_), AF.Copy, [pv_])
            tt(P, "dve", [Rr], v3(Rr), bcm(C.identf), p3(pv_), ALU.subtract, [C.identf, pv_])
            act(P, [Rb], v3(Rb), v3(Rr), AF.Copy, [Rr])
            Ncur, Xcur = Nn, Xx
            for lv in range(1, 6):
                nk, xk = Nk[lv % 2], Xk[lv % 2]
                for u in range(8):
                    mm(P, [pa], pa[:, u * 128:(u + 1) * 128], Xcur[:, u, :], Ncur[:, u, :], [Xcur, Ncur])
                act(P, [nk], v3(nk), p3(pa), AF.Copy, [pa])
                if lv < 5:
                    for u in range(8):
                        mm(P, [pkk], pkk[:, u * 128:(u + 1) * 128], Ncur[:, u, :], Xcur[:, u, :], [Xcur, Ncur])
                    cp(P, "dve", [xk], v3(xk), p3(pkk), [pkk])
                for hlf in range(2):
                    pr_ = C.psB[hlf]
                    for u in range(4):
                        mm(P, [pr_], pr_[:, u * 128:(u + 1) * 128], nk[:, 4 * hlf + u, :], Rb[:, 4 * hlf + u, :], [nk, Rb])
                for hlf in range(2):
                    pr_ = C.psB[hlf]
                    tt(P, "dve", [(Rr, hlf)], Rr[:, 4 * hlf:4 * hlf + 4, :], Rr[:, 4 * hlf:4 * hlf + 4, :],
                       pr_[:, :].rearrange("p (a t) -> p a t", a=4), ALU.add, [(Rr, hlf), pr_])
                act(P, [Rb], v3(Rb), v3(Rr), AF.Copy, [Rr])
                Ncur, Xcur = nk, xk
            for hlf in range(2):
                pu = C.psA[hlf]
                for u in range(4):
                    for hh in range(2):
                        cidx = u * 2 + hh
                        mm(P, [pu], pu[0:64, cidx * 128:(cidx + 1) * 128], Rb[:, 4 * hlf + u, hh * 64:(hh + 1) * 64],
                           bV[:, 4 * hlf + u, :], [Rb, bV])
                act(P, [(U, hlf)], U[:, 8 * hlf:8 * hlf + 8, :], pu[0:64, 0:1024].rearrange("p (a t) -> p a t", a=8),
                    AF.Copy, [pu])
            pw = C.psA[0]
            for u in range(8):
                mm(P, [pw], pw[:, u * 128:(u + 1) * 128], Kbe[:, u, :], Rb[:, u, :], [Kbe, Rb])
            cp(P, "dve", [WT], v3(WT), p3(pw), [pw])
            P.store(G.sWT, G.sWT[hd, t0:t0 + 8].rearrange("t p d -> p t d"), WT, v3(WT), key=(hd, g), eng="sp")
            P.store(G.sADT, G.sADT[hd, t0:t0 + 8].rearrange("t p d -> p t d"), ADT, v3(ADT), key=(hd, g), eng="sp")
            P.store(G.sKd, G.sKd[hd, t0:t0 + 8].rearrange("t p d -> p t d"), Kd, v3(Kd), key=(hd, g), eng="sp")
            P.store(G.sU, G.sU[hd, 2 * t0:2 * t0 + 16].rearrange("c p d -> p c d"), U, U[:, :, :], key=(hd, g), eng="sp")
    P.pop_scope()


def stage_gdn_rec(P, C, Dg, G):
    P.push_scope()
    BT = lambda nm: [P.sb("g2%s%d" % (nm, i), [128, 8, 128], BF16) for i in range(2)]
    WTt, ADTt, Kdt, qgt, gTt = BT("WT"), BT("ADT"), BT("Kd"), BT("qg"), BT("gT")
    Ut = [P.sb("g2U%d" % i, [64, 8, 2, 128], F32) for i in range(2)]
    vP = [P.sb("g2vP%d" % i, [128, 8, 128], BF16) for i in range(2)]
    for hh in range(2):
        mset(P, "pool", [vP[hh]], vP[hh][:, :, :], 0.0)
    Sf, Sb = P.sb("g2Sf", [128, 8, 128], F32), P.sb("g2Sb", [128, 8, 128], BF16)
    mset(P, "pool", [Sf], Sf[:, :, :], 0.0)
    mset(P, "pool", [Sb], Sb[:, :, :], 0.0)
    Otm = [P.sb("g2O%d" % i, [128, 8, 128], F32) for i in range(2)]
    sq = P.sb("g2sq", [128, 8, 128], F32)
    onb = P.sb("g2onb", [128, 8, 128], BF16)
    oTt = [P.sb("g2oT%d" % i, [128, 8, 128], BF16) for i in range(2)]
    ss8, rs8 = P.sb("g2ss", [128, 8], F32), P.sb("g2rs", [128, 8], F32)
    ng = P.sb("gng", [128, 1], F32)
    P.load(ng, ng[:, :], Dg["norm_g"], Dg["norm_g"][:, :])
    v3 = lambda b: b[:, :, :]
    p3 = lambda ps, n=128: ps[0:n, 0:1024].rearrange("p (a t) -> p a t", a=8)
    for t in range(NT):
        b = t % 2
        tsl = slice(t * 128, (t + 1) * 128)
        P.load(WTt[b], v3(WTt[b]), G.sWT, G.sWT[:, t].rearrange("h p d -> p h d"))
        P.load(ADTt[b], v3(ADTt[b]), G.sADT, G.sADT[:, t].rearrange("h p d -> p h d"))
        P.load(Kdt[b], v3(Kdt[b]), G.sKd, G.sKd[:, t].rearrange("h p d -> p h d"))
        P.load(qgt[b], v3(qgt[b]), G.qg_d, G.qg_d[:, :, tsl].rearrange("h p d -> p h d"))
        P.load(gTt[b], v3(gTt[b]), G.gT_d, G.gT_d[:, :, tsl].rearrange("h p d -> p h d"))
        for hh in range(2):
            P.load(Ut[b], Ut[b][:, :, hh, :], G.sU, G.sU[:, 2 * t + hh].rearrange("h p d -> p h d"))
        for hh in range(2):
            c = 2 * t + hh
            isl = slice(hh * 64, (hh + 1) * 64)
            p3_, p2 = C.psA
            for hf in range(2):
                p1 = C.psB[hf]
                for h4 in range(4):
                    h = 4 * hf + h4
                    mm(P, [p1], p1[0:64, h4 * 128:(h4 + 1) * 128], WTt[b][:, h, isl], Sb[:, h, :], [WTt[b], Sb])
                tt(P, "dve", [(vP[hh], hf)], vP[hh][isl, 4 * hf:4 * hf + 4, :], Ut[b][:, 4 * hf:4 * hf + 4, hh, :],
                   p1[0:64, :].rearrange("p (a t) -> p a t", a=4), ALU.subtract, [Ut[b], p1])
            for h in range(8):
                mm(P, [p2], p2[0:64, h * 128:(h + 1) * 128], qgt[b][:, h, isl], Sb[:, h, :], [qgt[b], Sb], start=True, stop=False)
                mm(P, [p2], p2[0:64, h * 128:(h + 1) * 128], ADTt[b][:, h, isl], vP[hh][:, h, :], [ADTt[b], vP[hh]],
                   start=False, stop=True)
            act(P, [(Otm[b], hh)], Otm[b][isl, :, :], p3(p2, 64), AF.Copy, [p2])
            for h in range(8):
                mm(P, [p3_], p3_[:, h * 128:(h + 1) * 128], Kdt[b][:, h, :], vP[hh][:, h, :], [Kdt[b], vP[hh]])
            tt(P, "pool", [Sf], v3(Sf), v3(Sf), G.DL[:, :, c:c + 1].to_broadcast([128, 8, 128]), ALU.mult, [Sf, G.DL])
            tt(P, "dve", [Sb], v3(Sb), v3(Sf), p3(p3_), ALU.add, [Sf, p3_])
            tt(P, "dve", [Sf], v3(Sf), v3(Sf), p3(p3_), ALU.add, [Sf, p3_])
        tt(P, "pool", [sq], v3(sq), v3(Otm[b]), v3(Otm[b]), ALU.mult, [Otm[b]])
        P.op("dve", lambda e, sq=sq: e.reduce_sum(ss8[:, :], sq[:, :, :], AX.X), [sq], [ss8])
        rstd_from_ss(P, C, ss8, ss8[:, :], rs8, rs8[:, :], 128)
        tt(P, "dve", [onb], v3(onb), v3(Otm[b]), rs8[:, :].unsqueeze(2).to_broadcast([128, 8, 128]), ALU.mult, [Otm[b], rs8])
        pt = C.psTs[b]
        for h in range(8):
            tr(P, [pt], pt[:, h * 128:(h + 1) * 128], onb[:, h, :], C.ident[:, :], [onb, C.ident])
        stt(P, "dve", [oTt[b]], v3(oTt[b]), pt[:, 0:1024].rearrange("p (a t) -> p a t", a=8), ng[:, 0:1], v3(gTt[b]),
            ALU.mult, ALU.mult, [pt, ng, gTt[b]])
        P.store(C.oT_d, C.oT_d[:, :, tsl].rearrange("h p d -> p h d"), oTt[b], v3(oTt[b]))
    P.pop_scope()


def stage_gdn(P, C, w_in_d, Dg):
    P.push_scope()
    G = Ctx()
    G.R0, G.R1 = P.sb("gR0", [8, S], F32), P.sb("gR1", [8, S], F32)
    G.cols = P.sb("gcols", [128, NT, 4, 8], F32)
    G.DL = P.sb("gDL", [128, 8, 64], F32)
    G.sel8 = P.sb("gsel8", [8, 8, 128], F32)
    P.load(G.sel8, G.sel8[:, :, :], Dg["sel8"], Dg["sel8"][:, :, :])
    G.qT_d, G.kT_d, G.vT_d, G.gT_d, G.qg_d = C.gdn_scr
    G.sWT, G.sADT, G.sKd = C.gdn_scr2
    G.sU = C.gdn_scrU
    stage_gdn_proj(P, C, w_in_d, Dg, G)
    if C.upto != "l1a":
        stage_gdn_rows(P, C, Dg, G)
        if C.upto != "l1b":
            stage_gdn_main(P, C, Dg, G)
            if C.upto != "l1c":
                stage_gdn_rec(P, C, Dg, G)
    P.pop_scope()


def t5_bucket_np(dist, buckets=32, max_dist=2048):
    dist = np.maximum(dist, 0)
    max_exact = buckets // 2
    large = max_exact + (np.log(np.maximum(dist, 1) / max_exact)
                         / math.log(max_dist / max_exact) * (buckets - max_exact)).astype(np.int32)
    large = np.minimum(large, buckets - 1)
    return np.where(dist < max_exact, dist, large).astype(np.int32)


def attn_tables(rel_bias):
    k = np.arange(128)[:, None]
    q = np.arange(128)[None, :]
    mask = np.zeros((128, 2, 128), np.float32)
    mask[:, 0, :] = (k >= q)
    mask[:, 1, :] = (q >= k)
    idx = np.zeros((3, 128, 2, 128), np.int64)
    for ci, (dil, nb) in enumerate(CFGS):
        idx[ci, :, 0, :] = t5_bucket_np(np.clip(q + 128 - k, 0, 128) * dil)
        idx[ci, :, 1, :] = t5_bucket_np(np.clip(q - k, 0, 128) * dil)
    G = rel_bias[idx]
    G = np.ascontiguousarray(np.transpose(G, (4, 1, 0, 2, 3))).reshape(8, 128, 768)
    return G.astype(np.float32), mask.reshape(128, 256)


def build(upto="all"):
    nc = bass.Bass("TRN2", target_bir_lowering=False)
    P = Prog(nc)
    C = Ctx()
    C.upto = upto
    dbg = upto != "all"
    C.x = P.dram("x", [S, D], F32, kind="ExternalInput")
    C.out = P.dram("out", [S, D], F32, kind="ExternalOutput")
    C.x1 = P.dram("x1", [S, D], F32, kind=("ExternalOutput" if dbg else "Internal"))
    setup(P, C)
    if dbg:
        C.oT_d = P.dram("dbg_oT", [8, 128, S], BF16, kind="ExternalOutput")
        C.dbg_h = P.dram("dbg_h", [8, 128, S], BF16, kind="ExternalOutput")
    C.w_in0 = P.dram("w_in0", [1024, 3072], F32, kind="ExternalInput")
    C.w_out0 = P.dram("w_out0", [1024, 1024], F32, kind="ExternalInput")
    C.w_in1 = P.dram("w_in1", [1024, 4112], F32, kind="ExternalInput")
    C.w_out1 = P.dram("w_out1", [1024, 1024], F32, kind="ExternalInput")
    C.Gd = P.dram("attn_G", [8, 128, 768], F32, kind="ExternalInput")
    C.maskd = P.dram("attn_mask", [128, 256], F32, kind="ExternalInput")
    C.uT_d = P.dram("uT_d", [4, 128, S], F32)
    C.gB_d = P.dram("gB_d", [4, 128, S], BF16)
    C.gdn_scr = [P.dram("gdn_scr%d" % i, [8, 128, S], BF16) for i in range(5)]
    C.gdn_scr2 = [P.dram("gdn_scrB%d" % i, [8, NT, 128, 128], BF16) for i in range(3)]
    C.gdn_scrU = P.dram("gdn_scrU", [8, 2 * NT, 64, 128], F32)
    Dm = {}
    for nm, shp in (("ar", [128, 16]), ("ai", [128, 16]), ("ldt", [128, 16]), ("b_re", [128, 16, 16]), ("b_im", [128, 16, 16]),
                    ("cT_re", [128, 16, 16]), ("cT_im", [128, 16, 16]), ("d_fm", [128, 4]), ("glu_b_fm", [128, 4]),
                    ("glu_w", [512, 512])):
        Dm[nm] = P.dram("s5_" + nm, shp, F32, kind="ExternalInput")
    Dg = {}
    for nm, shp in (("conv_fm", [128, 24, 4]), ("a_log", [8, 1]), ("dt_bias", [8, 1]), ("norm_g", [128, 1]),
                    ("maskadd", [128, 128]), ("strict01", [128, 128]), ("sel8", [8, 8, 128])):
        Dg[nm] = P.dram("gdn_" + nm, shp, F32, kind="ExternalInput")

    def layer0():
        stage_adaln(P, C, 0)
        if upto == "ada":
            C.dbg_m = P.dram("dbg_m", [128, 24 + 8 + 1024], F32, kind="ExternalOutput")
            P.store(C.dbg_m, C.dbg_m[:, 0:24], C.modfm, C.modfm[:, :], eng="sp")
            P.store(C.dbg_m, C.dbg_m[:, 24:32], C.Asc, C.Asc[:, :], eng="sp")
            P.store(C.dbg_m, C.dbg_m[:, 32:1056], C.GP, C.GP[:, :], eng="sp")
            return
        if upto == "all":
            stage_adaln(P, C, 1, defer_pop=True)
        stage_prenorm(P, C, C.x, 0)
        if upto == "all":
            P.pop_scope()
        if dbg:
            for k in range(8):
                P.store(C.dbg_h, C.dbg_h[k], C.big, C.big[:, k, :], eng="sp")
        if upto == "pre":
            return
        if upto != "s5":
            stage_attn(P, C, C.w_in0, C.Gd, C.maskd)
        if upto != "attn":
            stage_s5_proj(P, C, C.w_in0)
            stage_s5(P, C, Dm)
        if upto in ("attn", "s5"):
            return
        stage_out(P, C, C.w_out0, C.x, C.x1, 0)

    def layer1(xin, xout):
        if upto != "all":
            stage_adaln(P, C, 1)
        stage_prenorm(P, C, xin, 1)
        if dbg:
            for k in range(8):
                P.store(C.dbg_h, C.dbg_h[k], C.big, C.big[:, k, :], eng="sp")
        stage_gdn(P, C, C.w_in1, Dg)
        if upto in ("l1a", "l1b", "l1c"):
            return
        stage_out(P, C, C.w_out1, xin, xout, 1)

    if upto in ("l1", "l1a", "l1b", "l1c"):
        layer1(C.x, C.x1)
    else:
        layer0()
        if upto == "all":
            layer1(C.x1, C.out)
    P.emit()
    return nc, P


def host_inputs(inputs, b):
    f32 = np.float32
    G, mask = attn_tables(np.asarray(inputs["rel_bias"], f32))
    m = {
        "x": np.ascontiguousarray(inputs["x"][b], f32),
        "c_fm": np.ascontiguousarray(np.asarray(inputs["c"][b], f32).reshape(8, 128).T),
        "ada_w": np.ascontiguousarray(inputs["ada_w"], f32),
        "ada_b_fm": np.ascontiguousarray(np.transpose(np.asarray(inputs["ada_b"], f32).reshape(2, 24, 128), (0, 2, 1))),
        "ada_b_g": np.ascontiguousarray(np.asarray(inputs["ada_b"], f32)[:, 2048:3072].reshape(2, 1, 1024)),
        "pre_g_fm": np.ascontiguousarray(np.transpose(np.asarray(inputs["pre_g"], f32).reshape(2, 8, 128), (0, 2, 1))),
        "post_g_row": np.ascontiguousarray(np.asarray(inputs["post_g"], f32).reshape(2, 1, 1024)),
        "ident_in": np.eye(128).astype(ml_dtypes.bfloat16),
        "identf_in": np.eye(128).astype(f32),
        "w_in0": np.ascontiguousarray(inputs["ab_w_in"][0], f32),
        "w_out0": np.ascontiguousarray(inputs["ab_w_out"][0], f32),
        "attn_G": G, "attn_mask": mask,
    }

    def pair_layout(a):
        a = np.asarray(a, f32)
        a = a.reshape((16, 2, 64) + a.shape[2:])
        return np.ascontiguousarray(np.moveaxis(a, 0, 2).reshape((128, 16) + a.shape[3:]))
    m["s5_ar"] = pair_layout(inputs["s5_a_re"][0])
    m["s5_ai"] = pair_layout(inputs["s5_a_im"][0])
    m["s5_ldt"] = pair_layout(np.broadcast_to(np.asarray(inputs["s5_log_dt"][0], f32)[:, None], (32, 64)))
    m["s5_b_re"] = pair_layout(inputs["s5_b_re"][0])
    m["s5_b_im"] = pair_layout(inputs["s5_b_im"][0])
    m["s5_cT_re"] = pair_layout(np.transpose(np.asarray(inputs["s5_c_re"][0], f32), (0, 2, 1)))
    m["s5_cT_im"] = pair_layout(np.transpose(np.asarray(inputs["s5_c_im"][0], f32), (0, 2, 1)))
    m["s5_d_fm"] = np.ascontiguousarray(np.asarray(inputs["s5_d"][0], f32).reshape(4, 128).T)
    m["s5_glu_b_fm"] = np.ascontiguousarray(np.asarray(inputs["s5_glu_b"][0], f32).reshape(4, 128).T)
    m["s5_glu_w"] = np.ascontiguousarray(inputs["s5_glu_w"][0], f32)
    m["w_in1"] = np.ascontiguousarray(inputs["gdn_w_in"][0], f32)
    m["w_out1"] = np.ascontiguousarray(inputs["gdn_w_out"][0], f32)
    m["gdn_conv_fm"] = np.ascontiguousarray(np.transpose(np.asarray(inputs["gdn_conv"][0], f32).reshape(4, 24, 128), (2, 1, 0)))
    m["gdn_a_log"] = np.ascontiguousarray(np.asarray(inputs["gdn_a_log"][0], f32).reshape(8, 1))
    m["gdn_dt_bias"] = np.ascontiguousarray(np.asarray(inputs["gdn_dt_bias"][0], f32).reshape(8, 1))
    m["gdn_norm_g"] = np.ascontiguousarray(np.asarray(inputs["gdn_norm_g"][0], f32).reshape(128, 1))
    ii = np.arange(128)[:, None]
    jj = np.arange(128)[None, :]
    same = (ii // 64) == (jj // 64)
    m["gdn_maskadd"] = np.where(same & (ii >= jj), 0.0, -30000.0).astype(f32)
    m["gdn_strict01"] = (same & (ii > jj)).astype(f32)
    sel = np.zeros((8, 8, 128), f32)
    for h in range(8):
        sel[h, h, :] = 1.0
    m["gdn_sel8"] = sel
    return m


_PROG = {}


def kernel(**inputs):
    if "nc" not in _PROG:
        _PROG["nc"] = build("all")[0]
    nc = _PROG["nc"]
    nb = int(np.asarray(inputs["x"]).shape[0])
    maps = [host_inputs(inputs, b) for b in range(nb)]
    in_maps = [maps[i % nb] for i in range(8)]
    res = run_bass_kernel_spmd(nc, in_maps, core_ids=list(range(8)))
    out = np.stack([np.asarray(res.results[b]["out"], dtype=np.float32) for b in range(nb)], axis=0)
    return out
```

```python
import numpy as np
import concourse.bass as bass
import concourse.mybir as mybir
from concourse.bass_utils import run_bass_kernel_spmd

F32 = mybir.dt.float32
BF16 = mybir.dt.bfloat16
AF = mybir.ActivationFunctionType
ALU = mybir.AluOpType
AX = mybir.AxisListType

ENGS = ("pe", "act", "dve", "pool", "sp")
EPOCH = 16000


class Buf:
    _n = 0

    def __init__(self, name, t, kind):
        self.name = name
        self.t = t
        self.kind = kind
        Buf._n += 1
        self.id = Buf._n
        self.wr = {}
        self.pslast = {}
        self.rd = {}
        self.slot = None

    def __getitem__(self, idx):
        return self.t[idx]


class SemSlot:
    def __init__(self):
        self.handle = None
        self.count = 0
        self.last = None


class Op:
    __slots__ = ("eng", "fn", "reads", "writes", "deps", "is_dma", "dbuf", "sig", "sigval", "idx", "dmaval", "busy", "lat", "soft", "seg")

    def __init__(self, eng, fn, reads, writes, is_dma=False, dbuf=None):
        self.eng = eng
        self.fn = fn
        self.reads = reads
        self.writes = writes
        self.deps = set()
        self.is_dma = is_dma
        self.dbuf = dbuf
        self.sig = False
        self.sigval = None
        self.dmaval = None
        self.busy = 0.3
        self.lat = 0.3
        self.soft = set()


class PsView:
    def __init__(self, base, ap):
        self.base = base
        self.ap = ap

    def __getitem__(self, idx):
        return self.ap[idx]


def _acc(x):
    if isinstance(x, PsView):
        return (x.base, None)
    if isinstance(x, Buf):
        return (x, None)
    if isinstance(x[0], PsView):
        return (x[0].base, x[1])
    return x


class Prog:
    def __init__(self, nc):
        self.nc = nc
        self.ops = []
        self.bufs = []
        self.final_waits = []
        self.slots = []
        self.free_slots = []
        self.fence_deps = set()
        self.last_of_eng = {}
        self.scopes = []
        self.seg = 0

    def sb(self, name, shape, dtype=F32):
        if self.scopes:
            g = self.nc.sbuf_tensor(name + "_s%d" % len(self.bufs), list(shape), dtype)
            t = g.__enter__()
            b = Buf(name, t, "sb")
            self.scopes[-1].append((g, b))
        else:
            b = Buf(name, self.nc.alloc_sbuf_tensor(name, list(shape), dtype), "sb")
        self.bufs.append(b)
        return b

    def push_scope(self):
        self.scopes.append([])

    def pop_scope(self):
        self.fence()
        sc = self.scopes.pop()
        for (g, b) in reversed(sc):
            g.__exit__(None, None, None)
            if b.slot is not None:
                self.free_slots.append(b.slot)
                b.slot = None

    def fence(self):
        deps = set(self.last_of_eng.values())
        for sl in self.slots:
            if sl.last is not None:
                deps.add(sl.last)
        self.fence_deps = deps
        self.seg += 1

    def ps(self, name, shape, dtype=F32):
        b = Buf(name, self.nc.alloc_psum_tensor(name, list(shape), dtype), "ps")
        self.bufs.append(b)
        return b

    def dram(self, name, shape, dtype=F32, kind="Internal"):
        t = self.nc.dram_tensor(name, list(shape), dtype, kind=kind)
        b = Buf(name, t.ap(), "dr")
        self.bufs.append(b)
        return b

    def _overlap_w(self, b, k):
        if k is None:
            return list(b.wr.values())
        r = []
        if k in b.wr:
            r.append(b.wr[k])
        if None in b.wr:
            r.append(b.wr[None])
        return r

    def _overlap_r(self, b, k):
        if k is None:
            r = []
            for v in b.rd.values():
                r.extend(v)
            return r
        return list(b.rd.get(k, [])) + list(b.rd.get(None, []))

    def op(self, eng, fn, reads=(), writes=(), is_dma=False, dbuf=None, cost=None):
        o = Op(eng, fn, [_acc(x) for x in reads], [_acc(x) for x in writes], is_dma, dbuf)
        if cost is not None:
            o.busy, o.lat = cost
        o.idx = len(self.ops)
        o.seg = self.seg
        o.deps.update(self.fence_deps)
        self.last_of_eng[eng] = o.idx
        if is_dma:
            if dbuf.slot is None:
                if self.free_slots:
                    dbuf.slot = self.free_slots.pop()
                else:
                    dbuf.slot = SemSlot()
                    self.slots.append(dbuf.slot)
            sl = dbuf.slot
            sl.count += 16
            sl.last = o.idx
            o.dmaval = (sl, sl.count)
            sbacc = (dbuf, None)
            o.writes = [w for w in o.writes if w[0] is not dbuf] + [sbacc]
            o.reads = [r for r in o.reads if r[0] is not dbuf]
        psb = set(b for (b, k) in o.reads + o.writes if b.kind == "ps")
        o.reads = [(b, k) for (b, k) in o.reads if b.kind != "ps"]
        o.writes = [(b, k) for (b, k) in o.writes if b.kind != "ps"]
        for b in psb:
            for en, ix in b.pslast.items():
                if en != eng:
                    o.deps.add(ix)
                else:
                    o.soft.add(ix)
            b.pslast[eng] = o.idx
        for (b, k) in o.reads:
            o.deps.update(self._overlap_w(b, k))
        for (b, k) in o.writes:
            o.deps.update(self._overlap_w(b, k))
            o.deps.update(self._overlap_r(b, k))
        for (b, k) in o.reads:
            b.rd.setdefault(k, []).append(o.idx)
        for (b, k) in o.writes:
            if k is None:
                b.wr = {None: o.idx}
                b.rd = {}
            else:
                b.wr[k] = o.idx
                b.rd[k] = []
        o.deps.discard(o.idx)
        self.ops.append(o)
        return o

    def dma(self, out_ap, in_ap, sbuf, reads=(), writes=(), eng="sp", **kw):
        def fn(e, out_ap=out_ap, in_ap=in_ap, kw=kw):
            return e.dma_start(out=out_ap, in_=in_ap, **kw)
        n = 1
        for d_ in out_ap.shape:
            n *= int(d_)
        nbytes = n * (2 if out_ap.dtype == BF16 else 4)
        return self.op(eng, fn, reads, writes, is_dma=True, dbuf=sbuf, cost=(0.15, 2.0 + nbytes / 150e3))

    def schedule(self, window=48):
        ops = self.ops
        n = len(ops)
        segs = {}
        for o in ops:
            segs.setdefault(o.seg, []).append(o.idx)
        done = [False] * n
        fin = [0.0] * n
        free = {e: 0.0 for e in ENGS}
        order = []
        last_sched = {}
        for sg in sorted(segs):
            idxs = segs[sg]
            extra = set(last_sched.values())
            per = {e: [] for e in ENGS}
            for ix in idxs:
                ops[ix].deps.update(extra)
                per[ops[ix].eng].append(ix)
            alldeps = {ix: list(ops[ix].deps | ops[ix].soft) for ix in idxs}
            ptr = {e: 0 for e in ENGS}
            remaining = len(idxs)
            while remaining:
                best = None
                for e in ENGS:
                    lst = per[e]
                    p = ptr[e]
                    while p < len(lst) and done[lst[p]]:
                        p += 1
                    ptr[e] = p
                    cnt = 0
                    q = p
                    wnd = window * 8 if e == "pe" else window
                    while q < len(lst) and cnt < wnd:
                        ix = lst[q]
                        q += 1
                        if done[ix]:
                            continue
                        cnt += 1
                        ok = True
                        rdy = 0.0
                        for d in alldeps[ix]:
                            if not done[d]:
                                ok = False
                                break
                            t = fin[d] + (0.3 if ops[d].eng != e else 0.05)
                            if t > rdy:
                                rdy = t
                        if not ok:
                            continue
                        st = rdy if rdy > free[e] else free[e]
                        key = (st, ix)
                        if best is None or key < best[0]:
                            best = (key, e, ix)
                        if st <= free[e]:
                            break
                assert best is not None, "scheduler stuck"
                (st, _), e, ix = best
                done[ix] = True
                fin[ix] = st + ops[ix].lat
                free[e] = st + ops[ix].busy
                order.append(ix)
                last_sched[e] = ix
                remaining -= 1
        self.sim_time = max(fin) if fin else 0.0
        return order

    def load(self, sbuf, sb_ap, dr_buf, dr_ap, eng="sp", key=None, **kw):
        return self.dma(sb_ap, dr_ap, sbuf, reads=[(dr_buf, key)], writes=[sbuf], eng=eng, **kw)

    def store(self, dr_buf, dr_ap, sbuf, sb_ap, eng="act", key=None, **kw):
        return self.dma(dr_ap, sb_ap, sbuf, reads=[sbuf], writes=[(dr_buf, key)], eng=eng, **kw)

    def emit(self):
        nc = self.nc
        ops = self.ops
        order = self.schedule() if getattr(self, "reorder", True) else list(range(len(ops)))
        for o in ops:
            for d in o.deps:
                ops[d].sig = True
        cnt = {e: 0 for e in ENGS}
        sems = {e: [] for e in ENGS}
        oops = [ops[i] for i in order]
        for o in oops:
            if o.is_dma:
                sl = o.dmaval[0]
                if sl.handle is None:
                    sl.handle = nc.alloc_semaphore("dq_%d" % self.slots.index(sl))
            elif o.sig:
                c = cnt[o.eng]
                ep = c // EPOCH
                while len(sems[o.eng]) <= ep:
                    sems[o.eng].append(nc.alloc_semaphore("s_%s_%d" % (o.eng, len(sems[o.eng]))))
                o.sigval = (sems[o.eng][ep], c % EPOCH + 1, ep)
                cnt[o.eng] = c + 1
        engobj = {"pe": nc.tensor, "act": nc.scalar, "dve": nc.vector, "pool": nc.gpsimd, "sp": nc.sync}
        self.nwaits = 0

        def run_engine(ename, e):
            waited = {}
            for o in oops:
                if o.eng != ename:
                    continue
                need = {}
                for d in o.deps:
                    p = ops[d]
                    if p.is_dma:
                        s, v = p.dmaval[0].handle, p.dmaval[1]
                        key = ("d", id(p.dmaval[0]))
                    else:
                        s, v, ep = p.sigval
                        key = (p.eng, ep)
                    if key not in need or need[key][1] < v:
                        need[key] = (s, v)
                for key, (s, v) in need.items():
                    if key[0] != "d":
                        newer = [k2 for k2 in need if k2[0] == key[0] and k2[1] > key[1]]
                        if newer:
                            continue
                        w_ep = waited.get(("ep", key[0]), -1)
                        if w_ep > key[1]:
                            continue
                    if waited.get(key, 0) >= v:
                        continue
                    e.wait_ge(s, v)
                    self.nwaits += 1
                    waited[key] = v
                    if key[0] != "d":
                        waited[("ep", key[0])] = max(waited.get(("ep", key[0]), -1), key[1])
                ins = o.fn(e)
                if o.is_dma:
                    ins.then_inc(o.dmaval[0].handle, 16)
                elif o.sig:
                    ins.then_inc(o.sigval[0], 1)
            for (en, s, v) in self.final_waits:
                if en == ename:
                    e.wait_ge(s, v)

        for sl in self.slots:
            if sl.handle is not None:
                self.final_waits.append(("sp", sl.handle, sl.count))
        with nc.Block() as block:
            @block.tensor
            def _(e):
                run_engine("pe", e)

            @block.scalar
            def _(e):
                run_engine("act", e)

            @block.vector
            def _(e):
                run_engine("dve", e)

            @block.gpsimd
            def _(e):
                run_engine("pool", e)

            @block.sync
            def _(e):
                run_engine("sp", e)
        return nc


import math
import ml_dtypes

S = 4096
D = 1024
NT = S // 128
EPS = 1e-6
CFGS = ((1, 32), (4, 8), (16, 2))


def _fd(ap):
    n = 1
    for d_ in ap.shape[1:]:
        n *= int(d_)
    return n


def _ecost(eng, ap):
    fd = _fd(ap)
    if eng == "act":
        c = 0.2 + fd / 1200.0
    elif eng == "dve":
        c = 0.08 + fd / 960.0
    else:
        c = 0.12 + fd / 450.0
    return (c, c)


def mm(P, wr, out, lhsT, rhs, rd, start=True, stop=True):
    c = max(0.03, _fd(rhs) / 2400.0 * (4.0 if lhsT.dtype == F32 else 1.0)) + 0.01
    return P.op("pe", lambda e: e.matmul(out, lhsT, rhs, start=start, stop=stop), rd, wr, cost=(c, c + 0.1))


def tr(P, wr, out, in_, ident, rd):
    c = 0.28 if in_.dtype == F32 else 0.07
    return P.op("pe", lambda e: e.transpose(out, in_, ident), rd, wr, cost=(c, c + 0.1))


def act(P, wr, out, in_, func, rd, **kw):
    return P.op("act", lambda e: e.activation(out, in_, func, **kw), rd, wr, cost=_ecost("act", out))


def tt(P, eng, wr, out, in0, in1, op, rd):
    return P.op(eng, lambda e: e.tensor_tensor(out, in0, in1, op), rd, wr, cost=_ecost(eng, out))


def ts(P, eng, wr, out, in0, s1, s2, op0, rd, op1=None):
    if op1 is None:
        return P.op(eng, lambda e: e.tensor_scalar(out, in0, s1, None, op0), rd, wr, cost=_ecost(eng, out))
    return P.op(eng, lambda e: e.tensor_scalar(out, in0, s1, s2, op0, op1), rd, wr, cost=_ecost(eng, out))


def stt(P, eng, wr, out, in0, scalar, in1, op0, op1, rd):
    return P.op(eng, lambda e: e.scalar_tensor_tensor(out, in0, scalar, in1, op0, op1), rd, wr, cost=_ecost(eng, out))


def cp(P, eng, wr, out, in_, rd):
    return P.op(eng, lambda e: e.tensor_copy(out, in_), rd, wr, cost=_ecost(eng, out))


def mset(P, eng, wr, ap, val):
    return P.op(eng, lambda e: e.memset(ap, val), [], wr, cost=_ecost(eng, ap))


class Ctx:
    pass


def rstd_from_ss(P, C, ss_buf, ss_ap, out_buf, out_ap, n):
    act(P, [out_buf], out_ap, ss_ap, AF.Ln, [ss_buf, C.epsb], bias=C.epsb[0:ss_ap.shape[0], 0:1], scale=1.0 / n)
    act(P, [out_buf], out_ap, out_ap, AF.Exp, [out_buf], scale=-0.5)


def setup(P, C):
    C.ident = P.sb("ident", [128, 128], BF16)
    idd = P.dram("ident_in", [128, 128], BF16, kind="ExternalInput")
    P.load(C.ident, C.ident[:, :], idd, idd[:, :])
    C.identf = P.sb("identf", [128, 128], F32)
    iddf = P.dram("identf_in", [128, 128], F32, kind="ExternalInput")
    P.load(C.identf, C.identf[:, :], iddf, iddf[:, :])
    C.ones_f = P.sb("ones_f", [128, 128], F32)
    mset(P, "pool", [C.ones_f], C.ones_f[:, :], 1.0)
    C.ones_b = P.sb("ones_b", [128, 128], BF16)
    mset(P, "pool", [C.ones_b], C.ones_b[:, :], 1.0)
    C.epsb = P.sb("epsb", [128, 1], F32)
    mset(P, "pool", [C.epsb], C.epsb[:, :], EPS)
    C.psA = [P.ps("psA%d" % i, [128, 1024], F32) for i in range(2)]
    C.psB = [P.ps("psB%d" % i, [128, 512], F32) for i in range(2)]
    C.psTf = [P.ps("psT%d" % i, [128, 512], F32) for i in range(2)]
    C.psTs = [PsView(b, b.t[:, :].bitcast(BF16)) for b in C.psTf]
    C.psT = C.psTs[0]
    C.big = P.sb("big", [128, 8, S], BF16)
    C.oT_d = P.dram("oT_d", [8, 128, S], BF16)
    C.cfm = P.sb("cfm", [128, 8], F32)
    cd = P.dram("c_fm", [128, 8], F32, kind="ExternalInput")
    P.load(C.cfm, C.cfm[:, :], cd, cd[:, :])
    C.cact = P.sb("cact", [128, 8], F32)
    act(P, [C.cact], C.cact[:, :], C.cfm[:, :], AF.Silu, [C.cfm])
    C.ada_w = P.dram("ada_w", [2, 1024, 3072], F32, kind="ExternalInput")
    C.ada_b_fm = P.dram("ada_b_fm", [2, 128, 24], F32, kind="ExternalInput")
    C.ada_b_g = P.dram("ada_b_g", [2, 1, 1024], F32, kind="ExternalInput")
    C.pre_g_fm = P.dram("pre_g_fm", [2, 128, 8], F32, kind="ExternalInput")
    C.post_g_row = P.dram("post_g_row", [2, 1, 1024], F32, kind="ExternalInput")
    C.wstage = [P.sb("wstage%d" % i, [128, 8, 128], F32) for i in range(2)]
    C.wsi = 0
    C.ssq = P.sb("ssq", [128, 2 * NT], F32)
    C.rstd = P.sb("rstd", [128, 2 * NT], F32)
    C.modfm_l = [P.sb("modfm%d" % i, [128, 24], F32) for i in range(2)]
    C.Asc_l = [P.sb("Asc%d" % i, [128, 8], F32) for i in range(2)]
    C.GP_l = [P.sb("GP%d" % i, [128, 1024], F32) for i in range(2)]
    C.grow_l = [P.sb("grow%d" % i, [1, 1024], F32) for i in range(2)]
    C.modfm, C.Asc, C.GP, C.grow = C.modfm_l[0], C.Asc_l[0], C.GP_l[0], C.grow_l[0]
    C.tmpv = P.sb("tmpv", [128, 24], F32)
    C.trow = P.sb("trow", [1, 1024], F32)


def load_w_bf16(P, C, dst, dst_ap, wd, w_ap, ncols):
    st = C.wstage[C.wsi % 2]
    C.wsi += 1
    P.load(st, st[:, :, 0:ncols], wd, w_ap.rearrange("(k p) c -> p k c", p=128))
    eng = "pool" if (C.wsi % 2) else "dve"
    cp(P, eng, [dst], dst_ap, st[:, :, 0:ncols], [st])


def stage_adaln(P, C, layer, defer_pop=False):
    C.modfm, C.Asc, C.GP, C.grow = C.modfm_l[layer], C.Asc_l[layer], C.GP_l[layer], C.grow_l[layer]
    P.push_scope()
    awst = [P.sb("awst%d" % i, [128, 3072], F32) for i in range(2)]
    psF = C.psB[0]
    psG = C.psA[0]
    P.load(C.modfm, C.modfm[:, :], C.ada_b_fm, C.ada_b_fm[layer])
    P.load(C.grow, C.grow[:, :], C.ada_b_g, C.ada_b_g[layer])
    for k in range(8):
        st = awst[k % 2]
        P.load(st, st[:, :], C.ada_w, C.ada_w[layer, k * 128:(k + 1) * 128, :])
        for j in range(24):
            mm(P, [(psF, j)], psF[:, j:j + 1], st[:, j * 128:(j + 1) * 128], C.cact[:, k:k + 1], [st, C.cact])
        tt(P, "dve", [C.modfm], C.modfm[:, :], psF[:, 0:24], C.modfm[:, :], ALU.add, [psF, C.modfm])
        for cb in range(2):
            mm(P, [(psG, cb)], psG[0:1, cb * 512:(cb + 1) * 512], C.cact[:, k:k + 1],
               st[:, 2048 + cb * 512: 2048 + (cb + 1) * 512], [st, C.cact])
        tt(P, "dve", [C.grow], C.grow[:, :], psG[0:1, 0:1024], C.grow[:, :], ALU.add, [psG, C.grow])
    P.load(C.tmpv, C.tmpv[:, 0:8], C.pre_g_fm, C.pre_g_fm[layer])
    stt(P, "dve", [C.Asc], C.Asc[:, :], C.modfm[:, 8:16], 1.0, C.tmpv[:, 0:8], ALU.add, ALU.mult, [C.modfm, C.tmpv])
    P.load(C.trow, C.trow[:, :], C.post_g_row, C.post_g_row[layer])
    tt(P, "dve", [C.grow], C.grow[:, :], C.grow[:, :], C.trow[:, :], ALU.mult, [C.grow, C.trow])
    for cb in range(2):
        mm(P, [(psG, cb)], psG[:, cb * 512:(cb + 1) * 512], C.ones_f[0:1, :], C.grow[0:1, cb * 512:(cb + 1) * 512],
           [C.ones_f, C.grow])
    cp(P, "dve", [C.GP], C.GP[:, :], psG[:, :], [psG])
    if not defer_pop:
        P.pop_scope()


def stage_prenorm(P, C, xsrc, layer=0):
    C.modfm, C.Asc, C.GP = C.modfm_l[layer], C.Asc_l[layer], C.GP_l[layer]
    mset(P, "pool", [C.ssq], C.ssq[:, :], 0.0)
    P.push_scope()
    C.xt = [P.sb("xt%d" % i, [128, 1024], F32) for i in range(2)]
    C.xn = [P.sb("xn%d" % i, [128, 1024], BF16) for i in range(2)]
    C.junk = P.sb("junk", [128, 1024], BF16)
    for i in range(NT):
        xt = C.xt[i % 2]
        xn = C.xn[i % 2]
        P.load(xt, xt[:, :], xsrc, xsrc[i * 128:(i + 1) * 128, :])
        act(P, [C.junk, (C.ssq, i)], C.junk[:, :], xt[:, :], AF.Square, [xt], accum_out=C.ssq[:, i:i + 1])
        rstd_from_ss(P, C, (C.ssq, i), C.ssq[:, i:i + 1], (C.rstd, i), C.rstd[:, i:i + 1], D)
        ts(P, "dve", [xn], xn[:, :], xt[:, :], C.rstd[:, i:i + 1], None, ALU.mult, [xt, (C.rstd, i)])
        pt = C.psTs[i % 2]
        for j in range(8):
            tr(P, [(pt, j)], pt[:, j * 128:(j + 1) * 128], xn[:, j * 128:(j + 1) * 128], C.ident[:, :], [xn, C.ident])
        for j in range(8):
            ts(P, "dve", [(C.big, (j, i))], C.big[:, j, i * 128:(i + 1) * 128], pt[:, j * 128:(j + 1) * 128],
               C.Asc[:, j:j + 1], C.modfm[:, j:j + 1], ALU.mult, [(pt, j), C.Asc, C.modfm], op1=ALU.add)
    P.pop_scope()


def stage_out(P, C, w_out_d, xsrc, xdst, layer):
    C.GP = C.GP_l[layer]
    P.push_scope()
    wo = P.sb("wout", [128, 8, 1024], BF16)
    C.xt = [P.sb("xt%d" % i, [128, 1024], F32) for i in range(2)]
    C.yt = [P.sb("yt%d" % i, [128, 1024], F32) for i in range(2)]
    C.junk = P.sb("junk", [128, 1024], BF16)
    for k in range(8):
        for cb in range(8):
            st = C.wstage[C.wsi % 2]
            C.wsi += 1
            P.load(st, st[:, 0, :], w_out_d, w_out_d[k * 128:(k + 1) * 128, cb * 128:(cb + 1) * 128])
            cp(P, "pool" if cb % 2 else "dve", [(wo, (k, cb))], wo[:, k, cb * 128:(cb + 1) * 128], st[:, 0, :], [st])
    for k in range(8):
        P.load(C.big, C.big[:, k, :], C.oT_d, C.oT_d[k])
    for i in range(NT):
        py = C.psA[i % 2]
        for cb in range(2):
            for k in range(8):
                mm(P, [(py, cb)], py[:, cb * 512:(cb + 1) * 512], C.big[:, k, i * 128:(i + 1) * 128],
                   wo[:, k, cb * 512:(cb + 1) * 512], [C.big, wo], start=(k == 0), stop=(k == 7))
        col = NT + i
        act(P, [C.junk, (C.ssq, col)], C.junk[:, :], py[:, :], AF.Square, [py], accum_out=C.ssq[:, col:col + 1])
        rstd_from_ss(P, C, (C.ssq, col), C.ssq[:, col:col + 1], (C.rstd, col), C.rstd[:, col:col + 1], D)
        xt = C.xt[i % 2]
        P.load(xt, xt[:, :], xsrc, xsrc[i * 128:(i + 1) * 128, :])
        t = C.yt[i % 2]
        stt(P, "dve", [t], t[:, :], py[:, :], C.rstd[:, col:col + 1], C.GP[:, :], ALU.mult, ALU.mult,
            [py, (C.rstd, col), C.GP])
        tt(P, "pool", [t], t[:, :], t[:, :], xt[:, :], ALU.add, [t, xt])
        P.store(xdst, xdst[i * 128:(i + 1) * 128, :], t, t[:, :])
    P.pop_scope()


def blk_slice(dil, r, n, cnt=1):
    start = n * 128 * dil + r
    return slice(start, start + (128 * cnt - 1) * dil + 1, dil)


def proj_fm(P, C, w, ncols, ps_list, evac):
    hT = C.big
    for tb in range(8):
        tsl = slice(tb * 512, (tb + 1) * 512)
        ps = ps_list[tb % len(ps_list)]
        for k in range(8):
            mm(P, [ps], ps[0:ncols, :], w[:, k, 0:ncols], hT[:, k, tsl], [w, hT], start=(k == 0), stop=(k == 7))
        evac(ps, tb, tsl)


def stage_attn(P, C, w_in_d, Gd, maskd):
    P.push_scope()
    A = Ctx()
    A.w4 = [P.sb("aw%d" % i, [128, 8, 128], BF16) for i in range(4)]
    A.qT = P.sb("qT", [128, S], BF16)
    A.kT = P.sb("kT", [128, S], BF16)
    A.vT = P.sb("vT", [128, S], BF16)
    A.gT = P.sb("gT", [128, S], BF16)
    A.Vd = [P.sb("Vd%d" % i, [128, 32, 128], BF16) for i in range(2)]
    mset(P, "pool", [A.Vd[0]], A.Vd[0][:, :, 64:128], 1.0)
    mset(P, "pool", [A.Vd[1]], A.Vd[1][:, :, 0:64], 1.0)
    A.Gs = P.sb("Gs", [128, 768], F32)
    A.E = [P.sb("E%d" % i, [128, 3, 256], BF16) for i in range(2)]
    A.mask = P.sb("amask", [128, 256], F32)
    A.pex = [P.sb("pex%d" % i, [128, 1024], BF16) for i in range(2)]
    A.PT = [P.sb("PT%d" % i, [128, 1024], BF16) for i in range(2)]
    A.anum = P.sb("anum", [128, S], F32)
    A.aden = P.sb("aden", [128, S], F32)
    A.rden = [P.sb("rden%d" % i, [128, 512], F32) for i in range(2)]
    A.sqb = [P.sb("asqb%d" % i, [128, 512], BF16) for i in range(2)]
    A.blk1 = P.sb("ablk1", [128, 2], BF16)
    mset(P, "pool", [A.blk1], A.blk1[:, :], 0.0)
    mset(P, "pool", [A.blk1], A.blk1[0:64, 0:1], 1.0)
    mset(P, "pool", [A.blk1], A.blk1[64:128, 1:2], 1.0)
    A.mx = P.sb("amx", [2, 16], F32)
    A.m2 = P.sb("am2", [2, 4], F32)
    A.dg2 = P.sb("adg2", [2, 2], F32)
    A.nb = P.sb("anb", [128, 2], F32)
    A.oTc = [P.sb("oTc%d" % i, [128, 512], BF16) for i in range(2)]
    P.load(A.mask, A.mask[:, :], maskd, maskd[:, :])
    for pr in range(4):
        wq, wk, wv, wg = A.w4
        load_w_bf16(P, C, wq, wq[:, :, :], w_in_d, w_in_d[:, pr * 128:(pr + 1) * 128], 128)
        load_w_bf16(P, C, wk, wk[:, :, :], w_in_d, w_in_d[:, 512 + pr * 128: 512 + (pr + 1) * 128], 128)
        load_w_bf16(P, C, wv, wv[:, :, :], w_in_d, w_in_d[:, 1024 + pr * 128: 1024 + (pr + 1) * 128], 128)
        load_w_bf16(P, C, wg, wg[:, :, :], w_in_d, w_in_d[:, 2048 + pr * 128: 2048 + (pr + 1) * 128], 128)
        proj_fm(P, C, wq, 128, C.psB[0:2], lambda ps, tb, tsl: act(P, [(A.qT, tb)], A.qT[:, tsl], ps[:, :], AF.Copy, [ps], scale=0.125))
        proj_fm(P, C, wk, 128, C.psB[0:2], lambda ps, tb, tsl: cp(P, "dve", [(A.kT, tb)], A.kT[:, tsl], ps[:, :], [ps]))
        proj_fm(P, C, wv, 128, C.psB[0:2], lambda ps, tb, tsl: act(P, [(A.vT, tb)], A.vT[:, tsl], ps[:, :], AF.Copy, [ps]))
        proj_fm(P, C, wg, 128, C.psB[0:2], lambda ps, tb, tsl: act(P, [(A.gT, tb)], A.gT[:, tsl], ps[:, :], AF.Silu, [ps]))
        for wi, src in enumerate((A.qT, A.kT)):
            for tb in range(8):
                tsl = slice(tb * 512, (tb + 1) * 512)
                sqb = A.sqb[tb % 2]
                tt(P, "pool", [sqb], sqb[:, :], src[:, tsl], src[:, tsl], ALU.mult, [src])
                ps = C.psB[tb % 2]
                mm(P, [ps], ps[0:2, :], A.blk1[:, :], sqb[:, :], [A.blk1, sqb])
                P.op("dve", lambda e, ps=ps, wi=wi, tb=tb: e.reduce_max(A.mx[0:2, wi * 8 + tb: wi * 8 + tb + 1], ps[0:2, :], AX.X),
                     [ps], [(A.mx, (wi, tb))], cost=(0.7, 0.7))
        P.op("dve", lambda e: e.reduce_max(A.m2[0:2, 0:1], A.mx[0:2, 0:8], AX.X), [A.mx], [(A.m2, 0)], cost=(0.1, 0.1))
        P.op("dve", lambda e: e.reduce_max(A.m2[0:2, 1:2], A.mx[0:2, 8:16], AX.X), [A.mx], [(A.m2, 1)], cost=(0.1, 0.1))
        tt(P, "dve", [(A.m2, 2)], A.m2[0:2, 2:3], A.m2[0:2, 0:1], A.m2[0:2, 1:2], ALU.mult, [(A.m2, 0), (A.m2, 1)])
        act(P, [(A.m2, 3)], A.m2[0:2, 3:4], A.m2[0:2, 2:3], AF.Ln, [(A.m2, 2), C.epsb], bias=C.epsb[0:2, 0:1])
        act(P, [(A.m2, 3)], A.m2[0:2, 3:4], A.m2[0:2, 3:4], AF.Exp, [(A.m2, 3)], scale=0.5)
        ts(P, "dve", [(A.m2, 3)], A.m2[0:2, 3:4], A.m2[0:2, 3:4], -1.0, None, ALU.mult, [(A.m2, 3)])
        ts(P, "dve", [A.dg2], A.dg2[0:2, 0:2], C.identf[0:2, 0:2], A.m2[0:2, 3:4], None, ALU.mult, [C.identf, (A.m2, 3)])
        psn = C.psB[0]
        mm(P, [psn], psn[:, 0:2], C.ones_f[0:2, :], A.dg2[0:2, 0:2], [C.ones_f, A.dg2])
        cp(P, "dve", [A.nb], A.nb[:, :], psn[:, 0:2], [psn])
        for hh in range(2):
            hd = pr * 2 + hh
            P.load(A.Gs, A.Gs[:, :], Gd, Gd[hd])
            act(P, [A.Gs], A.Gs[:, :], A.Gs[:, :], AF.Exp, [A.Gs])
            tt(P, "dve", [A.E[hh]], A.E[hh][:, :, :], A.Gs[:, :].rearrange("p (c m) -> p c m", c=3),
               A.mask[:, :].unsqueeze(1).to_broadcast([128, 3, 256]), ALU.mult, [A.Gs, A.mask])
        gi = 0
        for ci, (dil, nb) in enumerate(CFGS):
            for bg in range(8):
                pt = C.psTs[bg % 2]
                for u in range(4):
                    bi = bg * 4 + u
                    r, n = bi // nb, bi % nb
                    tr(P, [(pt, u)], pt[:, u * 128:(u + 1) * 128], A.vT[:, blk_slice(dil, r, n)], C.ident[:, :],
                       [A.vT, C.ident])
                pv4 = pt[:, 0:512].rearrange("p (u d) -> p u d", u=4)
                act(P, [(A.Vd[0], bg)], A.Vd[0][:, bg * 4:(bg + 1) * 4, 0:64], pv4[:, :, 0:64], AF.Copy, [pt])
                cp(P, "dve", [(A.Vd[1], bg)], A.Vd[1][:, bg * 4:(bg + 1) * 4, 64:128], pv4[:, :, 64:128], [pt])
            for hh in range(2):
                rows = slice(hh * 64, (hh + 1) * 64)
                G = min(4, nb)
                for r in range(dil):
                    for n0 in range(0, nb, G):
                        pss = C.psA[gi % 2]
                        pex = A.pex[gi % 2]
                        PT = A.PT[gi % 2]
                        gi += 1
                        for g in range(G):
                            n = n0 + g
                            qs = blk_slice(dil, r, n)
                            if n > 0:
                                mm(P, [(pss, (g, 0))], pss[:, g * 256: g * 256 + 128],
                                   A.kT[rows, blk_slice(dil, r, n - 1)], A.qT[rows, qs], [A.kT, A.qT])
                            mm(P, [(pss, (g, 1))], pss[:, g * 256 + 128: g * 256 + 256], A.kT[rows, qs], A.qT[rows, qs],
                               [A.kT, A.qT])
                        act(P, [pex], pex[:, 0:G * 256], pss[:, 0:G * 256], AF.Exp, [pss, A.nb], bias=A.nb[:, hh:hh + 1])
                        tt(P, "dve", [PT], PT[:, 0:G * 256].rearrange("p (g m) -> p g m", g=G),
                           pex[:, 0:G * 256].rearrange("p (g m) -> p g m", g=G),
                           A.E[hh][:, ci, :].unsqueeze(1).to_broadcast([128, G, 256]), ALU.mult, [pex, A.E[hh]])
                        pn = C.psB[gi % 2]
                        Vh = A.Vd[hh]
                        for g in range(G):
                            n = n0 + g
                            kbs = [1] if n == 0 else [0, 1]
                            for ix, kb in enumerate(kbs):
                                bi = r * nb + (n - 1 + kb)
                                rhs = PT[:, g * 256 + kb * 128: g * 256 + (kb + 1) * 128]
                                mm(P, [(pn, g)], pn[:, g * 128:(g + 1) * 128], Vh[:, bi, :], rhs, [Vh, PT],
                                   start=(ix == 0), stop=(ix == len(kbs) - 1))
                        dsl = blk_slice(dil, r, n0, G)
                        orow = slice((1 - hh) * 64, (2 - hh) * 64)
                        if ci == 0:
                            cp(P, "dve", [(A.anum, hh)], A.anum[rows, dsl], pn[rows, 0:G * 128], [pn])
                            act(P, [(A.aden, hh)], A.aden[orow, dsl], pn[orow, 0:G * 128], AF.Copy, [pn])
                        else:
                            tt(P, "dve", [(A.anum, hh)], A.anum[rows, dsl], pn[rows, 0:G * 128], A.anum[rows, dsl],
                               ALU.add, [pn, (A.anum, hh)])
                            tt(P, "dve", [(A.aden, hh)], A.aden[orow, dsl], pn[orow, 0:G * 128], A.aden[orow, dsl],
                               ALU.add, [pn, (A.aden, hh)])
        for q4 in range(8):
            oc = A.oTc[q4 % 2]
            rd = A.rden[q4 % 2]
            csl = slice(q4 * 512, (q4 + 1) * 512)
            act(P, [(rd, 0)], rd[0:64, :], A.aden[64:128, csl], AF.Ln, [A.aden])
            act(P, [(rd, 0)], rd[0:64, :], rd[0:64, :], AF.Exp, [(rd, 0)], scale=-1.0)
            act(P, [(rd, 1)], rd[64:128, :], A.aden[0:64, csl], AF.Ln, [A.aden])
            act(P, [(rd, 1)], rd[64:128, :], rd[64:128, :], AF.Exp, [(rd, 1)], scale=-1.0)
            tt(P, "dve", [rd], rd[:, :], A.anum[:, csl], rd[:, :], ALU.mult, [A.anum, rd])
            tt(P, "pool", [oc], oc[:, :], rd[:, :], A.gT[:, csl], ALU.mult, [rd, A.gT])
            P.store(C.oT_d, C.oT_d[pr, :, csl], oc, oc[:, :], key=pr)
    P.pop_scope()


LSEG = 256


def sincos(P, V, th):
    I32 = mybir.dt.int32
    kf, ki = V("kf"), P.sb("s5_ki", [128, 16], I32)
    t = V("sc_t")
    ts(P, "dve", [t], t[:, :], th[:, :], 0.6366197723675814, None, ALU.mult, [th])
    cp(P, "dve", [ki], ki[:, :], t[:, :], [t])
    cp(P, "dve", [kf], kf[:, :], ki[:, :], [ki])
    r = V("sc_r")
    stt(P, "dve", [r], r[:, :], kf[:, :], -1.5707963705062866, th[:, :], ALU.mult, ALU.add, [kf, th])
    stt(P, "dve", [r], r[:, :], kf[:, :], 4.371139000186243e-08, r[:, :], ALU.mult, ALU.add, [kf, r])
    r2 = V("sc_r2")
    tt(P, "dve", [r2], r2[:, :], r[:, :], r[:, :], ALU.mult, [r])

    def horner(name, coefs):
        p = V(name)
        ts(P, "dve", [p], p[:, :], r2[:, :], coefs[0], coefs[1], ALU.mult, [r2], op1=ALU.add)
        for c in coefs[2:]:
            tt(P, "dve", [p], p[:, :], p[:, :], r2[:, :], ALU.mult, [p, r2])
            ts(P, "dve", [p], p[:, :], p[:, :], c, None, ALU.add, [p])
        return p
    sp = horner("sc_sp", [1.0 / 362880, -1.0 / 5040, 1.0 / 120, -1.0 / 6, 1.0])
    sr = V("sc_sr")
    tt(P, "dve", [sr], sr[:, :], sp[:, :], r[:, :], ALU.mult, [sp, r])
    cr = horner("sc_cr", [-1.0 / 3628800, 1.0 / 40320, -1.0 / 720, 1.0 / 24, -0.5, 1.0])
    fl, fi = V("sc_fl"), P.sb("s5_fi", [128, 16], I32)
    ts(P, "dve", [fl], fl[:, :], kf[:, :], -1.5, 0.25, ALU.add, [kf], op1=ALU.mult)
    cp(P, "dve", [fi], fi[:, :], fl[:, :], [fl])
    cp(P, "dve", [fl], fl[:, :], fi[:, :], [fi])
    q = V("sc_q")
    stt(P, "dve", [q], q[:, :], fl[:, :], -4.0, kf[:, :], ALU.mult, ALU.add, [fl, kf])
    qa, qab = V("sc_qa"), V("sc_qab")
    stt(P, "dve", [qa], qa[:, :], q[:, :], -1.0, q[:, :], ALU.add, ALU.mult, [q])
    stt(P, "dve", [qab], qab[:, :], q[:, :], -2.0, qa[:, :], ALU.add, ALU.mult, [q, qa])
    ts(P, "dve", [qab], qab[:, :], qab[:, :], 1.0 / 3.0, None, ALU.mult, [qab])
    cq, sq = V("sc_cq"), V("sc_sq")
    tt(P, "dve", [cq], cq[:, :], qab[:, :], q[:, :], ALU.subtract, [qab, q])
    ts(P, "dve", [cq], cq[:, :], cq[:, :], 1.0, None, ALU.add, [cq])
    tt(P, "dve", [sq], sq[:, :], qab[:, :], qa[:, :], ALU.subtract, [qab, qa])
    tt(P, "dve", [sq], sq[:, :], sq[:, :], q[:, :], ALU.add, [sq, q])
    co, si, t2 = V("sc_cos"), V("sc_sin"), V("sc_t2")
    tt(P, "dve", [co], co[:, :], cr[:, :], cq[:, :], ALU.mult, [cr, cq])
    tt(P, "dve", [t2], t2[:, :], sr[:, :], sq[:, :], ALU.mult, [sr, sq])
    tt(P, "dve", [co], co[:, :], co[:, :], t2[:, :], ALU.subtract, [co, t2])
    tt(P, "dve", [si], si[:, :], sr[:, :], cq[:, :], ALU.mult, [sr, cq])
    tt(P, "dve", [t2], t2[:, :], cr[:, :], sq[:, :], ALU.mult, [cr, sq])
    tt(P, "dve", [si], si[:, :], si[:, :], t2[:, :], ALU.add, [si, t2])
    return co, si


def cmul_small(P, V, name, ar_, ai_, br_, bi_, sl=None):
    o_r, o_i, t = V(name + "_r"), V(name + "_i"), V(name + "_t")
    tt(P, "dve", [o_r], o_r[:, :], ar_[:, :], br_[:, :], ALU.mult, [ar_, br_])
    tt(P, "dve", [t], t[:, :], ai_[:, :], bi_[:, :], ALU.mult, [ai_, bi_])
    tt(P, "dve", [o_r], o_r[:, :], o_r[:, :], t[:, :], ALU.subtract, [o_r, t])
    tt(P, "dve", [o_i], o_i[:, :], ar_[:, :], bi_[:, :], ALU.mult, [ar_, bi_])
    tt(P, "dve", [t], t[:, :], ai_[:, :], br_[:, :], ALU.mult, [ai_, br_])
    tt(P, "dve", [o_i], o_i[:, :], o_i[:, :], t[:, :], ALU.add, [o_i, t])
    return o_r, o_i


def stage_s5_proj(P, C, w_in_d):
    P.push_scope()
    w = P.sb("s5w", [128, 8, 128], BF16)
    ub = [P.sb("s5ub%d" % i, [128, 512], F32) for i in range(2)]
    gb = [P.sb("s5gb%d" % i, [128, 512], BF16) for i in range(2)]
    for cq in range(4):
        load_w_bf16(P, C, w, w[:, :, :], w_in_d, w_in_d[:, 1536 + cq * 128: 1536 + (cq + 1) * 128], 128)

        def ev_u(ps, tb, tsl, cq=cq):
            t = ub[tb % 2]
            act(P, [t], t[:, :], ps[:, :], AF.Copy, [ps])
            P.store(C.uT_d, C.uT_d[cq, :, tsl], t, t[:, :], key=cq)
        proj_fm(P, C, w, 128, C.psB, ev_u)
        load_w_bf16(P, C, w, w[:, :, :], w_in_d, w_in_d[:, 2560 + cq * 128: 2560 + (cq + 1) * 128], 128)

        def ev_g(ps, tb, tsl, cq=cq):
            t = gb[tb % 2]
            act(P, [t], t[:, :], ps[:, :], AF.Silu, [ps])
            P.store(C.gB_d, C.gB_d[cq, :, tsl], t, t[:, :], key=cq)
        proj_fm(P, C, w, 128, C.psB, ev_g)
    P.pop_scope()


def stage_s5(P, C, Dm):
    P.push_scope()
    nv = [0]

    def V(name):
        nv[0] += 1
        return P.sb("s5v_%s_%d" % (name, nv[0]), [128, 16], F32)

    def ldv(name, d):
        v = V(name)
        P.load(v, v[:, :], d, d[:, :])
        return v
    ar, ai, ldt = ldv("ar", Dm["ar"]), ldv("ai", Dm["ai"]), ldv("ldt", Dm["ldt"])
    dt, dar, mag, th = V("dt"), V("dar"), V("mag"), V("th")
    act(P, [dt], dt[:, :], ldt[:, :], AF.Exp, [ldt])
    tt(P, "dve", [dar], dar[:, :], dt[:, :], ar[:, :], ALU.mult, [dt, ar])
    act(P, [mag], mag[:, :], dar[:, :], AF.Exp, [dar])
    tt(P, "dve", [th], th[:, :], dt[:, :], ai[:, :], ALU.mult, [dt, ai])
    co, si = sincos(P, V, th)
    lr, li = V("lr"), V("li")
    tt(P, "dve", [lr], lr[:, :], mag[:, :], co[:, :], ALU.mult, [mag, co])
    tt(P, "dve", [li], li[:, :], mag[:, :], si[:, :], ALU.mult, [mag, si])
    lr1, den, fr, fi_, t = V("lr1"), V("den"), V("fr"), V("fi"), V("t")
    ts(P, "dve", [lr1], lr1[:, :], lr[:, :], -1.0, None, ALU.add, [lr])
    tt(P, "dve", [den], den[:, :], ar[:, :], ar[:, :], ALU.mult, [ar])
    tt(P, "dve", [t], t[:, :], ai[:, :], ai[:, :], ALU.mult, [ai])
    tt(P, "dve", [den], den[:, :], den[:, :], t[:, :], ALU.add, [den, t])
    P.op("dve", lambda e: e.reciprocal(den[:, :], den[:, :]), [den], [den])
    tt(P, "dve", [fr], fr[:, :], lr1[:, :], ar[:, :], ALU.mult, [lr1, ar])
    tt(P, "dve", [t], t[:, :], li[:, :], ai[:, :], ALU.mult, [li, ai])
    tt(P, "dve", [fr], fr[:, :], fr[:, :], t[:, :], ALU.add, [fr, t])
    tt(P, "dve", [fr], fr[:, :], fr[:, :], den[:, :], ALU.mult, [fr, den])
    tt(P, "dve", [fi_], fi_[:, :], li[:, :], ar[:, :], ALU.mult, [li, ar])
    tt(P, "dve", [t], t[:, :], lr1[:, :], ai[:, :], ALU.mult, [lr1, ai])
    tt(P, "dve", [fi_], fi_[:, :], fi_[:, :], t[:, :], ALU.subtract, [fi_, t])
    tt(P, "dve", [fi_], fi_[:, :], fi_[:, :], den[:, :], ALU.mult, [fi_, den])
    Bre, Bim = P.sb("s5Bre", [128, 16, 16], F32), P.sb("s5Bim", [128, 16, 16], F32)
    P.load(Bre, Bre[:, :, :], Dm["b_re"], Dm["b_re"][:, :, :])
    P.load(Bim, Bim[:, :, :], Dm["b_im"], Dm["b_im"][:, :, :])
    Bbr, Bbi, Bt = P.sb("s5Bbr", [128, 16, 16], F32), P.sb("s5Bbi", [128, 16, 16], F32), P.sb("s5Bt", [128, 16, 16], F32)
    bc = lambda v: v[:, :].unsqueeze(2).to_broadcast([128, 16, 16])
    tt(P, "dve", [Bbr], Bbr[:, :, :], Bre[:, :, :], bc(fr), ALU.mult, [Bre, fr])
    tt(P, "dve", [Bt], Bt[:, :, :], Bim[:, :, :], bc(fi_), ALU.mult, [Bim, fi_])
    tt(P, "dve", [Bbr], Bbr[:, :, :], Bbr[:, :, :], Bt[:, :, :], ALU.subtract, [Bbr, Bt])
    tt(P, "dve", [Bbi], Bbi[:, :, :], Bim[:, :, :], bc(fr), ALU.mult, [Bim, fr])
    tt(P, "dve", [Bt], Bt[:, :, :], Bre[:, :, :], bc(fi_), ALU.mult, [Bre, fi_])
    tt(P, "dve", [Bbi], Bbi[:, :, :], Bbi[:, :, :], Bt[:, :, :], ALU.add, [Bbi, Bt])
    BpT = P.sb("s5BpT", [128, 16, 2, 128], BF16)
    Bblk = [P.sb("s5Bblk%d" % i, [128, 128], F32) for i in range(2)]
    n = 0
    for q in range(16):
        base = 32 * (q % 4)
        for ri, src in enumerate((Bbr, Bbi)):
            bb = Bblk[n % 2]
            ps = C.psB[n % 2]
            n += 1
            mset(P, "pool", [bb], bb[:, :], 0.0)
            cp(P, "pool", [bb], bb[0:64, base:base + 16], src[0:64, q, :], [src, bb])
            cp(P, "pool", [bb], bb[64:128, base + 16:base + 32], src[64:128, q, :], [src, bb])
            tr(P, [ps], ps[:, 0:128], bb[:, :], C.identf[:, :], [bb, C.identf])
            act(P, [(BpT, (q, ri))], BpT[:, q, ri, :], ps[:, 0:128], AF.Copy, [ps])
    CpT = P.sb("s5CpT", [128, 16, 2, 128], BF16)
    Cre, Cim = P.sb("s5Cre", [128, 16, 16], F32), P.sb("s5Cim", [128, 16, 16], F32)
    P.load(Cre, Cre[:, :, :], Dm["cT_re"], Dm["cT_re"][:, :, :])
    P.load(Cim, Cim[:, :, :], Dm["cT_im"], Dm["cT_im"][:, :, :])
    ts(P, "dve", [Cim], Cim[:, :, :], Cim[:, :, :], -1.0, None, ALU.mult, [Cim])
    mset(P, "pool", [CpT], CpT[:, :, :, :], 0.0)
    for q in range(16):
        base = 32 * (q % 4)
        for ri, src in enumerate((Cre, Cim)):
            cp(P, "dve", [CpT], CpT[0:64, q, ri, base:base + 16], src[0:64, q, :], [src, CpT])
            cp(P, "dve", [CpT], CpT[64:128, q, ri, base + 16:base + 32], src[64:128, q, :], [src, CpT])
    L = LSEG
    ct, st = P.sb("s5ct", [128, 4, L], F32), P.sb("s5st", [128, 4, L], F32)
    tA, tB = P.sb("s5tA", [128, 4, L], F32), P.sb("s5tB", [128, 4, L], F32)
    btr, bti = P.sb("s5btr", [128, 4, L], F32), P.sb("s5bti", [128, 4, L], F32)
    xtr, xti = P.sb("s5xtr", [128, 4, L], F32), P.sb("s5xti", [128, 4, L], F32)
    xr, xi = P.sb("s5xr", [128, 4, L], BF16), P.sb("s5xi", [128, 4, L], BF16)
    uf = [P.sb("s5uf%d" % i, [128, L], F32) for i in range(2)]
    ubf = [P.sb("s5ubf%d" % i, [128, L], BF16) for i in range(2)]
    yv, y2, yw, ysg = (P.sb("s5y%d" % i, [128, L], F32) for i in range(4))
    car_r, car_i, cl_t = P.sb("s5car_r", [128, 4], F32), P.sb("s5car_i", [128, 4], F32), P.sb("s5cl_t", [128, 4], F32)
    ncr0, nci0 = P.sb("s5ncr", [128, 4], F32), P.sb("s5nci", [128, 4], F32)
    cl_t2 = P.sb("s5cl_t2", [128, 4], F32)
    dfm = P.sb("s5dfm", [128, 4], F32)
    P.load(dfm, dfm[:, :], Dm["d_fm"], Dm["d_fm"][:, :])
    ygT = C.big
    for cq in range(4):
        qs = slice(4 * cq, 4 * cq + 4)
        ur, ui = co, si
        mset(P, "pool", [ct], ct[:, :, 0:1], 1.0)
        mset(P, "pool", [st], st[:, :, 0:1], 0.0)
        cp(P, "dve", [ct], ct[:, :, 1:2], co[:, qs].unsqueeze(2), [co, ct])
        cp(P, "dve", [st], st[:, :, 1:2], si[:, qs].unsqueeze(2), [si, st])
        k = 1
        while (1 << k) < L:
            nn = 1 << k
            ur, ui = cmul_small(P, V, "u%d_%d" % (cq, k), ur, ui, ur, ui)
            bcu = lambda v: v[:, qs].unsqueeze(2).to_broadcast([128, 4, nn])
            tt(P, "dve", [tA], tA[:, :, 0:nn], ct[:, :, 0:nn], bcu(ur), ALU.mult, [ct, ur])
            tt(P, "dve", [tB], tB[:, :, 0:nn], st[:, :, 0:nn], bcu(ui), ALU.mult, [st, ui])
            tt(P, "dve", [ct], ct[:, :, nn:2 * nn], tA[:, :, 0:nn], tB[:, :, 0:nn], ALU.subtract, [tA, tB, ct])
            tt(P, "dve", [tA], tA[:, :, 0:nn], ct[:, :, 0:nn], bcu(ui), ALU.mult, [ct, ui])
            tt(P, "dve", [tB], tB[:, :, 0:nn], st[:, :, 0:nn], bcu(ur), ALU.mult, [st, ur])
            tt(P, "dve", [st], st[:, :, nn:2 * nn], tA[:, :, 0:nn], tB[:, :, 0:nn], ALU.add, [tA, tB, st])
            k += 1
        uLr, uLi = cmul_small(P, V, "uL%d" % cq, ur, ui, ur, ui)
        cars = ((car_r, car_i), (ncr0, nci0))
        mset(P, "pool", [car_r], car_r[:, :], 0.0)
        mset(P, "pool", [car_i], car_i[:, :], 0.0)
        for seg in range(S // L):
            tsl = slice(seg * L, (seg + 1) * L)
            car_a, car_b = cars[seg % 2]
            u_f, u_b = uf[seg % 2], ubf[seg % 2]
            P.load(u_f, u_f[:, :], C.uT_d, C.uT_d[cq, :, tsl], key=cq)
            cp(P, "pool", [u_b], u_b[:, :], u_f[:, :], [u_f])
            pre, pim = C.psA[0], C.psA[1]
            for pr in range(4):
                mm(P, [pre], pre[:, pr * L:(pr + 1) * L], BpT[:, 4 * cq + pr, 0, :], u_b[:, :], [BpT, u_b])
                mm(P, [pim], pim[:, pr * L:(pr + 1) * L], BpT[:, 4 * cq + pr, 1, :], u_b[:, :], [BpT, u_b])
            v3 = lambda b: b[:, :, :]
            p3 = lambda b: b[:, 0:4 * L].rearrange("p (a t) -> p a t", a=4)
            tt(P, "dve", [tA], v3(tA), p3(pre), v3(ct), ALU.mult, [pre, ct])
            tt(P, "dve", [tB], v3(tB), p3(pim), v3(st), ALU.mult, [pim, st])
            tt(P, "dve", [btr], v3(btr), v3(tA), v3(tB), ALU.add, [tA, tB])
            tt(P, "dve", [tA], v3(tA), p3(pim), v3(ct), ALU.mult, [pim, ct])
            tt(P, "dve", [tB], v3(tB), p3(pre), v3(st), ALU.mult, [pre, st])
            tt(P, "dve", [bti], v3(bti), v3(tA), v3(tB), ALU.subtract, [tA, tB])
            for pr in range(4):
                q = 4 * cq + pr
                for (src, dst, car) in ((btr, xtr, car_a), (bti, xti, car_b)):
                    P.op("dve", lambda e, src=src, dst=dst, car=car, pr=pr, q=q: e.tensor_tensor_scan(
                        dst[:, pr, :], mag[:, q:q + 1].to_broadcast([128, L]), src[:, pr, :], car[:, pr:pr + 1],
                        ALU.mult, ALU.add), [src, mag, car], [(dst, pr)])
            lre, lim = xtr[:, :, L - 1], xti[:, :, L - 1]
            ncr, nci = cars[(seg + 1) % 2]
            tt(P, "dve", [ncr], ncr[:, :], lre, uLr[:, qs], ALU.mult, [xtr, uLr])
            tt(P, "dve", [cl_t], cl_t[:, :], lim, uLi[:, qs], ALU.mult, [xti, uLi])
            tt(P, "dve", [ncr], ncr[:, :], ncr[:, :], cl_t[:, :], ALU.subtract, [ncr, cl_t])
            tt(P, "dve", [nci], nci[:, :], lre, uLi[:, qs], ALU.mult, [xtr, uLi])
            tt(P, "dve", [cl_t2], cl_t2[:, :], lim, uLr[:, qs], ALU.mult, [xti, uLr])
            tt(P, "dve", [nci], nci[:, :], nci[:, :], cl_t2[:, :], ALU.add, [nci, cl_t2])
            tt(P, "dve", [tA], v3(tA), v3(xtr), v3(ct), ALU.mult, [xtr, ct])
            tt(P, "pool", [tB], v3(tB), v3(xti), v3(st), ALU.mult, [xti, st])
            tt(P, "dve", [xr], v3(xr), v3(tA), v3(tB), ALU.subtract, [tA, tB])
            tt(P, "pool", [btr], v3(btr), v3(xtr), v3(st), ALU.mult, [xtr, st])
            tt(P, "pool", [bti], v3(bti), v3(xti), v3(ct), ALU.mult, [xti, ct])
            tt(P, "pool", [xi], v3(xi), v3(btr), v3(bti), ALU.add, [btr, bti])
            py = C.psB[seg % 2]
            for pr in range(4):
                q = 4 * cq + pr
                mm(P, [py], py[:, 0:L], CpT[:, q, 0, :], xr[:, pr, :], [CpT, xr], start=(pr == 0), stop=False)
                mm(P, [py], py[:, 0:L], CpT[:, q, 1, :], xi[:, pr, :], [CpT, xi], start=False, stop=(pr == 3))
            stt(P, "dve", [yv], yv[:, :], u_f[:, :], dfm[:, cq:cq + 1], py[:, 0:L], ALU.mult, ALU.add, [u_f, dfm, py])
            tt(P, "pool", [y2], y2[:, :], yv[:, :], yv[:, :], ALU.mult, [yv])
            ts(P, "pool", [y2], y2[:, :], y2[:, :], 0.044715, 1.0, ALU.mult, [y2], op1=ALU.add)
            tt(P, "pool", [yw], yw[:, :], y2[:, :], yv[:, :], ALU.mult, [y2, yv])
            act(P, [ysg], ysg[:, :], yw[:, :], AF.Sigmoid, [yw], scale=1.5957691216057308)
            tt(P, "pool", [(ygT, ("yg", cq, seg))], ygT[:, cq, tsl], yv[:, :], ysg[:, :], ALU.mult, [yv, ysg])
    gw = P.sb("s5gw", [128, 4, 128], BF16)
    gbias = P.sb("s5gbias", [128, 4], F32)
    P.load(gbias, gbias[:, :], Dm["glu_b_fm"], Dm["glu_b_fm"][:, :])
    gT = P.sb("s5gT", [128, S], BF16)
    sg = [P.sb("s5sg%d" % i, [128, 512], F32) for i in range(2)]
    oc_t = [P.sb("s5oc%d" % i, [128, 512], BF16) for i in range(2)]
    for oc in range(4):
        st_ = C.wstage[C.wsi % 2]
        C.wsi += 1
        P.load(st_, st_[:, 0:4, :], Dm["glu_w"], Dm["glu_w"][:, oc * 128:(oc + 1) * 128].rearrange("(k p) c -> p k c", p=128))
        cp(P, "dve", [gw], gw[:, :, :], st_[:, 0:4, :], [st_])
        P.load(gT, gT[:, :], C.gB_d, C.gB_d[oc], key=oc)
        for tb in range(8):
            tsl = slice(tb * 512, (tb + 1) * 512)
            ps = C.psB[tb % 2]
            for cq in range(4):
                mm(P, [ps], ps[:, :], gw[:, cq, :], ygT[:, cq, tsl], [gw, ygT], start=(cq == 0), stop=(cq == 3))
            s_ = sg[tb % 2]
            o_ = oc_t[tb % 2]
            act(P, [s_], s_[:, :], ps[:, :], AF.Sigmoid, [ps, gbias], bias=gbias[:, oc:oc + 1])
            tt(P, "dve", [s_], s_[:, :], s_[:, :], ygT[:, oc, tsl], ALU.mult, [s_, ygT])
            tt(P, "pool", [o_], o_[:, :], s_[:, :], gT[:, tsl], ALU.mult, [s_, gT])
            P.store(C.oT_d, C.oT_d[4 + oc, :, tsl], o_, o_[:, :], key=4 + oc)
    P.pop_scope()


def stage_gdn_proj(P, C, w_in_d, Dg, G):
    P.push_scope()
    w = P.sb("gdw", [128, 8, 128], BF16)
    convw = P.sb("gconvw", [128, 24, 4], F32)
    P.load(convw, convw[:, :, :], Dg["conv_fm"], Dg["conv_fm"][:, :, :])
    zp = [P.sb("gzp%d" % i, [128, 515], F32) for i in range(3)]
    acc = [P.sb("gacc%d" % i, [128, 512], F32) for i in range(4)]
    sl = [P.sb("gsl%d" % i, [128, 512], F32) for i in range(4)]
    sq = [P.sb("gsq%d" % i, [128, 512], F32) for i in range(4)]
    rs = [P.sb("grs%d" % i, [128, 512], F32) for i in range(4)]
    ob = [P.sb("gob%d" % i, [128, 512], BF16) for i in range(4)]
    dsts = (G.qT_d, G.kT_d, G.vT_d, G.gT_d)
    n = 0
    for typ in range(4):
        for hd in range(8):
            ch = typ * 8 + hd
            load_w_bf16(P, C, w, w[:, :, :], w_in_d, w_in_d[:, ch * 128:(ch + 1) * 128], 128)
            for tb in range(8):
                tsl = slice(tb * 512, (tb + 1) * 512)
                ps = (C.psB[0], C.psB[1], C.psTf[0])[n % 3]
                for k in range(8):
                    mm(P, [ps], ps[:, :], w[:, k, :], C.big[:, k, tsl], [w, C.big], start=(k == 0), stop=(k == 7))
                o_ = ob[n % 4]
                if typ == 3:
                    act(P, [o_], o_[:, :], ps[:, :], AF.Silu, [ps])
                else:
                    z, zprev = zp[tb % 3], zp[(tb + 2) % 3]
                    if tb == 0:
                        mset(P, "pool", [z], z[:, 0:3], 0.0)
                    else:
                        cp(P, "pool", [z], z[:, 0:3], zprev[:, 512:515], [zprev, z])
                    act(P, [z], z[:, 3:515], ps[:, :], AF.Copy, [ps, z])
                    a_ = acc[n % 4]
                    ts(P, "dve", [a_], a_[:, :], z[:, 3:515], convw[:, ch, 3:4], None, ALU.mult, [z, convw])
                    for j in (2, 1, 0):
                        stt(P, "dve", [a_], a_[:, :], z[:, j:j + 512], convw[:, ch, j:j + 1], a_[:, :], ALU.mult, ALU.add,
                            [z, convw, a_])
                    if typ == 2:
                        act(P, [o_], o_[:, :], a_[:, :], AF.Silu, [a_])
                    else:
                        s_, q_, r_ = sl[n % 4], sq[n % 4], rs[n % 4]
                        act(P, [s_], s_[:, :], a_[:, :], AF.Silu, [a_])
                        tt(P, "pool", [q_], q_[:, :], s_[:, :], s_[:, :], ALU.mult, [s_])
                        pss = (C.psA[0], C.psA[1], C.psTf[1])[n % 3]
                        mm(P, [pss], pss[:, 0:512], C.ones_f[:, :], q_[:, :], [C.ones_f, q_])
                        act(P, [r_], r_[:, :], pss[:, 0:512], AF.Ln, [pss, C.epsb], bias=C.epsb[:, 0:1])
                        act(P, [r_], r_[:, :], r_[:, :], AF.Exp, [r_], scale=-0.5)
                        stt(P, "dve", [o_], o_[:, :], s_[:, :], (128.0 ** -0.5) if typ == 0 else 1.0, r_[:, :],
                            ALU.mult, ALU.mult, [s_, r_])
                P.store(dsts[typ], dsts[typ][hd, :, tsl], o_, o_[:, :], key=hd)
                n += 1
    w8 = P.sb("gdw8", [128, 8, 8], BF16)
    for (c0, dst) in ((4096, G.R0), (4104, G.R1)):
        st_ = C.wstage[C.wsi % 2]
        C.wsi += 1
        P.load(st_, st_[:, :, 0:8], w_in_d, w_in_d[:, c0:c0 + 8].rearrange("(k p) c -> p k c", p=128))
        cp(P, "dve", [w8], w8[:, :, :], st_[:, :, 0:8], [st_])
        for tb in range(8):
            tsl = slice(tb * 512, (tb + 1) * 512)
            ps = C.psB[tb % 2]
            for k in range(8):
                mm(P, [ps], ps[0:8, :], w8[:, k, :], C.big[:, k, tsl], [w8, C.big], start=(k == 0), stop=(k == 7))
            act(P, [(dst, tb)], dst[0:8, tsl], ps[0:8, :], AF.Copy, [ps])
    P.pop_scope()


def stage_gdn_rows(P, C, Dg, G):
    P.push_scope()
    R0, R1 = G.R0, G.R1
    R2, R3 = P.sb("gR2", [8, S], F32), P.sb("gR3", [8, S], F32)
    alog, dtb, nA = P.sb("galog", [8, 1], F32), P.sb("gdtb", [8, 1], F32), P.sb("gnA", [8, 1], F32)
    P.load(alog, alog[:, :], Dg["a_log"], Dg["a_log"][:, :])
    P.load(dtb, dtb[:, :], Dg["dt_bias"], Dg["dt_bias"][:, :])
    act(P, [nA], nA[:, :], alog[:, :], AF.Exp, [alog])
    ts(P, "dve", [nA], nA[:, :], nA[:, :], -1.0, None, ALU.mult, [nA])
    act(P, [R0], R0[:, :], R0[:, :], AF.Sigmoid, [R0])
    act(P, [R1], R1[:, :], R1[:, :], AF.Exp, [R1, dtb], bias=dtb[:, 0:1])
    act(P, [R1], R1[:, :], R1[:, :], AF.Ln, [R1], bias=1.0)
    ts(P, "dve", [R1], R1[:, :], R1[:, :], nA[:, 0:1], None, ALU.mult, [R1, nA])
    mset(P, "pool", [R2], R2[:, :], 1.0)
    mset(P, "pool", [R2], R2[:, 0:S:64], 0.0)
    P.op("dve", lambda e: e.tensor_tensor_scan(R3[:, :], R2[:, :], R1[:, :], 0.0, ALU.mult, ALU.add), [R2, R1], [R3])

    def to_cols(row, kq):
        for t in range(NT):
            ps = C.psB[t % 2]
            tr(P, [ps], ps[:, 0:8], row[0:8, t * 128:(t + 1) * 128], C.identf[0:8, 0:8], [row, C.identf])
            act(P, [(G.cols, (t, kq))], G.cols[:, t, kq, :], ps[:, 0:8], AF.Copy, [ps])
    to_cols(R3, 0)
    to_cols(R0, 1)
    act(P, [R1], R1[:, :], R3[:, :], AF.Exp, [R3])
    tt(P, "dve", [R2], R2[:, :], R0[:, :], R1[:, :], ALU.mult, [R0, R1])
    to_cols(R2, 2)
    gc3 = R3[:, :].rearrange("p (c t) -> p c t", t=64)
    gl = R3[:, 63:S:64]
    tt(P, "dve", [R2], R2[:, :].rearrange("p (c t) -> p c t", t=64), gl.unsqueeze(2).to_broadcast([8, 64, 64]), gc3,
       ALU.subtract, [R3])
    act(P, [R2], R2[:, :], R2[:, :], AF.Exp, [R2])
    to_cols(R2, 3)
    dlrow = P.sb("gdlrow", [8, 64], F32)
    act(P, [dlrow], dlrow[:, :], gl, AF.Exp, [R3])
    for h in range(8):
        ps = C.psB[h % 2]
        mm(P, [ps], ps[:, 0:64], G.sel8[0:8, h, :], dlrow[0:8, :], [G.sel8, dlrow])
        act(P, [(G.DL, h)], G.DL[:, h, :], ps[:, 0:64], AF.Copy, [ps])
    ts(P, "dve", [R0], R0[:, :], R3[:, :], -1.0, None, ALU.mult, [R3])
    P.pop_scope()


def stage_gdn_main(P, C, Dg, G):
    P.push_scope()
    big = C.big
    sets = []
    for si in range(2):
        vcnt = [0]
        F3 = lambda nm: P.sb("g1%s_%d" % (nm, si), [128, 8, 128], F32)

        def B3(nm, si=si, vcnt=vcnt):
            if si == 0:
                return P.sb("g1%s_%d" % (nm, si), [128, 8, 128], BF16)
            k = vcnt[0]
            vcnt[0] += 1
            ap = big.t[:, 4 + k // 4, (k % 4) * 1024:(k % 4 + 1) * 1024].rearrange("p (a t) -> p a t", a=8)
            bb = Buf("g1v%s" % nm, ap, "sb")
            P.bufs.append(bb)
            return bb
        T = Ctx()
        T.Kbe, T.Kd, T.bV, T.ADf, T.ADT, T.Rb, T.WT = (B3(n) for n in ("Kbe", "Kd", "bV", "ADf", "ADT", "Rb", "WT"))
        T.E_, T.Rr = (F3(n) for n in ("E", "R"))
        T.Nn, T.Xx = B3("N"), B3("X")
        T.Nk, T.Xk = [B3("Nk%d" % i) for i in range(2)], [B3("Xk%d" % i) for i in range(2)]
        T.U = P.sb("g1U_%d" % si, [64, 16, 128], F32)
        sets.append(T)
    gcount = [0]
    madd, m01 = P.sb("gmadd", [128, 128], F32), P.sb("gm01", [128, 128], F32)
    P.load(madd, madd[:, :], Dg["maskadd"], Dg["maskadd"][:, :])
    P.load(m01, m01[:, :], Dg["strict01"], Dg["strict01"][:, :])
    qT, kT, vT, qgT = (big[:, i, :] for i in range(4))
    bc8 = lambda ap2: ap2.unsqueeze(2).to_broadcast([128, 8, 128])
    bcm = lambda m: m[:, :].unsqueeze(1).to_broadcast([128, 8, 128])
    v3 = lambda b: b[:, :, :]
    p3 = lambda ps: ps[:, 0:1024].rearrange("p (a t) -> p a t", a=8)
    for hd in range(8):
        for i_, d_ in enumerate((G.qT_d, G.kT_d, G.vT_d)):
            P.load(big, big[:, i_, :], d_, d_[hd], key=hd)
        for tb in range(8):
            tsl = slice(tb * 512, (tb + 1) * 512)
            ps = C.psB[tb % 2]
            mm(P, [ps], ps[:, :], G.sel8[0:8, hd, :], G.R1[0:8, tsl], [G.sel8, G.R1])
            tt(P, "dve", [(big, ("qg", tb))], big[:, 3, tsl], ps[:, :], big[:, 0, tsl], ALU.mult, [ps, (big, ("in", 0))])
        P.store(G.qg_d, G.qg_d[hd], big, big[:, 3, :], key=hd)
        for g in range(4):
            T = sets[gcount[0] % 2]
            gcount[0] += 1
            Kbe, Kd, bV, ADf, ADT, Rb, WT = T.Kbe, T.Kd, T.bV, T.ADf, T.ADT, T.Rb, T.WT
            E_, Rr, Nn, Xx, Nk, Xk, U = T.E_, T.Rr, T.Nn, T.Xx, T.Nk, T.Xk, T.U
            t0 = g * 8
            tsls = [slice((t0 + u) * 128, (t0 + u + 1) * 128) for u in range(8)]
            col = lambda kq: bc8(G.cols[:, t0:t0 + 8, kq, hd])
            pk_, pv_ = C.psTs
            for u in range(8):
                tr(P, [pk_], pk_[:, u * 128:(u + 1) * 128], kT[:, tsls[u]], C.ident[:, :], [big, C.ident])
            for u in range(8):
                tr(P, [pv_], pv_[:, u * 128:(u + 1) * 128], vT[:, tsls[u]], C.ident[:, :], [big, C.ident])
            tt(P, "dve", [Kbe], v3(Kbe), p3(pk_), col(2), ALU.mult, [pk_, G.cols])
            tt(P, "dve", [Kd], v3(Kd), p3(pk_), col(3), ALU.mult, [pk_, G.cols])
            tt(P, "dve", [bV], v3(bV), p3(pv_), col(1), ALU.mult, [pv_, G.cols])
            for hlf in range(2):
                pe_ = C.psB[hlf]
                mm(P, [pe_], pe_[:, :], G.sel8[0:8, hd, :], G.R0[0:8, (t0 + 4 * hlf) * 128:(t0 + 4 * hlf + 4) * 128], [G.sel8, G.R0])
                tt(P, "dve", [(E_, hlf)], E_[:, 4 * hlf:4 * hlf + 4, :], pe_[:, :].rearrange("p (a t) -> p a t", a=4),
                   madd[:, :].unsqueeze(1).to_broadcast([128, 4, 128]), ALU.add, [pe_, madd])
            tt(P, "pool", [E_], v3(E_), v3(E_), col(0), ALU.add, [E_, G.cols])
            act(P, [E_], v3(E_), v3(E_), AF.Exp, [E_])
            pkk, pa = C.psA
            for u in range(8):
                mm(P, [pkk], pkk[:, u * 128:(u + 1) * 128], kT[:, tsls[u]], kT[:, tsls[u]], [big])
            for u in range(8):
                mm(P, [pa], pa[:, u * 128:(u + 1) * 128], qT[:, tsls[u]], kT[:, tsls[u]], [big])
            tt(P, "dve", [ADf], v3(ADf), p3(pa), v3(E_), ALU.mult, [pa, E_])
            tt(P, "dve", [Nn], v3(Nn), p3(pkk), v3(E_), ALU.mult, [pkk, E_])
            tt(P, "pool", [Nn], v3(Nn), v3(Nn), col(1), ALU.mult, [Nn, G.cols])
            tt(P, "pool", [Nn], v3(Nn), v3(Nn), bcm(m01), ALU.mult, [Nn, m01])
            for u in range(8):
                tr(P, [pk_], pk_[:, u * 128:(u + 1) * 128], ADf[:, u, :], C.ident[:, :], [ADf, C.ident])
            act(P, [ADT], v3(ADT), p3(pk_), AF.Copy, [pk_])
            for u in range(8):
                tr(P, [pv_], pv_[:, u * 128:(u + 1) * 128], Nn[:, u, :], C.ident[:, :], [Nn, C.ident])
            act(P, [Xx], v3(Xx), p3(pv_), AF.Copy, [pv_])
            tt(P, "dve", [Rr], v3(Rr), bcm(C.identf), p3(pv_), ALU.subtract, [C.identf, pv_])
            act(P, [Rb], v3(Rb), v3(Rr), AF.Copy, [Rr])
            Ncur, Xcur = Nn, Xx
            for lv in range(1, 6):
                nk, xk = Nk[lv % 2], Xk[lv % 2]
                for u in range(8):
                    mm(P, [pa], pa[:, u * 128:(u + 1) * 128], Xcur[:, u, :], Ncur[:, u, :], [Xcur, Ncur])
                act(P, [nk], v3(nk), p3(pa), AF.Copy, [pa])
                if lv < 5:
                    for u in range(8):
                        mm(P, [pkk], pkk[:, u * 128:(u + 1) * 128], Ncur[:, u, :], Xcur[:, u, :], [Xcur, Ncur])
                    cp(P, "dve", [xk], v3(xk), p3(pkk), [pkk])
                for hlf in range(2):
                    pr_ = C.psB[hlf]
                    for u in range(4):
                        mm(P, [pr_], pr_[:, u * 128:(u + 1) * 128], nk[:, 4 * hlf + u, :], Rb[:, 4 * hlf + u, :], [nk, Rb])
                for hlf in range(2):
                    pr_ = C.psB[hlf]
                    tt(P, "dve", [(Rr, hlf)], Rr[:, 4 * hlf:4 * hlf + 4, :], Rr[:, 4 * hlf:4 * hlf + 4, :],
                       pr_[:, :].rearrange("p (a t) -> p a t", a=4), ALU.add, [(Rr, hlf), pr_])
                act(P, [Rb], v3(Rb), v3(Rr), AF.Copy, [Rr])
                Ncur, Xcur = nk, xk
            for hlf in range(2):
                pu = C.psA[hlf]
                for u in range(4):
                    for hh in range(2):
                        cidx = u * 2 + hh
                        mm(P, [pu], pu[0:64, cidx * 128:(cidx + 1) * 128], Rb[:, 4 * hlf + u, hh * 64:(hh + 1) * 64],
                           bV[:, 4 * hlf + u, :], [Rb, bV])
                act(P, [(U, hlf)], U[:, 8 * hlf:8 * hlf + 8, :], pu[0:64, 0:1024].rearrange("p (a t) -> p a t", a=8),
                    AF.Copy, [pu])
            pw = C.psA[0]
            for u in range(8):
                mm(P, [pw], pw[:, u * 128:(u + 1) * 128], Kbe[:, u, :], Rb[:, u, :], [Kbe, Rb])
            cp(P, "dve", [WT], v3(WT), p3(pw), [pw])
            P.store(G.sWT, G.sWT[hd, t0:t0 + 8].rearrange("t p d -> p t d"), WT, v3(WT), key=(hd, g), eng="sp")
            P.store(G.sADT, G.sADT[hd, t0:t0 + 8].rearrange("t p d -> p t d"), ADT, v3(ADT), key=(hd, g), eng="sp")
            P.store(G.sKd, G.sKd[hd, t0:t0 + 8].rearrange("t p d -> p t d"), Kd, v3(Kd), key=(hd, g), eng="sp")
            P.store(G.sU, G.sU[hd, 2 * t0:2 * t0 + 16].rearrange("c p d -> p c d"), U, U[:, :, :], key=(hd, g), eng="sp")
    P.pop_scope()


def stage_gdn_rec(P, C, Dg, G):
    P.push_scope()
    BT = lambda nm: [P.sb("g2%s%d" % (nm, i), [128, 8, 128], BF16) for i in range(2)]
    WTt, ADTt, Kdt, qgt, gTt = BT("WT"), BT("ADT"), BT("Kd"), BT("qg"), BT("gT")
    Ut = [P.sb("g2U%d" % i, [64, 8, 2, 128], F32) for i in range(2)]
    vP = [P.sb("g2vP%d" % i, [128, 8, 128], BF16) for i in range(2)]
    for hh in range(2):
        mset(P, "pool", [vP[hh]], vP[hh][:, :, :], 0.0)
    Sf, Sb = P.sb("g2Sf", [128, 8, 128], F32), P.sb("g2Sb", [128, 8, 128], BF16)
    mset(P, "pool", [Sf], Sf[:, :, :], 0.0)
    mset(P, "pool", [Sb], Sb[:, :, :], 0.0)
    Otm = [P.sb("g2O%d" % i, [128, 8, 128], F32) for i in range(2)]
    sq = P.sb("g2sq", [128, 8, 128], F32)
    onb = P.sb("g2onb", [128, 8, 128], BF16)
    oTt = [P.sb("g2oT%d" % i, [128, 8, 128], BF16) for i in range(2)]
    ss8, rs8 = P.sb("g2ss", [128, 8], F32), P.sb("g2rs", [128, 8], F32)
    ng = P.sb("gng", [128, 1], F32)
    P.load(ng, ng[:, :], Dg["norm_g"], Dg["norm_g"][:, :])
    v3 = lambda b: b[:, :, :]
    p3 = lambda ps, n=128: ps[0:n, 0:1024].rearrange("p (a t) -> p a t", a=8)
    for t in range(NT):
        b = t % 2
        tsl = slice(t * 128, (t + 1) * 128)
        P.load(WTt[b], v3(WTt[b]), G.sWT, G.sWT[:, t].rearrange("h p d -> p h d"))
        P.load(ADTt[b], v3(ADTt[b]), G.sADT, G.sADT[:, t].rearrange("h p d -> p h d"))
        P.load(Kdt[b], v3(Kdt[b]), G.sKd, G.sKd[:, t].rearrange("h p d -> p h d"))
        P.load(qgt[b], v3(qgt[b]), G.qg_d, G.qg_d[:, :, tsl].rearrange("h p d -> p h d"))
        P.load(gTt[b], v3(gTt[b]), G.gT_d, G.gT_d[:, :, tsl].rearrange("h p d -> p h d"))
        for hh in range(2):
            P.load(Ut[b], Ut[b][:, :, hh, :], G.sU, G.sU[:, 2 * t + hh].rearrange("h p d -> p h d"))
        for hh in range(2):
            c = 2 * t + hh
            isl = slice(hh * 64, (hh + 1) * 64)
            p3_, p2 = C.psA
            for hf in range(2):
                p1 = C.psB[hf]
                for h4 in range(4):
                    h = 4 * hf + h4
                    mm(P, [p1], p1[0:64, h4 * 128:(h4 + 1) * 128], WTt[b][:, h, isl], Sb[:, h, :], [WTt[b], Sb])
                tt(P, "dve", [(vP[hh], hf)], vP[hh][isl, 4 * hf:4 * hf + 4, :], Ut[b][:, 4 * hf:4 * hf + 4, hh, :],
                   p1[0:64, :].rearrange("p (a t) -> p a t", a=4), ALU.subtract, [Ut[b], p1])
            for h in range(8):
                mm(P, [p2], p2[0:64, h * 128:(h + 1) * 128], qgt[b][:, h, isl], Sb[:, h, :], [qgt[b], Sb], start=True, stop=False)
                mm(P, [p2], p2[0:64, h * 128:(h + 1) * 128], ADTt[b][:, h, isl], vP[hh][:, h, :], [ADTt[b], vP[hh]],
                   start=False, stop=True)
            act(P, [(Otm[b], hh)], Otm[b][isl, :, :], p3(p2, 64), AF.Copy, [p2])
            for h in range(8):
                mm(P, [p3_], p3_[:, h * 128:(h + 1) * 128], Kdt[b][:, h, :], vP[hh][:, h, :], [Kdt[b], vP[hh]])
            tt(P, "pool", [Sf], v3(Sf), v3(Sf), G.DL[:, :, c:c + 1].to_broadcast([128, 8, 128]), ALU.mult, [Sf, G.DL])
            tt(P, "dve", [Sb], v3(Sb), v3(Sf), p3(p3_), ALU.add, [Sf, p3_])
            tt(P, "dve", [Sf], v3(Sf), v3(Sf), p3(p3_), ALU.add, [Sf, p3_])
        tt(P, "pool", [sq], v3(sq), v3(Otm[b]), v3(Otm[b]), ALU.mult, [Otm[b]])
        P.op("dve", lambda e, sq=sq: e.reduce_sum(ss8[:, :], sq[:, :, :], AX.X), [sq], [ss8])
        rstd_from_ss(P, C, ss8, ss8[:, :], rs8, rs8[:, :], 128)
        tt(P, "dve", [onb], v3(onb), v3(Otm[b]), rs8[:, :].unsqueeze(2).to_broadcast([128, 8, 128]), ALU.mult, [Otm[b], rs8])
        pt = C.psTs[b]
        for h in range(8):
            tr(P, [pt], pt[:, h * 128:(h + 1) * 128], onb[:, h, :], C.ident[:, :], [onb, C.ident])
        stt(P, "dve", [oTt[b]], v3(oTt[b]), pt[:, 0:1024].rearrange("p (a t) -> p a t", a=8), ng[:, 0:1], v3(gTt[b]),
            ALU.mult, ALU.mult, [pt, ng, gTt[b]])
        P.store(C.oT_d, C.oT_d[:, :, tsl].rearrange("h p d -> p h d"), oTt[b], v3(oTt[b]))
    P.pop_scope()


def stage_gdn(P, C, w_in_d, Dg):
    P.push_scope()
    G = Ctx()
    G.R0, G.R1 = P.sb("gR0", [8, S], F32), P.sb("gR1", [8, S], F32)
    G.cols = P.sb("gcols", [128, NT, 4, 8], F32)
    G.DL = P.sb("gDL", [128, 8, 64], F32)
    G.sel8 = P.sb("gsel8", [8, 8, 128], F32)
    P.load(G.sel8, G.sel8[:, :, :], Dg["sel8"], Dg["sel8"][:, :, :])
    G.qT_d, G.kT_d, G.vT_d, G.gT_d, G.qg_d = C.gdn_scr
    G.sWT, G.sADT, G.sKd = C.gdn_scr2
    G.sU = C.gdn_scrU
    stage_gdn_proj(P, C, w_in_d, Dg, G)
    if C.upto != "l1a":
        stage_gdn_rows(P, C, Dg, G)
        if C.upto != "l1b":
            stage_gdn_main(P, C, Dg, G)
            if C.upto != "l1c":
                stage_gdn_rec(P, C, Dg, G)
    P.pop_scope()


def t5_bucket_np(dist, buckets=32, max_dist=2048):
    dist = np.maximum(dist, 0)
    max_exact = buckets // 2
    large = max_exact + (np.log(np.maximum(dist, 1) / max_exact)
                         / math.log(max_dist / max_exact) * (buckets - max_exact)).astype(np.int32)
    large = np.minimum(large, buckets - 1)
    return np.where(dist < max_exact, dist, large).astype(np.int32)


def attn_tables(rel_bias):
    k = np.arange(128)[:, None]
    q = np.arange(128)[None, :]
    mask = np.zeros((128, 2, 128), np.float32)
    mask[:, 0, :] = (k >= q)
    mask[:, 1, :] = (q >= k)
    idx = np.zeros((3, 128, 2, 128), np.int64)
    for ci, (dil, nb) in enumerate(CFGS):
        idx[ci, :, 0, :] = t5_bucket_np(np.clip(q + 128 - k, 0, 128) * dil)
        idx[ci, :, 1, :] = t5_bucket_np(np.clip(q - k, 0, 128) * dil)
    G = rel_bias[idx]
    G = np.ascontiguousarray(np.transpose(G, (4, 1, 0, 2, 3))).reshape(8, 128, 768)
    return G.astype(np.float32), mask.reshape(128, 256)


def build(upto="all"):
    nc = bass.Bass("TRN2", target_bir_lowering=False)
    P = Prog(nc)
    C = Ctx()
    C.upto = upto
    dbg = upto != "all"
    C.x = P.dram("x", [S, D], F32, kind="ExternalInput")
    C.out = P.dram("out", [S, D], F32, kind="ExternalOutput")
    C.x1 = P.dram("x1", [S, D], F32, kind=("ExternalOutput" if dbg else "Internal"))
    setup(P, C)
    if dbg:
        C.oT_d = P.dram("dbg_oT", [8, 128, S], BF16, kind="ExternalOutput")
        C.dbg_h = P.dram("dbg_h", [8, 128, S], BF16, kind="ExternalOutput")
    C.w_in0 = P.dram("w_in0", [1024, 3072], F32, kind="ExternalInput")
    C.w_out0 = P.dram("w_out0", [1024, 1024], F32, kind="ExternalInput")
    C.w_in1 = P.dram("w_in1", [1024, 4112], F32, kind="ExternalInput")
    C.w_out1 = P.dram("w_out1", [1024, 1024], F32, kind="ExternalInput")
    C.Gd = P.dram("attn_G", [8, 128, 768], F32, kind="ExternalInput")
    C.maskd = P.dram("attn_mask", [128, 256], F32, kind="ExternalInput")
    C.uT_d = P.dram("uT_d", [4, 128, S], F32)
    C.gB_d = P.dram("gB_d", [4, 128, S], BF16)
    C.gdn_scr = [P.dram("gdn_scr%d" % i, [8, 128, S], BF16) for i in range(5)]
    C.gdn_scr2 = [P.dram("gdn_scrB%d" % i, [8, NT, 128, 128], BF16) for i in range(3)]
    C.gdn_scrU = P.dram("gdn_scrU", [8, 2 * NT, 64, 128], F32)
    Dm = {}
    for nm, shp in (("ar", [128, 16]), ("ai", [128, 16]), ("ldt", [128, 16]), ("b_re", [128, 16, 16]), ("b_im", [128, 16, 16]),
                    ("cT_re", [128, 16, 16]), ("cT_im", [128, 16, 16]), ("d_fm", [128, 4]), ("glu_b_fm", [128, 4]),
                    ("glu_w", [512, 512])):
        Dm[nm] = P.dram("s5_" + nm, shp, F32, kind="ExternalInput")
    Dg = {}
    for nm, shp in (("conv_fm", [128, 24, 4]), ("a_log", [8, 1]), ("dt_bias", [8, 1]), ("norm_g", [128, 1]),
                    ("maskadd", [128, 128]), ("strict01", [128, 128]), ("sel8", [8, 8, 128])):
        Dg[nm] = P.dram("gdn_" + nm, shp, F32, kind="ExternalInput")

    def layer0():
        stage_adaln(P, C, 0)
        if upto == "ada":
            C.dbg_m = P.dram("dbg_m", [128, 24 + 8 + 1024], F32, kind="ExternalOutput")
            P.store(C.dbg_m, C.dbg_m[:, 0:24], C.modfm, C.modfm[:, :], eng="sp")
            P.store(C.dbg_m, C.dbg_m[:, 24:32], C.Asc, C.Asc[:, :], eng="sp")
            P.store(C.dbg_m, C.dbg_m[:, 32:1056], C.GP, C.GP[:, :], eng="sp")
            return
        if upto == "all":
            stage_adaln(P, C, 1, defer_pop=True)
        stage_prenorm(P, C, C.x, 0)
        if upto == "all":
            P.pop_scope()
        if dbg:
            for k in range(8):
                P.store(C.dbg_h, C.dbg_h[k], C.big, C.big[:, k, :], eng="sp")
        if upto == "pre":
            return
        if upto != "s5":
            stage_attn(P, C, C.w_in0, C.Gd, C.maskd)
        if upto != "attn":
            stage_s5_proj(P, C, C.w_in0)
            stage_s5(P, C, Dm)
        if upto in ("attn", "s5"):
            return
        stage_out(P, C, C.w_out0, C.x, C.x1, 0)

    def layer1(xin, xout):
        if upto != "all":
            stage_adaln(P, C, 1)
        stage_prenorm(P, C, xin, 1)
        if dbg:
            for k in range(8):
                P.store(C.dbg_h, C.dbg_h[k], C.big, C.big[:, k, :], eng="sp")
        stage_gdn(P, C, C.w_in1, Dg)
        if upto in ("l1a", "l1b", "l1c"):
            return
        stage_out(P, C, C.w_out1, xin, xout, 1)

    if upto in ("l1", "l1a", "l1b", "l1c"):
        layer1(C.x, C.x1)
    else:
        layer0()
        if upto == "all":
            layer1(C.x1, C.out)
    P.emit()
    return nc, P


def host_inputs(inputs, b):
    f32 = np.float32
    G, mask = attn_tables(np.asarray(inputs["rel_bias"], f32))
    m = {
        "x": np.ascontiguousarray(inputs["x"][b], f32),
        "c_fm": np.ascontiguousarray(np.asarray(inputs["c"][b], f32).reshape(8, 128).T),
        "ada_w": np.ascontiguousarray(inputs["ada_w"], f32),
        "ada_b_fm": np.ascontiguousarray(np.transpose(np.asarray(inputs["ada_b"], f32).reshape(2, 24, 128), (0, 2, 1))),
        "ada_b_g": np.ascontiguousarray(np.asarray(inputs["ada_b"], f32)[:, 2048:3072].reshape(2, 1, 1024)),
        "pre_g_fm": np.ascontiguousarray(np.transpose(np.asarray(inputs["pre_g"], f32).reshape(2, 8, 128), (0, 2, 1))),
        "post_g_row": np.ascontiguousarray(np.asarray(inputs["post_g"], f32).reshape(2, 1, 1024)),
        "ident_in": np.eye(128).astype(ml_dtypes.bfloat16),
        "identf_in": np.eye(128).astype(f32),
        "w_in0": np.ascontiguousarray(inputs["ab_w_in"][0], f32),
        "w_out0": np.ascontiguousarray(inputs["ab_w_out"][0], f32),
        "attn_G": G, "attn_mask": mask,
    }

    def pair_layout(a):
        a = np.asarray(a, f32)
        a = a.reshape((16, 2, 64) + a.shape[2:])
        return np.ascontiguousarray(np.moveaxis(a, 0, 2).reshape((128, 16) + a.shape[3:]))
    m["s5_ar"] = pair_layout(inputs["s5_a_re"][0])
    m["s5_ai"] = pair_layout(inputs["s5_a_im"][0])
    m["s5_ldt"] = pair_layout(np.broadcast_to(np.asarray(inputs["s5_log_dt"][0], f32)[:, None], (32, 64)))
    m["s5_b_re"] = pair_layout(inputs["s5_b_re"][0])
    m["s5_b_im"] = pair_layout(inputs["s5_b_im"][0])
    m["s5_cT_re"] = pair_layout(np.transpose(np.asarray(inputs["s5_c_re"][0], f32), (0, 2, 1)))
    m["s5_cT_im"] = pair_layout(np.transpose(np.asarray(inputs["s5_c_im"][0], f32), (0, 2, 1)))
    m["s5_d_fm"] = np.ascontiguousarray(np.asarray(inputs["s5_d"][0], f32).reshape(4, 128).T)
    m["s5_glu_b_fm"] = np.ascontiguousarray(np.asarray(inputs["s5_glu_b"][0], f32).reshape(4, 128).T)
    m["s5_glu_w"] = np.ascontiguousarray(inputs["s5_glu_w"][0], f32)
    m["w_in1"] = np.ascontiguousarray(inputs["gdn_w_in"][0], f32)
    m["w_out1"] = np.ascontiguousarray(inputs["gdn_w_out"][0], f32)
    m["gdn_conv_fm"] = np.ascontiguousarray(np.transpose(np.asarray(inputs["gdn_conv"][0], f32).reshape(4, 24, 128), (2, 1, 0)))
    m["gdn_a_log"] = np.ascontiguousarray(np.asarray(inputs["gdn_a_log"][0], f32).reshape(8, 1))
    m["gdn_dt_bias"] = np.ascontiguousarray(np.asarray(inputs["gdn_dt_bias"][0], f32).reshape(8, 1))
    m["gdn_norm_g"] = np.ascontiguousarray(np.asarray(inputs["gdn_norm_g"][0], f32).reshape(128, 1))
    ii = np.arange(128)[:, None]
    jj = np.arange(128)[None, :]
    same = (ii // 64) == (jj // 64)
    m["gdn_maskadd"] = np.where(same & (ii >= jj), 0.0, -30000.0).astype(f32)
    m["gdn_strict01"] = (same & (ii > jj)).astype(f32)
    sel = np.zeros((8, 8, 128), f32)
    for h in range(8):
        sel[h, h, :] = 1.0
    m["gdn_sel8"] = sel
    return m


_PROG = {}


def kernel(**inputs):
    if "nc" not in _PROG:
        _PROG["nc"] = build("all")[0]
    nc = _PROG["nc"]
    nb = int(np.asarray(inputs["x"]).shape[0])
    maps = [host_inputs(inputs, b) for b in range(nb)]
    in_maps = [maps[i % nb] for i in range(8)]
    res = run_bass_kernel_spmd(nc, in_maps, core_ids=list(range(8)))
    out = np.stack([np.asarray(res.results[b]["out"], dtype=np.float32) for b in range(nb)], axis=0)
    return out
```
